# Optimizing a Trainium2 kernel written in Bass

```python
import jax, jax.numpy as jnp
from jax import lax
import numpy as np

D_MODEL = 2048
BATCH = 16
SEQ = 256
DEPTH = 2
DEC_BATCH = 8
DEC_SEQ = 1024
PAST_LEN = 256

GRID_W = 64
HEAD_DIM = 64
MIX_WIDTH = D_MODEL
BRANCH_W = MIX_WIDTH // 4
FOURIER_GROUPS = 8
FOURIER_GROUP_W = BRANCH_W // FOURIER_GROUPS
HYENA_W = BRANCH_W
SHORT_CONV = 3
FILTER_FREQS = 16
FILTER_FEATS = 1 + 2 * FILTER_FREQS
FILTER_HIDDEN = 64
DECAY_SHIFT = 0.05
N_HEADS = BRANCH_W // HEAD_DIM
N_KV_HEADS = 2
KV_W = N_KV_HEADS * HEAD_DIM
WINDOW = 128
BLOCK = 128
ROPE_THETA = 10000.0
ALPHA = (2 * DEPTH) ** 0.25
BETA = (8 * DEPTH) ** -0.25
LN_EPS = 1e-5
RMS_EPS = 1e-6
NEG_INF = -1e30
IN_SIZES = (BRANCH_W, BRANCH_W, 3 * HYENA_W, HYENA_W,
            BRANCH_W, KV_W, KV_W, BRANCH_W,
            BRANCH_W, KV_W, KV_W, BRANCH_W)
IN_WIDTH = sum(IN_SIZES)

kernel_name = 'hybrid_fourier_hyena_swa_gqa_prefix_dit'

F32 = jnp.float32


def split_points(sizes):
    pts, acc = [], 0
    for s in sizes[:-1]:
        acc += s
        pts.append(acc)
    return pts


def layer_norm(x, g=None, b=None):
    xf = x.astype(F32)
    mu = jnp.mean(xf, -1, keepdims=True)
    var = jnp.mean(jnp.square(xf - mu), -1, keepdims=True)
    y = (xf - mu) * lax.rsqrt(var + LN_EPS)
    if g is not None:
        y = y * g.astype(F32) + b.astype(F32)
    return y.astype(x.dtype)


def rms_norm(x, g):
    xf = x.astype(F32)
    y = xf * lax.rsqrt(jnp.mean(xf * xf, -1, keepdims=True) + RMS_EPS) * g.astype(F32)
    return y.astype(x.dtype)


def axial_rope_angles(seq_len):
    rows = seq_len // GRID_W
    row = jnp.repeat(jnp.arange(rows, dtype=F32), GRID_W)
    col = jnp.tile(jnp.arange(GRID_W, dtype=F32), rows)
    half = HEAD_DIM // 2
    inv = ROPE_THETA ** (-jnp.arange(0, half, 2, dtype=F32) / half)
    return row[:, None] * inv, col[:, None] * inv


def rotate(x, ang):
    m = ang.shape[-1]
    cos = jnp.cos(ang)[None, :, None, :].astype(x.dtype)
    sin = jnp.sin(ang)[None, :, None, :].astype(x.dtype)
    x1, x2 = x[..., :m], x[..., m:]
    return jnp.concatenate([x1 * cos - x2 * sin, x2 * cos + x1 * sin], -1)


def axial_rope(x, ang_row, ang_col):
    half = HEAD_DIM // 2
    return jnp.concatenate([rotate(x[..., :half], ang_row), rotate(x[..., half:], ang_col)], -1)


def modulation(cond, w_ada, b_ada):
    m = jax.nn.silu(cond) @ w_ada + b_ada
    shift, scale, gate = jnp.split(m, 3, axis=-1)
    return shift[:, None, :], scale[:, None, :], gate[:, None, :]


def fourier_mix(a, w_f):
    b, l, _ = a.shape
    a4 = a.reshape(b, l, FOURIER_GROUPS, FOURIER_GROUP_W).astype(F32)
    f = jnp.fft.fft2(a4, axes=(1, 3), norm='ortho').real
    return f.reshape(b, l, BRANCH_W).astype(a.dtype) @ w_f


def short_conv3(u, w, b):
    up = jnp.pad(u, ((0, 0), (1, 1), (0, 0)))
    return up[:, :-2] * w[0] + up[:, 1:-1] * w[1] + up[:, 2:] * w[2] + b


def hyena_filters(seq_len, w1, b1, w2, b2, w3, log_decay):
    t = jnp.arange(seq_len, dtype=F32)
    tn = t / seq_len
    freqs = jnp.arange(1, FILTER_FREQS + 1, dtype=F32)
    ang = 2.0 * jnp.pi * tn[:, None] * freqs[None, :]
    feats = jnp.concatenate([tn[:, None], jnp.cos(ang), jnp.sin(ang)], -1)
    h = jnp.sin(feats @ w1.astype(F32) + b1.astype(F32))
    h = jnp.sin(h @ w2.astype(F32) + b2.astype(F32))
    h = h @ w3.astype(F32)
    dist = jnp.abs(t - seq_len // 2) / (seq_len / 2)
    window = jnp.exp(-dist[:, None] * jnp.exp(log_decay.astype(F32))[None, :]) + DECAY_SHIFT
    h = h * window
    return h * lax.rsqrt(jnp.sum(h * h, 0, keepdims=True) + RMS_EPS)


def long_conv(u, h, skip):
    l = u.shape[1]
    n = 2 * l
    c0 = l // 2
    uf = jnp.fft.rfft(u.astype(F32), n=n, axis=1)
    hf = jnp.fft.rfft(h, n=n, axis=0)
    z = jnp.fft.irfft(uf * hf[None], n=n, axis=1)[:, c0:c0 + l]
    return (z + u.astype(F32) * skip.astype(F32)).astype(u.dtype)


def hyena_mix(u, conv_w, conv_b, fw1, fb1, fw2, fb2, fw3, log_decay, skip):
    u = short_conv3(u, conv_w, conv_b)
    v, x1, x2 = jnp.split(u, 3, axis=-1)
    filt = hyena_filters(u.shape[1], fw1, fb1, fw2, fb2, fw3, log_decay)
    z = x1 * long_conv(v, filt[:, :HYENA_W], skip[:HYENA_W])
    return x2 * long_conv(z, filt[:, HYENA_W:], skip[HYENA_W:])


def to_heads(x, n):
    return x.reshape(x.shape[0], x.shape[1], n, HEAD_DIM)


def dense_block_attention(q, k, v, sink):
    b, lq, h, d = q.shape
    kvh = k.shape[2]
    g = h // kvh
    nb = lq // BLOCK
    lk = k.shape[1]
    scale = d ** -0.5
    qb = q.reshape(b, nb, BLOCK, kvh, g, d).swapaxes(0, 1)

    def one_block(qblk):
        s = jnp.einsum('bqkgd,bskd->bkgqs', qblk, k, preferred_element_type=F32) * scale
        if sink is not None:
            sk = jnp.broadcast_to(sink.astype(F32).reshape(kvh, g)[None, :, :, None, None],
                                  s.shape[:-1] + (1,))
            s = jnp.concatenate([s, sk], -1)
        w = jax.nn.softmax(s, axis=-1)[..., :lk]
        return jnp.einsum('bkgqs,bskd->bqkgd', w.astype(v.dtype), v)

    o = lax.map(one_block, qb)
    return o.swapaxes(0, 1).reshape(b, lq, h * d)


def windowed_attention(q, k, v, k_ctx, v_ctx, sink):
    b, l, h, d = q.shape
    kvh = k.shape[2]
    g = h // kvh
    nb = l // BLOCK
    span = BLOCK + 2 * WINDOW
    lc = k_ctx.shape[1]
    scale = d ** -0.5
    pad = ((0, 0), (WINDOW, WINDOW), (0, 0), (0, 0))
    idx = jnp.arange(nb)[:, None] * BLOCK + jnp.arange(span)[None, :]
    kb = jnp.pad(k, pad)[:, idx]
    vb = jnp.pad(v, pad)[:, idx]
    qb = q.reshape(b, nb, BLOCK, kvh, g, d)
    s_loc = jnp.einsum('bnqkgd,bnskd->bnkgqs', qb, kb, preferred_element_type=F32) * scale
    qpos = jnp.arange(nb)[:, None, None] * BLOCK + jnp.arange(BLOCK)[None, :, None]
    kpos = jnp.arange(nb)[:, None, None] * BLOCK + jnp.arange(span)[None, None, :] - WINDOW
    valid = (jnp.abs(kpos - qpos) <= WINDOW) & (kpos >= 0) & (kpos < l)
    s_loc = jnp.where(valid[None, :, None, None], s_loc, NEG_INF)
    s_ctx = jnp.einsum('bnqkgd,bckd->bnkgqc', qb, k_ctx.astype(qb.dtype), preferred_element_type=F32) * scale
    s_sink = jnp.broadcast_to(sink.astype(F32).reshape(kvh, g)[None, None, :, :, None, None],
                              s_loc.shape[:-1] + (1,))
    w = jax.nn.softmax(jnp.concatenate([s_loc, s_ctx, s_sink], -1), axis=-1).astype(v.dtype)
    o = (jnp.einsum('bnkgqs,bnskd->bnqkgd', w[..., :span], vb)
         + jnp.einsum('bnkgqc,bckd->bnqkgd', w[..., span:span + lc], v_ctx.astype(v.dtype)))
    return o.reshape(b, l, h * d)


def trunk_layer(x, mod, p, ctx_kv=None, rope=None):
    shift, scale, gate = mod
    hmod = layer_norm(x) * (1 + scale) + shift
    parts = jnp.split(hmod @ p['w_in'], split_points(IN_SIZES), axis=-1)
    a_in, a_g, b_in, b_g, cq, ck, cv, c_g, dq, dk, dv, d_g = parts
    y_a = fourier_mix(a_in, p['w_fourier']) * jax.nn.silu(a_g)
    y_b = hyena_mix(b_in, p['conv_w'], p['conv_b'], p['filt_w1'], p['filt_b1'], p['filt_w2'],
                    p['filt_b2'], p['filt_w3'], p['filt_log_decay'], p['hyena_skip']) * jax.nn.silu(b_g)
    cq, ck, cv = to_heads(cq, N_HEADS), to_heads(ck, N_KV_HEADS), to_heads(cv, N_KV_HEADS)
    dq = rms_norm(to_heads(dq, N_HEADS), p['q_norm'])
    dk = rms_norm(to_heads(dk, N_KV_HEADS), p['k_norm'])
    dv = to_heads(dv, N_KV_HEADS)
    if ctx_kv is None:
        o_c = dense_block_attention(cq, ck, cv, p['sink'])
        o_d = dense_block_attention(dq, dk, dv, None)
        new_kv = (ck, cv, dk, dv)
    else:
        ang_r, ang_c = rope
        kc_ctx, vc_ctx, kd_ctx, vd_ctx = ctx_kv
        o_c = windowed_attention(axial_rope(cq, ang_r, ang_c), axial_rope(ck, ang_r, ang_c), cv,
                                 kc_ctx, vc_ctx, p['sink'])
        k_all = jnp.concatenate([axial_rope(dk, ang_r, ang_c), kd_ctx.astype(dk.dtype)], axis=1)
        v_all = jnp.concatenate([dv, vd_ctx.astype(dv.dtype)], axis=1)
        o_d = dense_block_attention(axial_rope(dq, ang_r, ang_c), k_all, v_all, None)
        new_kv = None
    y = jnp.concatenate([y_a, y_b, o_c * jax.nn.silu(c_g), o_d * jax.nn.silu(d_g)], -1) @ p['w_out']
    x = layer_norm(ALPHA * x + gate * y, p['ln_g'], p['ln_b'])
    return x, new_kv


def setup_inputs(seed: int = 0) -> dict:
    key = jax.random.key(seed)
    ks = jax.random.split(key, 28)

    def nrm(k, shape, s):
        return jax.random.normal(k, shape, F32) * s

    cache_shape = (DEC_BATCH, DEPTH, PAST_LEN, N_KV_HEADS, HEAD_DIM)
    base_decay = jnp.log(jnp.linspace(3.0, 15.0, 2 * HYENA_W, dtype=F32))
    return {
        'x_prompt': nrm(ks[0], (BATCH, SEQ, D_MODEL), 1.0),
        'x_sample': nrm(ks[1], (DEC_BATCH, DEC_SEQ, D_MODEL), 1.0),
        'cache_attn_c_k': nrm(ks[2], cache_shape, 1.0),
        'cache_attn_c_v': nrm(ks[3], cache_shape, 1.0),
        'cache_attn_d_k': nrm(ks[4], cache_shape, 1.0),
        'cache_attn_d_v': nrm(ks[5], cache_shape, 1.0),
        'c': nrm(ks[6], (DEC_BATCH, D_MODEL), 1.0),
        'c_ctx': nrm(ks[7], (D_MODEL,), 1.0),
        'w_ada': nrm(ks[8], (DEPTH, D_MODEL, 3 * D_MODEL), 0.5 * D_MODEL ** -0.5),
        'b_ada': nrm(ks[9], (DEPTH, 3 * D_MODEL), 0.01),
        'w_in': nrm(ks[10], (DEPTH, D_MODEL, IN_WIDTH), D_MODEL ** -0.5),
        'w_fourier': nrm(ks[11], (DEPTH, BRANCH_W, BRANCH_W), BRANCH_W ** -0.5),
        'conv_w': nrm(ks[12], (DEPTH, SHORT_CONV, 3 * HYENA_W), SHORT_CONV ** -0.5),
        'conv_b': nrm(ks[13], (DEPTH, 3 * HYENA_W), 0.01),
        'filt_w1': nrm(ks[14], (DEPTH, FILTER_FEATS, FILTER_HIDDEN), 1.0),
        'filt_b1': nrm(ks[15], (DEPTH, FILTER_HIDDEN), 0.5),
        'filt_w2': nrm(ks[16], (DEPTH, FILTER_HIDDEN, FILTER_HIDDEN), FILTER_HIDDEN ** -0.5),
        'filt_b2': nrm(ks[17], (DEPTH, FILTER_HIDDEN), 0.5),
        'filt_w3': nrm(ks[18], (DEPTH, FILTER_HIDDEN, 2 * HYENA_W), FILTER_HIDDEN ** -0.5),
        'filt_log_decay': base_decay[None, :] + nrm(ks[19], (DEPTH, 2 * HYENA_W), 0.01),
        'hyena_skip': nrm(ks[20], (DEPTH, 2 * HYENA_W), 0.5),
        'sink_logit': nrm(ks[21], (DEPTH, N_HEADS), 0.5),
        'q_norm': 1.0 + nrm(ks[22], (DEPTH, HEAD_DIM), 0.01),
        'k_norm': 1.0 + nrm(ks[23], (DEPTH, HEAD_DIM), 0.01),
        'w_out': nrm(ks[24], (DEPTH, MIX_WIDTH, D_MODEL), BETA * MIX_WIDTH ** -0.5),
        'ln_g': 1.0 + nrm(ks[25], (DEPTH, D_MODEL), 0.01),
        'ln_b': nrm(ks[26], (DEPTH, D_MODEL), 0.01),
    }


def reference(x_prompt, x_sample, cache_attn_c_k, cache_attn_c_v, cache_attn_d_k, cache_attn_d_v,
              c, c_ctx, w_ada, b_ada, w_in, w_fourier, conv_w, conv_b, filt_w1, filt_b1, filt_w2,
              filt_b2, filt_w3, filt_log_decay, hyena_skip, sink_logit, q_norm, k_norm, w_out,
              ln_g, ln_b):
    rope = axial_rope_angles(x_sample.shape[1])
    y_p, y_s = x_prompt, x_sample
    kc_list, vc_list, kd_list, vd_list = [], [], [], []
    for l in range(DEPTH):
        p = {
            'w_in': w_in[l], 'w_fourier': w_fourier[l], 'conv_w': conv_w[l], 'conv_b': conv_b[l],
            'filt_w1': filt_w1[l], 'filt_b1': filt_b1[l], 'filt_w2': filt_w2[l],
            'filt_b2': filt_b2[l], 'filt_w3': filt_w3[l], 'filt_log_decay': filt_log_decay[l],
            'hyena_skip': hyena_skip[l], 'sink': sink_logit[l], 'q_norm': q_norm[l],
            'k_norm': k_norm[l], 'w_out': w_out[l], 'ln_g': ln_g[l], 'ln_b': ln_b[l],
        }
        mod_ctx = modulation(c_ctx[None, :], w_ada[l], b_ada[l])
        mod_lat = modulation(c, w_ada[l], b_ada[l])
        y_p, kv = trunk_layer(y_p, mod_ctx, p)
        kc_list.append(kv[0])
        vc_list.append(kv[1])
        kd_list.append(kv[2])
        vd_list.append(kv[3])
        ctx_kv = (cache_attn_c_k[:, l], cache_attn_c_v[:, l], cache_attn_d_k[:, l], cache_attn_d_v[:, l])
        y_s, _ = trunk_layer(y_s, mod_lat, p, ctx_kv=ctx_kv, rope=rope)
    new_c_k = jnp.stack(kc_list, axis=1)
    new_c_v = jnp.stack(vc_list, axis=1)
    new_d_k = jnp.stack(kd_list, axis=1)
    new_d_v = jnp.stack(vd_list, axis=1)
    return (y_p, y_s, new_c_k, new_c_v, new_d_k, new_d_v)
```

```python
import math
import os
from contextlib import ExitStack

import numpy as np
import ml_dtypes

import concourse.bass as bass
import concourse.mybir as mybir
from concourse.bass_utils import run_bass_kernel_spmd

F32 = mybir.dt.float32
BF16 = mybir.dt.bfloat16
AF = mybir.ActivationFunctionType
ALU = mybir.AluOpType

D = 2048
KC = 16
INW = 5632
DEPTH = 2
HD = 64
ALPHA = (2 * DEPTH) ** 0.25
LN_EPS = 1e-5
RMS_EPS = 1e-6
PI = math.pi
TWO_PI = 2.0 * math.pi
SIN_OFF = PI + 16 * TWO_PI

O_A, O_AG = 0, 512
O_BV, O_BX1, O_BX2, O_BG = 1024, 1536, 2048, 2560
O_C, O_D = 3072, 4352


class Op:
    __slots__ = ("eng", "fn", "deps", "dma", "sem", "val", "needed", "idx", "cost", "tbl", "done")

    def __init__(self, eng, fn, dma):
        self.eng = eng
        self.fn = fn
        self.dma = dma
        self.deps = set()
        self.sem = None
        self.val = 0
        self.needed = False


class Sched:
    ENGS = ("pe", "act", "dve", "pool", "sp")
    NDSEM = 12

    def __init__(self, nc, es):
        self.nc = nc
        self.ops = {e: [] for e in self.ENGS}
        self.last_w = {}
        self.readers = {}
        self.esem = {e: es.enter_context(nc.semaphore("sem_" + e)) for e in ("pe", "act", "dve", "pool")}
        self.dsem = {q: [es.enter_context(nc.semaphore("dsem_%s_%d" % (q, i))) for i in range(self.NDSEM)]
                     for q in ("sp", "pool", "act")}
        self.dcount = {q: [0] * self.NDSEM for q in self.dsem}
        self.dlast = {q: [None] * self.NDSEM for q in self.dsem}
        self.dnext = {q: 0 for q in self.dsem}
        self.nops = 0
        self.stores = []

    def op(self, eng, fn, r=(), w=(), dma=False, store=False, cost=0.5, tbl=None):
        pr = [k for k in r if isinstance(k, tuple) and k[0] == "ps"]
        if pr:
            w = list(w) + pr
        o = Op(eng, fn, dma)
        o.cost = cost
        o.tbl = tbl
        o.idx = self.nops
        self.nops += 1
        deps = o.deps
        for k in r:
            lw = self.last_w.get(k)
            if lw is not None:
                deps.add(lw)
        for k in w:
            lw = self.last_w.get(k)
            if lw is not None:
                deps.add(lw)
            rd = self.readers.get(k)
            if rd:
                deps.update(rd.values())
        for k in r:
            self.readers.setdefault(k, {})[o.idx] = o
        for k in w:
            self.last_w[k] = o
            self.readers[k] = {}
        if dma:
            q = eng
            i = self.dnext[q]
            self.dnext[q] = (i + 1) % self.NDSEM
            prev = self.dlast[q][i]
            if prev is not None:
                deps.add(prev)
            self.dcount[q][i] += 16
            o.sem = self.dsem[q][i]
            o.val = self.dcount[q][i]
            self.dlast[q][i] = o
            o.needed = True
            if store:
                self.stores.append(o)
        deps.discard(o)
        self.ops[eng].append(o)
        return o

    def reorder(self, window=48):
        ENG = self.ENGS
        pend = {e: list(self.ops[e]) for e in ENG}
        head = {e: 0 for e in ENG}
        sched = {e: [False] * len(pend[e]) for e in ENG}
        free = {e: 0.0 for e in ENG}
        neworder = {e: [] for e in ENG}
        cur_tbl = [None]
        dma_free = [0.0]
        for e in ENG:
            for o in pend[e]:
                o.done = None
        remaining = sum(len(v) for v in pend.values())
        LAT = float(os.environ.get("KB_LAT", "0.3"))
        best = {e: None for e in ENG}
        rtc = {}
        USE_RANK = os.environ.get("KB_RANK", "1") == "1"
        allo = sorted((o for v in pend.values() for o in v), key=lambda o: o.idx)
        rank = {}
        if USE_RANK:
            succ_best = {}
            for o in reversed(allo):
                r0 = succ_best.get(o.idx, 0.0) + (o.cost + (2.0 if o.dma else 0.0))
                rank[o.idx] = r0
                for d in o.deps:
                    v = r0 + (LAT if (d.eng != o.eng or d.dma) else 0.0)
                    if v > succ_best.get(d.idx, 0.0):
                        succ_best[d.idx] = v

        def find(e):
            lst, sc = pend[e], sched[e]
            h = head[e]
            n = len(lst)
            while h < n and sc[h]:
                h += 1
            head[e] = h
            bt, bi, brk = None, -1, 0.0
            cnt = 0
            i = h
            f = free[e]
            while i < n and cnt < window:
                if not sc[i]:
                    cnt += 1
                    o = lst[i]
                    rt = rtc.get(o.idx)
                    ok = True
                    if rt is None:
                        rt = 0.0
                        for d in o.deps:
                            dd = d.done
                            if dd is None:
                                ok = False
                                break
                            if d.eng != e or d.dma:
                                dd += LAT
                            elif e != "pe":
                                dd += 0.06
                            if dd > rt:
                                rt = dd
                        if ok:
                            rtc[o.idx] = rt
                    if ok:
                        st = rt if rt > f else f
                        if USE_RANK:
                            rk = rank[o.idx]
                            if bt is None or st < bt - 1e-9 or (st <= bt + 1e-9 and rk > brk):
                                bt, bi, brk = st, i, rk
                        elif bt is None or st < bt - 1e-9:
                            bt, bi = st, i
                            if st <= f + 1e-9:
                                break
                i += 1
            best[e] = (bt, bi) if bt is not None else None

        for e in ENG:
            find(e)
        while remaining:
            be, bt, bi = None, None, -1
            for e in ENG:
                b = best[e]
                if b is not None and (bt is None or b[0] < bt):
                    be, bt, bi = e, b[0], b[1]
            assert be is not None, "scheduler deadlock"
            o = pend[be][bi]
            sched[be][bi] = True
            neworder[be].append(o)
            remaining -= 1
            if o.dma:
                free[be] = bt + 0.06
                s0 = max(bt, dma_free[0])
                dma_free[0] = s0 + o.cost
                o.done = dma_free[0] + 2.0
            else:
                c = o.cost
                if be == "act" and o.tbl is not None and o.tbl != cur_tbl[0]:
                    c += 1.3
                    cur_tbl[0] = o.tbl
                o.done = bt + c
                free[be] = o.done
            for e in ENG:
                find(e)
        self.ops = neworder
        self.sim_time = max(free.values())

    def emit(self):
        nc = self.nc
        if os.environ.get("KB_NOREORDER") != "1":
            self.reorder(window=256)
        for e in self.ENGS:
            for o in self.ops[e]:
                for d in o.deps:
                    if d.dma:
                        continue
                    if d.eng == "pe" and o.eng == "pe" and not o.dma:
                        continue
                    d.needed = True
        for e in ("pe", "act", "dve", "pool"):
            c = 0
            for o in self.ops[e]:
                if o.dma:
                    continue
                if o.needed:
                    c += 1
                    o.sem = self.esem[e]
                    o.val = c
        stores = self.stores

        def run(engname, eng, final=False):
            known = {}
            for o in self.ops[engname]:
                waits = {}
                for d in o.deps:
                    if (not d.dma) and d.eng == "pe" and engname == "pe" and not o.dma:
                        continue
                    s = d.sem
                    if waits.get(s, (None, 0))[1] < d.val:
                        waits[s] = (s, d.val)
                for s, v in waits.values():
                    if known.get(s, 0) >= v:
                        continue
                    known[s] = v
                    eng.wait_ge(s, v)
                ins = o.fn(eng)
                if o.needed:
                    ins.then_inc(o.sem, 16 if o.dma else 1)
            if final:
                waits = {}
                for d in stores:
                    if waits.get(d.sem, (None, 0))[1] < d.val:
                        waits[d.sem] = (d.sem, d.val)
                for s, v in waits.values():
                    eng.wait_ge(s, v)

        with nc.Block() as block:
            @block.sync
            def _(e):
                run("sp", e, final=True)

            @block.gpsimd
            def _(e):
                run("pool", e)

            @block.scalar
            def _(e):
                run("act", e)

            @block.vector
            def _(e):
                run("dve", e)

            @block.tensor
            def _(e):
                run("pe", e)


def _chunked(a):
    L = a.shape[0]
    return np.ascontiguousarray(a.reshape(L // 128, 128, -1).transpose(1, 0, 2))


def _bf(a):
    return np.ascontiguousarray(a.astype(np.float32)).astype(ml_dtypes.bfloat16)


CF = {}
_cf_off = 0
for _name, _w in (("ident", 128), ("ones", 128), ("bdc", 128), ("bds", 128),
                  ("cos", 1024), ("sin", 1024),
                  ("feats1024", 1024), ("feats256", 256),
                  ("ndist1024", 8), ("ndist256", 2),
                  ("cab1024", 3), ("cab256", 3)):
    CF[_name] = (_cf_off, _w)
    _cf_off += _w
CF_W = _cf_off

CB = {}
_cb_off = 0
for _name, _w in (("ident", 128), ("prot", 128), ("triu", 128), ("tril", 128), ("onesblk", 128),
                  ("smt0_1024", 8 * 128), ("sm0_1024", 1024), ("smt0_256", 2 * 128), ("sm0_256", 256),
                  ("t256", 4 * 2 * 256)):
    CB[_name] = (_cb_off, _w)
    _cb_off += _w
CB_W = _cb_off


def make_tables():
    cf = np.zeros((128, CF_W), np.float64)
    cb = np.zeros((128, CB_W), np.float64)

    def putf(name, arr):
        o, w = CF[name]
        arr = np.asarray(arr, np.float64)
        cf[: arr.shape[0], o:o + w] = arr.reshape(arr.shape[0], w)

    def putb(name, arr):
        o, w = CB[name]
        arr = np.asarray(arr, np.float64)
        cb[: arr.shape[0], o:o + w] = arr.reshape(arr.shape[0], w)

    putf("ident", np.eye(128))
    putf("ones", np.ones((128, 128)))
    w64 = np.arange(64)
    c64 = np.cos(2 * np.pi * np.outer(w64, w64) / 64) / 8.0
    s64 = np.sin(2 * np.pi * np.outer(w64, w64) / 64) / 8.0
    z = np.zeros((64, 64))
    putf("bdc", np.block([[c64, z], [z, c64]]))
    putf("bds", -np.block([[s64, z], [z, s64]]))
    t = np.arange(1024)
    inv = 10000.0 ** (-np.arange(0, 32, 2) / 32.0)
    cosT = np.zeros((128, 1024))
    sinT = np.zeros((128, 1024))
    for p in range(128):
        d = p % 64
        j = d % 16
        pos = (t // 64) if d < 32 else (t % 64)
        ang = (pos.astype(np.float32) * inv[j].astype(np.float32)).astype(np.float32)
        cosT[p] = np.cos(ang)
        sinT[p] = np.sin(ang)
    putf("cos", cosT)
    putf("sin", sinT)
    prot = np.zeros((128, 128))
    for m in range(128):
        if m % 32 < 16:
            prot[m + 16, m] = -1.0
        else:
            prot[m - 16, m] = 1.0
    putb("prot", prot)
    putb("ident", np.eye(128))
    ii = np.arange(128)[:, None]
    jj = np.arange(128)[None, :]
    putb("triu", (ii <= jj).astype(np.float64))
    putb("tril", (jj <= ii).astype(np.float64))
    putb("onesblk", (ii // 64 == jj // 64).astype(np.float64))
    big = {}
    for L in (1024, 256):
        tt = np.arange(L, dtype=np.float64)
        tn = (tt.astype(np.float32) / np.float32(L)).astype(np.float64)
        fr = np.arange(1, 17, dtype=np.float64)
        ang = 2.0 * np.pi * tn[:, None] * fr[None, :]
        feats = np.concatenate([tn[:, None], np.cos(ang), np.sin(ang)], -1)
        putf("feats%d" % L, feats.T)
        dist = np.abs(tt - L // 2) / (L / 2)
        putf("ndist%d" % L, (-dist).reshape(L // 128, 128).T)
        pm = np.arange(128) % 4
        ca = np.array([1.0, 0.0, -1.0, 0.0])[pm] / L
        cbv = np.array([0.0, 1.0, 0.0, -1.0])[pm] / L
        putf("cab%d" % L, np.stack([ca, cbv, -ca], 1))
        f = np.arange(L)[:, None]
        n = np.arange(L)[None, :]
        Cm = np.cos(np.pi * f * n / L)
        Sm = np.sin(np.pi * f * n / L)
        CL = np.cos(2 * np.pi * f * n / L) / np.sqrt(L)
        SL = np.sin(2 * np.pi * f * n / L) / np.sqrt(L)
        smt0 = Sm[:, 0:128].copy()
        smt0[:, 0] = (-1.0) ** np.arange(L)
        sm0 = Sm[0:128, :].copy()
        sm0[0, :] = (-1.0) ** np.arange(L)
        putb("smt0_%d" % L, _chunked(smt0).reshape(128, -1))
        putb("sm0_%d" % L, sm0)
        big[L] = np.stack([_chunked(CL), _chunked(SL), _chunked(Cm), _chunked(Sm)], 1)
    putb("t256", big[256].reshape(128, -1))
    return dict(cf32=np.ascontiguousarray(cf.astype(np.float32)),
                cbf=_bf(cb),
                t1024=_bf(big[1024]))


def _ap(t, off_extra, dims):
    return bass.AP(t.tensor, t.offset + off_extra, [list(t.ap[0])] + [list(d) for d in dims])


class KB:
    def __init__(self, debug=None, stop_after=None):
        self.debug = debug or {}
        self.stop_after = stop_after
        self.es = ExitStack()
        nc = self.nc = bass.Bass("TRN2", target_bir_lowering=False)
        self.s = Sched(nc, self.es)
        self.rot_i = 0
        self.dbg_outs = {}
        self._decl()

    def dram(self, name, shape, dt=F32, kind="ExternalInput"):
        return self.nc.dram_tensor(name, list(shape), dt, kind=kind).ap()

    def sb(self, name, shape, dt=F32):
        return self.es.enter_context(self.nc.sbuf_tensor(name, list(shape), dt))

    def _decl(self):
        nc = self.nc
        d = self.dram
        self.x_in = {"s": d("x_s", [1024, D]), "p": d("x_p", [512, D])}
        self.cache = {k: d(k, [2, 256, 128]) for k in ("ck_c", "cv_c", "ck_d", "cv_d")}
        self.cond = d("cond", [32, 128])
        self.w_ada = d("w_ada", [2, D, 3 * D])
        self.b_ada = d("b_ada", [2, 48, 128])
        self.w_in = d("w_in", [2, D, INW])
        self.w_f = d("w_f", [2, 512, 512])
        self.conv_w = d("conv_w", [2, 36, 128])
        self.conv_b = d("conv_b", [2, 12, 128])
        self.fw1 = d("fw1", [2, 33, 64])
        self.fb1 = d("fb1", [2, 64, 1])
        self.fw2 = d("fw2", [2, 64, 64])
        self.fb2 = d("fb2", [2, 64, 1])
        self.fw3 = d("fw3", [2, 64, 1024])
        self.ldec = d("ldec", [2, 1, 1024])
        self.skip = d("skip", [2, 1, 1024])
        self.sink = d("sink", [2, 1, 8])
        self.qn = d("qn", [2, 64, 1])
        self.kn = d("kn", [2, 64, 1])
        self.knr = d("knr", [2, 1, 64])
        self.w_out = d("w_out", [2, D, D])
        self.ln_g = d("ln_g", [2, 1, D])
        self.ln_b = d("ln_b", [2, 1, D])
        self.cf32_d = d("cf32", [128, CF_W])
        self.cbf_d = d("cbf", [128, CB_W], BF16)
        self.t1024_d = d("t1024", [128, 4, 8, 1024], BF16)
        o = lambda n, s: self.dram(n, s, F32, "ExternalOutput")
        self.y_out = {"s": o("y_s", [1024, D]), "p": o("y_p", [512, D])}
        self.kv_out = {k: o(k, [2, 2, 256, 128]) for k in ("nck", "ncv", "ndk", "ndv")}
        self.wcache = [self.dram("wcache%d" % l, [44, 128, 2048], BF16, "Internal") for l in range(DEPTH)]
        self.xm = {"s": self.dram("xm_s", [1024, D], F32, "Internal"),
                   "p": self.dram("xm_p", [512, D], F32, "Internal")}
        self.cf = self.sb("cf", [128, CF_W])
        self.cb = self.sb("cb", [128, CB_W], BF16)
        self.big1 = self.sb("big1", [128, 16, 1024], BF16)
        self.big2 = self.sb("big2", [128, 16, 1024], BF16)
        self.mix = self.sb("mix", [128, 16, 1024], BF16)
        self.NSLAB = 5
        self.slab = [self.sb("slab%d" % i, [128, 16, 128], BF16) for i in range(self.NSLAB)]
        self.slab_i = 0
        self.bg_enable = False
        self.bg_jobs = []
        self.modfm = self.sb("modfm", [128, 2, 48, 2])
        self.sc1 = self.sb("sc1", [128, 2, 16, 2])
        self.scT = self.sb("scT", [128, 32], BF16)
        self.small = self.sb("small", [128, 256])
        self.SCRW = 14848
        self.scr = self.sb("scr", [128, self.SCRW])
        self.ps = [self.es.enter_context(nc.psum_tensor("ps%d" % i, [128, 512], F32)) for i in range(8)]
        self.psb = [p.bitcast(BF16) for p in self.ps]

    def cfv(self, name, rows=128):
        o, w = CF[name]
        return self.cf[0:rows, o:o + w]

    def cbv(self, name):
        o, w = CB[name]
        return self.cb[:, o:o + w]

    def arena(self, off_words, nwords, dt=F32, shape=None):
        assert off_words + nwords <= self.SCRW, (off_words, nwords)
        a = self.scr[:, off_words:off_words + nwords]
        if dt == BF16:
            a = a.bitcast(BF16)
        if shape is not None:
            names = " ".join("d%d" % i for i in range(len(shape)))
            a = a.rearrange("p (%s) -> p %s" % (names, names), **{"d%d" % i: shape[i] for i in range(len(shape))})
        keys = [("S", u) for u in range(off_words // 128, (off_words + nwords + 127) // 128)]
        return a, keys

    nrot = 8

    def rot(self):
        i = self.rot_i % self.nrot
        self.rot_i = (i + 1) % self.nrot
        return i

    def op(self, *a, **k):
        return self.s.op(*a, **k)

    @staticmethod
    def _nfree(ap):
        n = 1
        for d in ap.shape[1:]:
            n *= d
        return n

    def dma(self, q, out, in_, r=(), w=(), store=False):
        nb = 1
        for d in in_.shape:
            nb *= d
        nb *= 4 if in_.dtype == F32 else 2
        return self.s.op(q, lambda e: e.dma_start(out=out, in_=in_), r=r, w=w, dma=True, store=store,
                         cost=nb / 230e3)

    def mm(self, out, lhsT, rhs, start, stop, r, w):
        n = self._nfree(out)
        c = max(64, n) / 1950.0 + 0.01
        if lhsT.dtype == F32:
            c *= 4
        return self.s.op("pe", lambda e: e.matmul(out, lhsT=lhsT, rhs=rhs, start=start, stop=stop,
                                                  skip_group_check=True), r=r, w=w, cost=c)

    def tr(self, out, in_, ident, r, w):
        c = 0.07 * (4 if in_.dtype == F32 else 1)
        return self.s.op("pe", lambda e: e.transpose(out, in_, ident), r=r, w=w, cost=c)

    def act(self, out, in_, func, r, w, bias=None, scale=None, eng="act"):
        kw = {}
        if bias is not None:
            kw["bias"] = bias
        if scale is not None:
            kw["scale"] = scale
        tbl = {AF.Exp: "exp", AF.Silu: "silu", AF.Sin: "silu", AF.Sqrt: "sqrt"}.get(func)
        return self.s.op(eng, lambda e: e.activation(out, in_, func, **kw), r=r, w=w,
                         cost=0.2 + 0.00075 * self._nfree(out), tbl=tbl)

    def tt(self, eng, out, in0, in1, op_, r, w):
        return self.s.op(eng, lambda e: e.tensor_tensor(out, in0, in1, op_), r=r, w=w,
                         cost=0.08 + 0.0011 * self._nfree(out))

    def ts(self, eng, out, in0, s1, s2, op0, op1, r, w):
        c = 0.08 + 0.0011 * self._nfree(out)
        if op1 is None:
            return self.s.op(eng, lambda e: e.tensor_scalar(out, in0, s1, None, op0), r=r, w=w, cost=c)
        return self.s.op(eng, lambda e: e.tensor_scalar(out, in0, s1, s2, op0, op1), r=r, w=w, cost=c)

    def stt(self, eng, out, in0, sc, in1, op0, op1, r, w):
        return self.s.op(eng, lambda e: e.scalar_tensor_tensor(out, in0, sc, in1, op0, op1), r=r, w=w,
                         cost=0.08 + 0.0011 * self._nfree(out))

    def cp(self, eng, out, in_, r, w):
        if eng == "act":
            return self.s.op(eng, lambda e: e.copy(out, in_), r=r, w=w, cost=0.2 + 0.00075 * self._nfree(out))
        return self.s.op(eng, lambda e: e.tensor_copy(out, in_), r=r, w=w, cost=0.08 + 0.0011 * self._nfree(out))

    def dump(self, name, src_ap, shape, keys, dt=F32):
        if name not in self.debug:
            return
        o = self.dram("dbg_" + name, shape, dt, "ExternalOutput")
        self.dbg_outs["dbg_" + name] = shape
        self.dma("sp", o, src_ap, r=keys, store=True)

    def load_fm(self, dst, src, n, dkeys, tmp_off):
        rows, rk = self.arena(tmp_off, 128)
        b = self.rot()
        self.dma("sp", rows[0:n, :], src, w=rk)
        self.tr(self.ps[b][:, 0:n], rows[0:n, :], self.cfv("ident")[0:n, 0:n], r=rk + ["cf"], w=[("ps", b)])
        self.cp("dve", dst, self.ps[b][:, 0:n], r=[("ps", b)], w=dkeys)

    def load_slab(self, wsrc, col0, ncols=128, bg=True):
        if bg and self.bg_enable:
            self.bg_tick()
        i = self.slab_i
        self.slab_i = (i + 1) % self.NSLAB
        sl = self.slab[i]
        src = wsrc[:, col0:col0 + ncols].rearrange("(kc p) c -> p kc c", p=128)
        self.dma("pool", sl[:, :, 0:ncols], src, w=[("slab", i)])
        return sl, [("slab", i)]

    def load_win(self, l, col0):
        idx = col0 // 128
        if self.cur_g == "s":
            sl, sk = self.load_slab(self.w_in[l], col0)
            self.dma("sp", self.wcache[l][idx], sl[:].rearrange("p a b -> p (a b)"), r=sk, w=[("wc", l, idx)])
            return sl, sk
        if self.bg_enable:
            self.bg_tick()
        i = self.slab_i
        self.slab_i = (i + 1) % self.NSLAB
        sl = self.slab[i]
        self.dma("sp", sl[:].rearrange("p a b -> p (a b)"), self.wcache[l][idx], r=[("wc", l, idx)],
                 w=[("slab", i)])
        return sl, [("slab", i)]

    def mod_slab(self, l, s):
        sl, sk = self.load_slab(self.w_ada[l], s * 128, bg=False)
        b = self.rot()
        for kc in range(KC):
            self.mm(self.ps[b][:, 0:2], sl[:, kc, :], self.scT[:, kc:32:16], kc == 0, kc == KC - 1,
                    r=sk + ["scT"], w=[("ps", b)])
        bcol = self.bfm[:, l, s:s + 1]
        key = ("modfm", l, s // 16)
        self.ts("dve", self.modfm[:, l, s, :], self.ps[b][:, 0:2], bcol, None, ALU.add, None,
                r=[("ps", b), ("bfm", l)], w=[key])
        if s // 16 == 1:
            self.ts("dve", self.sc1[:, l, s - 16, :], self.ps[b][:, 0:2], bcol, 1.0, ALU.add, ALU.add,
                    r=[("ps", b), ("bfm", l)], w=[("sc1", l)])

    def bg_tick(self):
        if self.bg_jobs:
            l, s = self.bg_jobs.pop(0)
            self.mod_slab(l, s)

    def bg_flush(self, l):
        while self.bg_jobs and self.bg_jobs[0][0] <= l:
            self.bg_tick()

    def setup(self):
        self.dma("sp", self.cf[:], self.cf32_d, w=["cf"])
        self.dma("sp", self.cb[:], self.cbf_d, w=["cb"])
        sm = self.small
        self.c_lneps = sm[:, 200:201]
        self.c_rmseps = sm[:, 201:202]
        self.op("dve", lambda e: e.memset(sm[:, 200:201], LN_EPS), w=["consts"])
        self.op("dve", lambda e: e.memset(sm[:, 201:202], RMS_EPS), w=["consts"])
        self.c_lneps_c = sm[:, 202:203]
        self.op("dve", lambda e: e.memset(sm[:, 202:203], LN_EPS / (ALPHA * ALPHA)), w=["consts"])
        cfm, ck = self.arena(256, 32)
        self.load_fm(cfm, self.cond, 32, ck, 0)
        self.act(self.scT[:], cfm, AF.Silu, r=ck, w=["scT"])
        self.bfm = self.sb("bfm", [128, 2, 48])
        for l in range(DEPTH):
            self.load_fm(self.bfm[:, l, :], self.b_ada[l], 48, [("bfm", l)], 1024 + 128 * l)
        self.bg_jobs = []
        for s in range(32):
            self.mod_slab(0, s)
        self.bg_jobs = [(0, s) for s in range(32, 48)] + [(1, s) for s in range(48)]

    @staticmethod
    def group(g):
        if g == "s":
            return dict(T=1024, cond=0, seqs=[(0, 1024)], L=1024)
        return dict(T=512, cond=1, seqs=[(0, 256), (256, 256)], L=256)

    cur_g = "s"

    def hm(self, kc, t0, t1):
        if self.cur_g == "s":
            return self.big1[:, kc, t0:t1]
        return self.mix[:, kc, 512 + t0:512 + t1]

    def hm_keys(self, kc, t0, t1):
        if self.cur_g == "s":
            return [("hm", kc, t) for t in range(t0 // 128, (t1 + 127) // 128)]
        return [("mx", kc, 4 + t) for t in range(t0 // 128, (t1 + 127) // 128)]

    def mx_keys(self, kc, t0, t1):
        return [("mx", kc, t) for t in range(t0 // 128, (t1 + 127) // 128)]

    def phase_a(self, l, g):
        G = self.group(g)
        self.cur_g = g
        while self.bg_jobs and (self.bg_jobs[0][0] < l or (self.bg_jobs[0][0] == l and self.bg_jobs[0][1] < 32)):
            self.bg_tick()
        src = self.x_in[g] if l == 0 else self.xm[g]
        cj = G["cond"]
        NB_A = 4
        xts = [self.arena(2048 * i, 2048) for i in range(NB_A)]
        xns = [self.arena(2048 * NB_A + 1024 * i, 1024, BF16) for i in range(NB_A)]
        sm = self.small
        SB = (0, 32, 96, 128)
        for tt in range(G["T"] // 128):
            bi = tt % NB_A
            xt, xk = xts[bi]
            xn, nk = xns[bi]
            sb0 = SB[bi]
            st = sm[:, sb0:sb0 + 24].rearrange("p (a b) -> p a b", b=6)
            mv = sm[:, sb0 + 24:sb0 + 26]
            rstd = sm[:, sb0 + 26:sb0 + 27]
            nmr = sm[:, sb0 + 27:sb0 + 28]
            sk = [("smA", bi)]
            self.dma("sp", xt, src[tt * 128:(tt + 1) * 128, :], w=xk)
            for j in range(4):
                self.op("dve", lambda e, j=j, st=st, xt=xt: e.bn_stats(st[:, j, :], xt[:, j * 512:(j + 1) * 512]),
                        r=xk, w=sk, cost=0.65)
            self.op("dve", lambda e, st=st, mv=mv: e.bn_aggr(mv, st), r=sk, w=sk)
            self.act(rstd, mv[:, 1:2], AF.Sqrt, r=sk + ["consts"], w=sk, bias=self.c_lneps)
            self.op("dve", lambda e, rstd=rstd: e.reciprocal(rstd, rstd), r=sk, w=sk)
            self.stt("dve", nmr, mv[:, 0:1], -1.0, rstd, ALU.mult, ALU.mult, r=sk, w=sk)
            self.act(xn, xt, AF.Identity, r=xk + sk, w=nk, bias=nmr, scale=rstd)
            for half in range(2):
                b = self.rot()
                for j in range(8):
                    kc = half * 8 + j
                    self.tr(self.psb[b][:, j * 128:(j + 1) * 128], xn[:, kc * 128:(kc + 1) * 128],
                            self.cbv("ident"), r=nk + ["cb"], w=[("ps", b)])
                for j in range(8):
                    kc = half * 8 + j
                    dst = self.hm(kc, tt * 128, (tt + 1) * 128)
                    srcp = self.psb[b][:, j * 128:(j + 1) * 128]
                    s1 = self.sc1[:, l, kc, cj:cj + 1]
                    sh = self.modfm[:, l, kc, cj:cj + 1]
                    wk = self.hm_keys(kc, tt * 128, tt * 128 + 128)
                    mk = [("ps", b), ("sc1", l), ("modfm", l, 0)]
                    if j % 2 == 0:
                        self.ts("dve", dst, srcp, s1, sh, ALU.mult, ALU.add, r=mk, w=wk)
                    else:
                        self.act(dst, srcp, AF.Identity, r=mk, w=wk, bias=sh, scale=s1)
        if l == 0:
            self.dump("hmodT_" + g, self.big1[:, :, 0:G["T"]] if g == "s" else self.mix[:, :, 512:1024], [128, 16, G["T"]],
                      [k for kc in range(16) for k in self.hm_keys(kc, 0, G["T"])], BF16)


_TABLES = None


def shared_inputs(inp):
    global _TABLES
    if _TABLES is None:
        _TABLES = make_tables()
    f = lambda a: np.ascontiguousarray(np.asarray(a, dtype=np.float32))
    sh = dict(
        w_ada=f(inp["w_ada"]),
        b_ada=f(inp["b_ada"]).reshape(2, 48, 128),
        w_in=f(inp["w_in"]),
        w_f=f(inp["w_fourier"]),
        conv_w=f(inp["conv_w"]).reshape(2, 36, 128),
        conv_b=f(inp["conv_b"]).reshape(2, 12, 128),
        fw1=f(inp["filt_w1"]),
        fb1=f(inp["filt_b1"]).reshape(2, 64, 1),
        fw2=f(inp["filt_w2"]),
        fb2=f(inp["filt_b2"]).reshape(2, 64, 1),
        fw3=f(inp["filt_w3"]),
        ldec=f(inp["filt_log_decay"]).reshape(2, 1, 1024),
        skip=f(inp["hyena_skip"]).reshape(2, 1, 1024),
        sink=f(inp["sink_logit"]).reshape(2, 1, 8),
        qn=f(inp["q_norm"]).reshape(2, 64, 1),
        kn=f(inp["k_norm"]).reshape(2, 64, 1),
        knr=f(inp["k_norm"]).reshape(2, 1, 64),
        w_out=f(inp["w_out"]),
        ln_g=f(inp["ln_g"]).reshape(2, 1, D),
        ln_b=f(inp["ln_b"]).reshape(2, 1, D),
    )
    sh.update(_TABLES)
    return sh


def core_inputs(inp, i, sh):
    f = lambda a: np.ascontiguousarray(np.asarray(a, dtype=np.float32))
    m = dict(sh)
    m["x_s"] = f(inp["x_sample"][i])
    m["x_p"] = f(inp["x_prompt"][2 * i:2 * i + 2]).reshape(512, D)
    m["ck_c"] = f(inp["cache_attn_c_k"][i]).reshape(2, 256, 128)
    m["cv_c"] = f(inp["cache_attn_c_v"][i]).reshape(2, 256, 128)
    m["ck_d"] = f(inp["cache_attn_d_k"][i]).reshape(2, 256, 128)
    m["cv_d"] = f(inp["cache_attn_d_v"][i]).reshape(2, 256, 128)
    m["cond"] = np.ascontiguousarray(
        np.concatenate([f(inp["c"][i]).reshape(16, 128), f(inp["c_ctx"]).reshape(16, 128)], 0))
    return m


def _proj_fm(self, l, col0, T, handler, lhs_of=None):
    sl, sk = self.load_win(l, col0)
    for tc in range(T // 512):
        b = self.rot()
        for kc in range(KC):
            lhsT = sl[:, kc, :] if lhs_of is None else lhs_of(sl, kc)
            self.mm(self.ps[b][:, :], lhsT, self.hm(kc, tc * 512, (tc + 1) * 512), kc == 0, kc == KC - 1,
                    r=sk + self.hm_keys(kc, tc * 512, tc * 512 + 512), w=[("ps", b)])
        handler(tc, b)
    return sl, sk


def _proj_tm(self, l, col0, T, handler):
    sl, sk = self.load_win(l, col0)
    for tt in range(T // 128):
        b = self.rot()
        for kc in range(KC):
            self.mm(self.ps[b][:, 0:128], self.hm(kc, tt * 128, (tt + 1) * 128), sl[:, kc, :],
                    kc == 0, kc == KC - 1, r=sk + self.hm_keys(kc, tt * 128, tt * 128 + 128), w=[("ps", b)])
        handler(tt, b)


def _b2_keys(self, tbl, tc0=0, tc1=8):
    return [("b2", tbl * 8 + t) for t in range(tc0, tc1)]


def _load_tables(self, first):
    v = self.big2[:].rearrange("p (a b) n -> p a b n", a=2)
    for j in range(2):
        self.dma("sp", v[:, j], self.t1024_d[:, first + j], w=self.b2_keys(j))


def _tbl(self, L, which):
    if L == 1024:
        v = self.big2[:].rearrange("p (a b) n -> p a b n", a=2)
        j = which % 2
        return (lambda tc, n0, n1: v[:, j, tc, n0:n1]), (lambda tc: [("b2", j * 8 + tc)])
    o, _ = CB["t256"]
    v = self.cb[:, o:o + 2048].rearrange("p (a b n) -> p a b n", a=4, b=2)
    return (lambda tc, n0, n1: v[:, which, tc, n0:n1]), (lambda tc: ["cb"])


def _branch_a(self, l, g):
    G = self.group(g)
    T, NT = G["T"], G["T"] // 128
    aT, aTk = self.arena(0, 2048, BF16, [4, 1024])
    gA, gAk = self.arena(2048, 2048, BF16, [4, 1024])
    wf, wfk = self.arena(4096, 2048, F32, [4, 512])
    Wc, Wck = self.arena(6144, 1024, BF16, [4, 512])
    Ws, Wsk = self.arena(7168, 1024, BF16, [4, 512])
    ac, ack = self.arena(8192, 2048, BF16, [8, 512])
    as_, ask = self.arena(10240, 2048, BF16, [8, 512])
    if g == "s":
        self.load_tables(0)
    self.dma("sp", wf, self.w_f[l].rearrange("(cc p) n -> p cc n", p=128), w=wfk)
    for (W, Wk, tbl) in ((Wc, Wck, "bdc"), (Ws, Wsk, "bds")):
        for cc in range(4):
            b = self.rot()
            self.mm(self.ps[b][:, :], self.cfv(tbl), wf[:, cc, :], True, True, r=wfk + ["cf"], w=[("ps", b)])
            self.cp("act", W[:, cc, :], self.ps[b][:, :], r=[("ps", b)], w=Wk)
    for s in range(4):
        def h_a(tc, b, s=s):
            self.cp("act", aT[:, s, tc * 512:(tc + 1) * 512], self.ps[b][:, :], r=[("ps", b)], w=aTk)
        self.proj_fm(l, O_A + s * 128, T, h_a)
    for s in range(4):
        def h_g(tc, b, s=s):
            self.act(gA[:, s, tc * 512:(tc + 1) * 512], self.ps[b][:, :], AF.Silu, r=[("ps", b)], w=gAk)
        self.proj_fm(l, O_AG + s * 128, T, h_g)
    for (dst, dk, W, Wk) in ((ac, ack, Wc, Wck), (as_, ask, Ws, Wsk)):
        for tt in range(NT):
            b = self.rot()
            for cc in range(4):
                self.mm(self.ps[b][:, :], aT[:, cc, tt * 128:(tt + 1) * 128], W[:, cc, :], cc == 0, cc == 3,
                        r=aTk + Wk, w=[("ps", b)])
            self.cp("act" if tt % 2 else "dve", dst[:, tt, :], self.ps[b][:, :], r=[("ps", b)], w=dk)
    for (o, L) in G["seqs"]:
        cl, clk = self.tbl(L, 0)
        sl_, slk = self.tbl(L, 1)
        N = min(L, 512)
        ntc = L // 128
        for cb4 in range(4):
            for n0 in range(0, L, N):
                b = self.rot()
                i = 0
                for (src, sk2, tab, tabk) in ((ac, ack, cl, clk), (as_, ask, sl_, slk)):
                    for tc in range(ntc):
                        self.mm(self.ps[b][:, 0:N], src[:, o // 128 + tc, cb4 * 128:(cb4 + 1) * 128],
                                tab(tc, n0, n0 + N), i == 0, i == 2 * ntc - 1,
                                r=sk2 + tabk(tc), w=[("ps", b)])
                        i += 1
                t0 = o + n0
                self.tt("dve", self.mix[:, cb4, t0:t0 + N], self.ps[b][:, 0:N], gA[:, cb4, t0:t0 + N], ALU.mult,
                        r=[("ps", b)] + gAk, w=self.mx_keys(cb4, t0, t0 + N))


for _n, _f in (("proj_fm", _proj_fm), ("proj_tm", _proj_tm), ("b2_keys", _b2_keys), ("load_tables", _load_tables),
               ("tbl", _tbl), ("branch_a", _branch_a)):
    setattr(KB, _n, _f)


def _dup64(ap2d, c0):
    return bass.AP(ap2d.tensor, ap2d.offset + c0, [list(ap2d.ap[0]), [0, 2], [1, 64]])


def _branch_attn(self, l, g, which):
    self.nrot = 6
    try:
        _branch_attn_body(self, l, g, which)
    finally:
        self.nrot = 8


def _branch_attn_body(self, l, g, which):
    G = self.group(g)
    T, NT = G["T"], G["T"] // 128
    sample = (g == "s")
    is_d = (which == "d")
    base = O_D if is_d else O_C
    blk0 = 12 if is_d else 8
    Tk = T + (256 if sample else 0)
    NKT = Tk // 128
    sm = self.small
    off = [0]

    def take(nwords, dt=F32, shape=None):
        a = self.arena(off[0], nwords, dt, shape)
        off[0] += nwords
        return a
    qT, qTk = take(2 * T, BF16, [4, T])
    gT, gTk = take(2 * T, BF16, [4, T])
    kT2, kT2k = take(2 * Tk, BF16, [2, 2, Tk])
    vaug, vaugk = take(NKT * 256, BF16, [NKT, 2, 2, 128])
    NE = 5
    Es = [take(256, BF16) for _ in range(NE)]
    raws = [take(512) for _ in range(2)]
    tmps = [take(512) for _ in range(2)]
    xbs = [take(256, BF16) for _ in range(2)]
    rec, reck = take(T)
    stage, stagek = take(256, F32, [2, 128])
    kb16, kb16k = take(128, BF16, [2, 128])
    kn_, knk = take(256, BF16)
    if not sample:
        kout, koutk = take(NT * 128, F32, [NT, 128])
        vout, voutk = take(NT * 128, F32, [NT, 128])
        knbc, knbck = take(64)
    cnt = {"raw": 0, "E": 0}
    gq, gk, esink = sm[:, 64:65], sm[:, 65:66], sm[:, 72:80]
    if is_d:
        for half in range(2):
            self.dma("sp", sm[half * 64:half * 64 + 64, 64:65], self.qn[l], w=["attnp"])
            self.dma("sp", sm[half * 64:half * 64 + 64, 65:66], self.kn[l], w=["attnp"])
        if not sample:
            self.dma("sp", knbc, self.knr[l].broadcast_to([128, 64]), w=knbck)
    else:
        self.dma("sp", esink, self.sink[l].broadcast_to([128, 8]), w=["attnp"])
        self.act(esink, esink, AF.Exp, r=["attnp"], w=["attnp"])
    self.op("dve", lambda e: e.memset(vaug[:, :, :, 0, 64:128], 1.0), w=vaugk)
    self.op("dve", lambda e: e.memset(vaug[:, :, :, 1, 0:64], 1.0), w=vaugk)

    def finish_qk(b, dst, dstk, tok0, gcol):
        i = cnt["raw"] % 2
        cnt["raw"] += 1
        raw, rawk = raws[i]
        tmp, tmpk = tmps[i]
        xb, xbk = xbs[i]
        if not is_d and not sample:
            self.cp("act", dst, self.ps[b][:, :], r=[("ps", b)], w=dstk)
            return
        self.cp("act", raw, self.ps[b][:, :], r=[("ps", b)], w=rawk)
        cur, curk = raw, rawk
        if is_d:
            self.act(xb, raw, AF.Square, r=rawk, w=xbk)
            b2 = self.rot()
            self.mm(self.ps[b2][:, :], self.cbv("onesblk"), xb, True, True, r=xbk + ["cb"], w=[("ps", b2)])
            self.act(tmp, self.ps[b2][:, :], AF.Sqrt, r=[("ps", b2), "consts"], w=tmpk,
                     bias=self.c_rmseps, scale=1.0 / 64.0)
            self.op("dve", lambda e, tmp=tmp: e.reciprocal(tmp, tmp), r=tmpk, w=tmpk, cost=2.2)
            if sample:
                self.stt("dve", raw, raw, gcol, tmp, ALU.mult, ALU.mult, r=rawk + tmpk + ["attnp"], w=rawk)
            else:
                self.stt("dve", dst, raw, gcol, tmp, ALU.mult, ALU.mult, r=rawk + tmpk + ["attnp"], w=dstk)
                return
        self.cp("act", xb, raw, r=rawk, w=xbk)
        b3 = self.rot()
        self.mm(self.ps[b3][:, :], self.cbv("prot"), xb, True, True, r=xbk + ["cb"], w=[("ps", b3)])
        co, _ = CF["cos"]
        so, _ = CF["sin"]
        self.tt("dve", tmp, raw, self.cf[:, co + tok0:co + tok0 + 512], ALU.mult, r=rawk + ["cf"], w=tmpk)
        self.tt("dve", raw, self.ps[b3][:, :], self.cf[:, so + tok0:so + tok0 + 512], ALU.mult,
                r=[("ps", b3), "cf"], w=rawk)
        self.tt("dve", dst, raw, tmp, ALU.add, r=rawk + tmpk, w=dstk)

    self.op("dve", lambda e: e.memset(kT2[64:128, :, 0, :], 0.0), w=kT2k, cost=1.5)
    self.op("dve", lambda e: e.memset(kT2[0:64, :, 1, :], 0.0), w=kT2k, cost=1.5)

    def spread_k(src, srck, c0, n):
        self.cp("act", kT2[0:64, 0, 0, c0:c0 + n], src[0:64, 0:n], r=srck, w=kT2k)
        self.cp("dve", kT2[64:128, 0, 1, c0:c0 + n], src[0:64, 0:n], r=srck, w=kT2k)
        self.cp("dve", kT2[0:64, 1, 0, c0:c0 + n], src[64:128, 0:n], r=srck, w=kT2k)
        self.cp("act", kT2[64:128, 1, 1, c0:c0 + n], src[64:128, 0:n], r=srck, w=kT2k)

    def h_k(tc, b):
        finish_qk(b, kn_, knk, tc * 512, gk)
        spread_k(kn_, knk, tc * 512, 512)
    self.proj_fm(l, base + 512, T, h_k)
    for s in range(4):
        def h_q(tc, b, s=s):
            finish_qk(b, qT[:, s, tc * 512:(tc + 1) * 512], qTk, tc * 512, gq)
        self.proj_fm(l, base + s * 128, T, h_q)
    def put_v(src, srck, tt):
        sv = src.rearrange("p (a b) -> p a b", b=64)
        self.cp("act", vaug[:, tt, :, 0, 0:64], sv, r=srck, w=vaugk)
        self.cp("act", vaug[:, tt, :, 1, 64:128], sv, r=srck, w=vaugk)

    def h_v(tt, b):
        put_v(self.ps[b][:, 0:128], [("ps", b)], tt)
        if not sample:
            self.cp("act", vout[:, tt, :], self.ps[b][:, 0:128], r=[("ps", b)], w=voutk)
    self.proj_tm(l, base + 640, T, h_v)
    if not sample:
        def h_ko(tt, b):
            if not is_d:
                self.cp("act", kout[:, tt, :], self.ps[b][:, 0:128], r=[("ps", b)], w=koutk)
                return
            raw, rawk = raws[tt % 2]
            tmp, tmpk = tmps[tt % 2]
            self.cp("act", raw[:, 0:128], self.ps[b][:, 0:128], r=[("ps", b)], w=rawk)
            self.act(tmp[:, 0:128], raw[:, 0:128], AF.Square, r=rawk, w=tmpk)
            self.op("dve", lambda e, tmp=tmp: e.tensor_reduce(
                tmp[:, 128:130], tmp[:, 0:128].rearrange("p (a b) -> p a b", b=64),
                mybir.AxisListType.X, ALU.add), r=tmpk, w=tmpk)
            self.act(tmp[:, 128:130], tmp[:, 128:130], AF.Sqrt, r=tmpk + ["consts"], w=tmpk,
                     bias=self.c_rmseps, scale=1.0 / 64.0)
            self.op("dve", lambda e, tmp=tmp: e.reciprocal(tmp[:, 128:130], tmp[:, 128:130]), r=tmpk, w=tmpk)
            for kvh in range(2):
                self.stt("dve", kout[:, tt, kvh * 64:(kvh + 1) * 64], raw[:, kvh * 64:(kvh + 1) * 64],
                         tmp[:, 128 + kvh:129 + kvh], knbc, ALU.mult, ALU.mult, r=rawk + tmpk + knbck, w=koutk)
        self.proj_tm(l, base + 512, T, h_ko)
        kname, vname = ("ndk", "ndv") if is_d else ("nck", "ncv")
        for si in range(2):
            self.dma("sp", self.kv_out[kname][si, l].rearrange("(a p) c -> p a c", p=128),
                     kout[:, 2 * si:2 * si + 2, :], r=koutk, store=True)
            self.dma("sp", self.kv_out[vname][si, l].rearrange("(a p) c -> p a c", p=128),
                     vout[:, 2 * si:2 * si + 2, :], r=voutk, store=True)
    if sample:
        kc_d, vc_d = (self.cache["ck_d"], self.cache["cv_d"]) if is_d else (self.cache["ck_c"], self.cache["cv_c"])
        self.dma("sp", stage, vc_d[l].rearrange("(a p) c -> p a c", p=128), w=stagek)
        for c in range(2):
            put_v(stage[:, c, :], stagek, NT + c)
        self.dma("sp", stage, kc_d[l].rearrange("(a p) c -> p a c", p=128), r=[], w=stagek)
        self.cp("act", kb16, stage, r=stagek, w=kb16k)
        for c in range(2):
            b = self.rot()
            self.tr(self.psb[b][:, 0:128], kb16[:, c, :], self.cbv("ident"), r=kb16k + ["cb"], w=[("ps", b)])
            self.cp("act", kn_[:, 0:128], self.psb[b][:, 0:128], r=[("ps", b)], w=knk)
            spread_k(kn_, knk, T + c * 128, 128)
    for s in range(4):
        def h_g(tc, b, s=s):
            self.act(gT[:, s, tc * 512:(tc + 1) * 512], self.ps[b][:, :], AF.Silu, r=[("ps", b)], w=gTk)
        self.proj_fm(l, base + 768 + s * 128, T, h_g)

    def score_pv(h, kb, q0, q1, acc_b, acc_c0, start, stop, masks=()):
        kvh, s, half = h // 4, h // 2, h % 2
        rows = slice(half * 64, half * 64 + 64)
        n = q1 - q0
        b = self.rot()
        self.mm(self.ps[b][:, 0:n], kT2[:, kvh, half, kb * 128:(kb + 1) * 128], qT[:, s, q0:q1], True, True,
                r=kT2k + qTk, w=[("ps", b)])
        E, Ek = Es[cnt["E"] % NE]
        cnt["E"] += 1
        self.act(E[:, 0:n], self.ps[b][:, 0:n], AF.Exp, r=[("ps", b)], w=Ek, scale=0.125)
        for (c0, mname) in masks:
            self.tt("dve", E[:, c0:c0 + 128], E[:, c0:c0 + 128], self.cbv(mname), ALU.mult, r=Ek + ["cb"], w=Ek)
        self.mm(self.ps[acc_b][:, acc_c0:acc_c0 + n], vaug[:, kb, kvh, half, :], E[:, 0:n], start, stop,
                r=vaugk + Ek, w=[("ps", acc_b)])

    def finalize(h, acc_b, ncols, tok0):
        s, half = h // 2, h % 2
        rows = slice(half * 64, half * 64 + 64)
        drows = slice(64, 128) if half == 0 else slice(0, 64)
        tmp, tmpk = tmps[h % 2]
        if is_d:
            self.op("dve", lambda e: e.reciprocal(rec[drows, tok0:tok0 + ncols], self.ps[acc_b][drows, 0:ncols]),
                    r=[("ps", acc_b)], w=reck, cost=2.2)
        else:
            self.ts("dve", rec[drows, tok0:tok0 + ncols], self.ps[acc_b][drows, 0:ncols], esink[drows, h:h + 1],
                    None, ALU.add, None, r=[("ps", acc_b), "attnp"], w=reck)
            self.op("dve", lambda e: e.reciprocal(rec[drows, tok0:tok0 + ncols], rec[drows, tok0:tok0 + ncols]),
                    r=reck, w=reck, cost=2.2)
        self.tt("dve", tmp[rows, 0:ncols], self.ps[acc_b][rows, 0:ncols], rec[drows, tok0:tok0 + ncols], ALU.mult,
                r=[("ps", acc_b)] + reck, w=tmpk)
        self.tt("dve", self.mix[rows, blk0 + s, tok0:tok0 + ncols], tmp[rows, 0:ncols],
                gT[rows, s, tok0:tok0 + ncols], ALU.mult, r=tmpk + gTk,
                w=self.mx_keys(blk0 + s, tok0, tok0 + ncols))

    accn = [0]

    def next_acc():
        b = 6 + (accn[0] % 2)
        accn[0] += 1
        return b

    for h in range(8):
        if not sample:
            acc = next_acc()
            for si, (o, L) in enumerate(G["seqs"]):
                for kb in range(2):
                    score_pv(h, (o // 128) + kb, o, o + L, acc, si * 256, kb == 0, kb == 1)
            finalize(h, acc, 512, 0)
        elif is_d:
            for qc in range(2):
                acc = next_acc()
                for kb in range(NKT):
                    score_pv(h, kb, qc * 512, qc * 512 + 512, acc, 0, kb == 0, kb == NKT - 1)
                finalize(h, acc, 512, qc * 512)
        else:
            for qc in range(2):
                acc = next_acc()
                for kb in (8, 9):
                    score_pv(h, kb, qc * 512, qc * 512 + 512, acc, 0, kb == 8, False)
                for kb in range(8):
                    q0, q1 = max(0, kb - 1) * 128, min(8, kb + 2) * 128
                    a, bq = max(q0, qc * 512), min(q1, qc * 512 + 512)
                    if a >= bq:
                        continue
                    masks = []
                    if kb >= 1 and a <= (kb - 1) * 128 < bq:
                        masks.append(((kb - 1) * 128 - a, "triu"))
                    if kb <= 6 and a <= (kb + 1) * 128 < bq:
                        masks.append(((kb + 1) * 128 - a, "tril"))
                    score_pv(h, kb, a, bq, acc, a % 512, False, False, masks)
                finalize(h, acc, 512, qc * 512)


def _proj_fm2(self, l, col0, T, handler, lhs_of=None, slab=None):
    if slab is None:
        sl, sk = self.load_win(l, col0)
    else:
        sl, sk = slab
    for tc in range(T // 512):
        b = self.rot()
        for kc in range(KC):
            lhsT = sl[:, kc, :] if lhs_of is None else lhs_of(sl, kc)
            self.mm(self.ps[b][:, :], lhsT, self.hm(kc, tc * 512, (tc + 1) * 512), kc == 0, kc == KC - 1,
                    r=sk + self.hm_keys(kc, tc * 512, tc * 512 + 512), w=[("ps", b)])
        handler(tc, b)
    return sl, sk


KB.proj_fm = _proj_fm2
KB.branch_attn = _branch_attn
KB.branch_c = lambda self, l, g: _branch_attn(self, l, g, "c")
KB.branch_d = lambda self, l, g: _branch_attn(self, l, g, "d")


MAGIC = 12582912.0


def _branch_b(self, l, g):
    G = self.group(g)
    T, L, seqs = G["T"], G["L"], G["seqs"]
    NS = len(seqs)
    n128 = L // 128
    W = NS * 128
    off = [0]

    def take(nwords, dt=F32, shape=None):
        a = self.arena(off[0], nwords, dt, shape)
        off[0] += nwords
        return a
    h2T, h2Tk = take(1024)
    w3p, w3pk = take(128)
    edp, edpk = take(128)
    skp, skpk = take(128)
    cfm, cfmk = take(64)
    fwt, fwtk = take(192)
    rsb, rsbk = take(128)
    g1q0, g1q0k = take(128)
    rawb, rawbk = take(512, BF16)
    cv, cvk = take(1024)
    vb, vbk = take(512, BF16)
    x1b, x1bk = take(512, BF16)
    x2g, x2gk = take(512, BF16)
    gB, gBk = take(512, BF16)
    vtm, vtmk = take(n128 * W // 2, BF16, [n128, W])
    AB, ABk = take(n128 * 2 * W, F32, [n128, 2, W])
    G1, G1k = take(n128 * 128, F32, [n128, 128])
    G2, G2k = take(n128 * 128, F32, [n128, 128])
    Pb, Pbk = take(n128 * W // 2, BF16, [n128, W])
    Qb, Qbk = take(n128 * W // 2, BF16, [n128, W])
    hw, hwk = take(1024)
    htm, htmk = take(n128 * 64, BF16, [n128, 128])
    zb, zbk = take(512, BF16)
    hw3 = hw[:, 0:n128 * 128].rearrange("p (a b) -> p a b", b=128)
    cm, cmk = self.tbl(L, 2)
    smm, smk = self.tbl(L, 3)
    o_, w_ = CB["smt0_%d" % L]
    smt0 = self.cb[:, o_:o_ + w_].rearrange("p (a b) -> p a b", b=128)
    o_, w_ = CB["sm0_%d" % L]
    sm0 = self.cb[:, o_:o_ + w_]
    ndist = self.cfv("ndist%d" % L)
    cab = self.cfv("cab%d" % L)
    feats = self.cfv("feats%d" % L)
    if g == "s":
        self.load_tables(2)
    self.load_fm(cfm[:, 0:36], self.conv_w[l], 36, cfmk, 13312)
    self.load_fm(cfm[:, 36:48], self.conv_b[l], 12, cfmk, 13440)
    self.dma("sp", fwt[0:33, 0:64], self.fw1[l], w=fwtk)
    self.dma("sp", fwt[0:64, 64:128], self.fw2[l], w=fwtk)
    self.dma("sp", fwt[0:64, 128:129], self.fb1[l], w=fwtk)
    self.dma("sp", fwt[0:64, 129:130], self.fb2[l], w=fwtk)
    t0 = hw[:, 0:512]
    t1 = hw[:, 512:1024]
    h1T = cv
    N = min(L, 512)
    for (dst, dstk, wl, kdim, bcol, src, srck) in ((h1T, cvk, fwt[0:33, 0:64], 33, fwt[0:64, 128:129], feats, ["cf"]),
                                                    (h2T, h2Tk, fwt[0:64, 64:128], 64, fwt[0:64, 129:130], h1T, cvk)):
        for n0 in range(0, L, N):
            b = self.rot()
            self.mm(self.ps[b][0:64, 0:N], wl, src[0:kdim, n0:n0 + N], True, True, r=fwtk + srck, w=[("ps", b)])
            self.ts("dve", t0[0:64, 0:N], self.ps[b][0:64, 0:N], bcol, None, ALU.add, None,
                    r=[("ps", b)] + fwtk, w=hwk)
            self.ts("dve", t1[0:64, 0:N], t0[0:64, 0:N], 1.0 / TWO_PI, MAGIC, ALU.mult, ALU.add, r=hwk, w=hwk)
            self.ts("dve", t1[0:64, 0:N], t1[0:64, 0:N], -MAGIC, None, ALU.add, None, r=hwk, w=hwk)
            self.stt("dve", t0[0:64, 0:N], t1[0:64, 0:N], -TWO_PI, t0[0:64, 0:N], ALU.mult, ALU.add, r=hwk, w=hwk)
            self.act(dst[0:64, n0:n0 + N], t0[0:64, 0:N], AF.Sin, r=hwk, w=dstk, scale=0.999999)
    h2keep = h2T

    def short_conv(which, cbi, dst, dstk):
        jb = which * 4 + cbi
        w0, w1c, w2c = cfm[:, jb:jb + 1], cfm[:, 12 + jb:13 + jb], cfm[:, 24 + jb:25 + jb]
        bc = cfm[:, 36 + jb:37 + jb]

        def h_raw(tc, b):
            self.cp("act", rawb[:, tc * 512:(tc + 1) * 512], self.ps[b][:, :], r=[("ps", b)], w=rawbk)
        self.proj_fm(l, (O_BV, O_BX1, O_BX2)[which] + cbi * 128, T, h_raw)
        for (o, Ls) in seqs:
            e = o + Ls
            self.ts("dve", cv[:, o:e], rawb[:, o:e], w1c, bc, ALU.mult, ALU.add, r=rawbk + cfmk, w=cvk)
            self.stt("dve", cv[:, o + 1:e], rawb[:, o:e - 1], w0, cv[:, o + 1:e], ALU.mult, ALU.add,
                     r=rawbk + cfmk + cvk, w=cvk)
            self.stt("dve", dst[:, o:e - 1], rawb[:, o + 1:e], w2c, cv[:, o:e - 1], ALU.mult, ALU.add,
                     r=rawbk + cfmk + cvk, w=dstk)
            self.cp("dve", dst[:, e - 1:e], cv[:, e - 1:e], r=cvk, w=dstk)

    def fwd_dft(src_tm, srck, dstAB, dstk):
        width = src_tm.shape[-1]
        for fb in range(n128):
            b = self.rot()
            for tc in range(n128):
                self.mm(self.ps[b][:, 0:width], cm(tc, fb * 128, fb * 128 + 128), src_tm[:, tc, :],
                        tc == 0, tc == n128 - 1, r=cmk(tc) + srck, w=[("ps", b)])
            for tc in range(n128):
                lhs = smt0[:, tc, :] if fb == 0 else smm(tc, fb * 128, fb * 128 + 128)
                self.mm(self.ps[b][:, 256:256 + width], lhs, src_tm[:, tc, :],
                        tc == 0, tc == n128 - 1, r=(["cb"] if fb == 0 else smk(tc)) + srck, w=[("ps", b)])
            pv = self.ps[b][:, :].rearrange("p (a b) -> p a b", a=2)[:, :, 0:width]
            self.cp("act", dstAB[:, fb, :, 0:width], pv, r=[("ps", b)], w=dstk)

    def make_filter(conv, cbi):
        c0 = conv * 512 + cbi * 128
        self.dma("sp", w3p[0:64, :], self.fw3[l][:, c0:c0 + 128], w=w3pk)
        self.dma("sp", edp, self.ldec[l][:, c0:c0 + 128].broadcast_to([128, 128]), w=edpk)
        self.dma("sp", skp[0:1, :], self.skip[l][:, c0:c0 + 128], w=skpk)
        self.act(edp, edp, AF.Exp, r=edpk, w=edpk)
        for tc in range(n128):
            self.act(hw3[:, tc, :], edp, AF.Exp, r=edpk + ["cf"], w=hwk, scale=ndist[:, tc:tc + 1])
        for q in range(0, n128, 4):
            b = self.rot()
            nq = min(4, n128 - q)
            for j in range(nq):
                self.mm(self.ps[b][:, j * 128:(j + 1) * 128], h2keep[0:64, (q + j) * 128:(q + j + 1) * 128],
                        w3p[0:64, :], True, True, r=h2Tk + w3pk, w=[("ps", b)])
            self.stt("dve", hw[:, q * 128:(q + nq) * 128], hw[:, q * 128:(q + nq) * 128], DECAY_SHIFT,
                     self.ps[b][:, 0:nq * 128], ALU.add, ALU.mult, r=hwk + [("ps", b)], w=hwk)
        hsq = AB[:, :, 0, 0:128]
        self.act(hsq, hw3, AF.Square, r=hwk, w=ABk)
        b = self.rot()
        for tc in range(n128):
            self.mm(self.ps[b][:, 0:128], self.cfv("ones"), hsq[:, tc, :], tc == 0, tc == n128 - 1,
                    r=ABk + ["cf"], w=[("ps", b)])
        self.act(rsb, self.ps[b][:, 0:128], AF.Sqrt, r=[("ps", b), "consts"], w=rsbk, bias=self.c_rmseps)
        self.op("dve", lambda e: e.reciprocal(rsb, rsb), r=rsbk, w=rsbk, cost=0.6)
        rs_b = bass.AP(rsb.tensor, rsb.offset, [list(rsb.ap[0]), [0, n128], [1, 128]])
        self.tt("dve", hw3, hw3, rs_b, ALU.mult, r=hwk + rsbk, w=hwk)
        mid = (L // 2) // 128
        self.tt("dve", hw3[0:1, mid, :], hw3[0:1, mid, :], skp[0:1, :], ALU.add, r=hwk + skpk, w=hwk)
        self.cp("act", htm, hw3, r=hwk, w=htmk)
        fwd_dft(htm, htmk, AB, ABk)
        Ah, Bh = AB[:, :, 0, 0:128], AB[:, :, 1, 0:128]
        ca, cbb, nca = cab[:, 0:1], cab[:, 1:2], cab[:, 2:3]
        self.ts("dve", G1, Ah, ca, None, ALU.mult, None, r=ABk + ["cf"], w=G1k)
        self.stt("dve", G1, Bh, cbb, G1, ALU.mult, ALU.add, r=ABk + ["cf"] + G1k, w=G1k)
        self.ts("dve", G2, Ah, cbb, None, ALU.mult, None, r=ABk + ["cf"], w=G2k)
        self.stt("dve", G2, Bh, nca, G2, ALU.mult, ALU.add, r=ABk + ["cf"] + G2k, w=G2k)
        self.ts("dve", g1q0[0:1, :], AB[0:1, 0, 1, 0:128], 1.0 / (2 * L), None, ALU.mult, None, r=ABk, w=g1q0k)
        self.ts("dve", G1[0:1, 0, :], G1[0:1, 0, :], 0.5, None, ALU.mult, None, r=G1k, w=G1k)
        self.op("dve", lambda e: e.memset(G2[0:1, 0, :], 0.0), w=G2k)

    def long_conv(src, srck, mulsrc, mulk, dst_of, dstk_of):
        per_bank = 8
        idx = [(tc, si) for tc in range(n128) for si in range(NS)]
        for q in range(0, len(idx), per_bank):
            b = self.rot()
            grp = idx[q:q + per_bank]
            for j, (tc, si) in enumerate(grp):
                o = seqs[si][0]
                self.tr(self.psb[b][:, j * 128:(j + 1) * 128], src[:, o + tc * 128:o + (tc + 1) * 128],
                        self.cbv("ident"), r=srck + ["cb"], w=[("ps", b)])
            flat = vtm.rearrange("p a b -> p (a b)")
            self.cp("act", flat[:, q * 128:(q + len(grp)) * 128], self.psb[b][:, 0:len(grp) * 128],
                    r=[("ps", b)], w=vtmk)
        fwd_dft(vtm, vtmk, AB, ABk)
        A_, B_ = AB[:, :, 0, :], AB[:, :, 1, :]
        if NS == 1:
            g1b, g2b = G1, G2
            t1v = hw[:, 0:n128 * W].rearrange("p (a b) -> p a b", b=W)
            t2v = cv[:, 0:n128 * W].rearrange("p (a b) -> p a b", b=W)
            A4, B4, P4, Q4 = A_, B_, Pb, Qb
        else:
            def bc4(t):
                return bass.AP(t.tensor, t.offset, [list(t.ap[0]), [128, n128], [0, NS], [1, 128]])
            g1b, g2b = bc4(G1), bc4(G2)
            r4 = lambda t: t.rearrange("p a (s c) -> p a s c", s=NS)
            t1v = r4(hw[:, 0:n128 * W].rearrange("p (a b) -> p a b", b=W))
            t2v = r4(cv[:, 0:n128 * W].rearrange("p (a b) -> p a b", b=W))
            A4, B4, P4, Q4 = r4(A_), r4(B_), r4(Pb), r4(Qb)
        rk = ABk + G1k + G2k
        self.tt("dve", t1v, A4, g1b, ALU.mult, r=rk, w=hwk)
        self.tt("dve", t2v, B4, g2b, ALU.mult, r=rk, w=cvk)
        self.tt("dve", P4, t1v, t2v, ALU.add, r=hwk + cvk, w=Pbk)
        self.tt("dve", t1v, B4, g1b, ALU.mult, r=rk, w=hwk)
        self.tt("dve", t2v, A4, g2b, ALU.mult, r=rk, w=cvk)
        self.tt("dve", Q4, t1v, t2v, ALU.subtract, r=hwk + cvk, w=Qbk)
        if NS == 1:
            self.tt("dve", Qb[0:1, 0, :], AB[0:1, 0, 1, :], g1q0[0:1, :], ALU.mult, r=ABk + g1q0k + Qbk, w=Qbk)
        else:
            for si in range(NS):
                self.tt("dve", Qb[0:1, 0, si * 128:(si + 1) * 128], AB[0:1, 0, 1, si * 128:(si + 1) * 128],
                        g1q0[0:1, :], ALU.mult, r=ABk + g1q0k + Qbk, w=Qbk)
        Nn = min(L, 512)
        for si, (o, Ls) in enumerate(seqs):
            for n0 in range(0, Ls, Nn):
                b = self.rot()
                i = 0
                for fb in range(n128):
                    self.mm(self.ps[b][:, 0:Nn], Pb[:, fb, si * 128:(si + 1) * 128], cm(fb, n0, n0 + Nn),
                            i == 0, False, r=Pbk + cmk(fb), w=[("ps", b)])
                    i += 1
                for fb in range(n128):
                    rhs = sm0[:, n0:n0 + Nn] if fb == 0 else smm(fb, n0, n0 + Nn)
                    self.mm(self.ps[b][:, 0:Nn], Qb[:, fb, si * 128:(si + 1) * 128], rhs,
                            False, fb == n128 - 1, r=Qbk + (["cb"] if fb == 0 else smk(fb)), w=[("ps", b)])
                t0_ = o + n0
                self.tt("dve", dst_of(t0_, Nn), self.ps[b][:, 0:Nn], mulsrc[:, t0_:t0_ + Nn], ALU.mult,
                        r=[("ps", b)] + mulk, w=dstk_of(t0_, Nn))

    for cbi in range(4):
        short_conv(0, cbi, vb, vbk)
        short_conv(1, cbi, x1b, x1bk)
        short_conv(2, cbi, x2g, x2gk)

        def h_g(tc, b):
            self.act(gB[:, tc * 512:(tc + 1) * 512], self.ps[b][:, :], AF.Silu, r=[("ps", b)], w=gBk)
        self.proj_fm(l, O_BG + cbi * 128, T, h_g)
        self.tt("dve", x2g[:, 0:T], x2g[:, 0:T], gB[:, 0:T], ALU.mult, r=x2gk + gBk, w=x2gk)
        make_filter(0, cbi)
        long_conv(vb, vbk, x1b, x1bk, lambda t0_, n: zb[:, t0_:t0_ + n], lambda t0_, n: zbk)
        make_filter(1, cbi)
        long_conv(zb, zbk, x2g, x2gk, lambda t0_, n: self.mix[:, 4 + cbi, t0_:t0_ + n],
                  lambda t0_, n: self.mx_keys(4 + cbi, t0_, t0_ + n))


DECAY_SHIFT = 0.05
KB.branch_b = _branch_b


def _phase_c(self, l, g):
    G = self.group(g)
    self.bg_flush(l)
    T, NT = G["T"], G["T"] // 128
    cj = G["cond"]
    src = self.x_in[g] if l == 0 else self.xm[g]
    dst = self.xm[g] if l == 0 else self.y_out[g]
    wv = [self.big1[:].rearrange("p a b -> p (a b)").rearrange("p (k n) -> p k n", n=2048),
          self.big2[:].rearrange("p a b -> p (a b)").rearrange("p (k n) -> p k n", n=2048)]

    def wkeys(kc):
        j = kc % 8
        if kc < 8:
            return [("hm", 2 * j + d, t) for d in range(2) for t in range(8)]
        return [("b2", 2 * j), ("b2", 2 * j + 1)]
    if g == "s":
        for kc in range(KC):
            self.dma("pool", wv[kc // 8][:, kc % 8, :], self.w_out[l][kc * 128:(kc + 1) * 128, :], w=wkeys(kc))
    xts = [self.arena(0, 2048), self.arena(2048, 2048), self.arena(11392, 2048)]
    gbc, gbck = self.arena(4096, 2048)
    lng, lngk = self.arena(6144, 2048)
    lnb, lnbk = self.arena(8192, 2048)
    gblk, gblkk = self.arena(10240, 128)
    tbs = [self.arena(10368, 512), self.arena(10880, 512), self.arena(13440, 512), self.arena(13952, 512)]
    sm = self.small
    self.dma("sp", lng, self.ln_g[l].broadcast_to([128, D]), w=lngk)
    self.dma("sp", lnb, self.ln_b[l].broadcast_to([128, D]), w=lnbk)
    for q in range(4):
        b = self.rot()
        for j in range(4):
            blk = q * 4 + j
            gcol = self.modfm[:, l, 32 + blk, cj:cj + 1]
            gsrc = bass.AP(gcol.tensor, gcol.offset, [list(gcol.ap[0]), [0, 128]])
            self.cp("dve", gblk, gsrc, r=[("modfm", l, 2)], w=gblkk)
            self.mm(self.ps[b][:, j * 128:(j + 1) * 128], gblk, self.cfv("ident"), True, True,
                    r=gblkk + ["cf"], w=[("ps", b)])
        self.act(gbc[:, q * 512:(q + 1) * 512], self.ps[b][:, :], AF.Identity, r=[("ps", b)], w=gbck,
                 scale=1.0 / ALPHA)
    SB = (0, 32, 96)
    for tt in range(NT):
        bi = tt % 3
        xt, xk = xts[bi]
        sb0 = SB[bi]
        st = sm[:, sb0:sb0 + 24].rearrange("p (a b) -> p a b", b=6)
        mv = sm[:, sb0 + 24:sb0 + 26]
        rstd = sm[:, sb0 + 26:sb0 + 27]
        nmr = sm[:, sb0 + 27:sb0 + 28]
        sk = [("smA", bi)]
        rows = slice(tt * 128, (tt + 1) * 128)
        self.dma("sp", xt, src[rows, :], r=[("xm", g, tt)] if l > 0 else [], w=xk)
        for nb in range(4):
            b = self.rot()
            for kc in range(KC):
                self.mm(self.ps[b][:, :], self.mix[:, kc, rows], wv[kc // 8][:, kc % 8, nb * 512:(nb + 1) * 512],
                        kc == 0, kc == KC - 1, r=self.mx_keys(kc, tt * 128, tt * 128 + 128) + wkeys(kc),
                        w=[("ps", b)])
            tb, tbk = tbs[nb % 4]
            cols = slice(nb * 512, (nb + 1) * 512)
            self.tt("dve", tb, self.ps[b][:, :], gbc[:, cols], ALU.mult, r=[("ps", b)] + gbck, w=tbk)
            self.tt("dve", xt[:, cols], xt[:, cols], tb, ALU.add, r=xk + tbk, w=xk)
        for j in range(4):
            self.op("dve", lambda e, j=j, st=st, xt=xt: e.bn_stats(st[:, j, :], xt[:, j * 512:(j + 1) * 512]),
                    r=xk, w=sk, cost=0.65)
        self.op("dve", lambda e, st=st, mv=mv: e.bn_aggr(mv, st), r=sk, w=sk)
        self.act(rstd, mv[:, 1:2], AF.Sqrt, r=sk + ["consts"], w=sk, bias=self.c_lneps_c)
        self.op("dve", lambda e, rstd=rstd: e.reciprocal(rstd, rstd), r=sk, w=sk)
        self.stt("dve", nmr, mv[:, 0:1], -1.0, rstd, ALU.mult, ALU.mult, r=sk, w=sk)
        self.act(xt, xt, AF.Identity, r=xk + sk, w=xk, bias=nmr, scale=rstd)
        self.tt("dve", xt, xt, lng, ALU.mult, r=xk + lngk, w=xk)
        self.tt("dve", xt, xt, lnb, ALU.add, r=xk + lnbk, w=xk)
        self.dma("sp", dst[rows, :], xt, r=xk, w=[("xm", g, tt)] if l == 0 else [], store=(l == DEPTH - 1))


KB.phase_c = _phase_c


def build_full(debug=None):
    kb = KB(debug=debug)
    kb.setup()
    kb.bg_enable = True
    for l in range(DEPTH):
        for g in ("s", "p"):
            kb.phase_a(l, g)
            kb.branch_a(l, g)
            kb.branch_b(l, g)
            kb.branch_c(l, g)
            kb.branch_d(l, g)
            kb.phase_c(l, g)
    kb.s.emit()
    return kb


_KB = None


def kernel(**inputs):
    global _KB
    if _KB is None:
        _KB = build_full()
    kb = _KB
    sh = shared_inputs(inputs)
    in_maps = [core_inputs(inputs, i, sh) for i in range(8)]
    res = run_bass_kernel_spmd(kb.nc, in_maps, core_ids=list(range(8)))
    rs = res.results
    y_p = np.concatenate([np.asarray(r["y_p"], np.float32).reshape(2, 256, D) for r in rs], 0)
    y_s = np.stack([np.asarray(r["y_s"], np.float32) for r in rs], 0)
    kv = []
    for name in ("nck", "ncv", "ndk", "ndv"):
        kv.append(np.concatenate([np.asarray(r[name], np.float32).reshape(2, 2, 256, 2, 64) for r in rs], 0))
    return (y_p, y_s, kv[0], kv[1], kv[2], kv[3])
```

```python
import math
import os
from contextlib import ExitStack

import numpy as np
import ml_dtypes

import concourse.bass as bass
import concourse.mybir as mybir
from concourse.bass_utils import run_bass_kernel_spmd

F32 = mybir.dt.float32
BF16 = mybir.dt.bfloat16
AF = mybir.ActivationFunctionType
ALU = mybir.AluOpType

D = 2048
KC = 16
INW = 5632
DEPTH = 2
HD = 64
ALPHA = (2 * DEPTH) ** 0.25
LN_EPS = 1e-5
RMS_EPS = 1e-6
PI = math.pi
TWO_PI = 2.0 * math.pi
SIN_OFF = PI + 16 * TWO_PI

O_A, O_AG = 0, 512
O_BV, O_BX1, O_BX2, O_BG = 1024, 1536, 2048, 2560
O_C, O_D = 3072, 4352


class Op:
    __slots__ = ("eng", "fn", "deps", "dma", "sem", "val", "needed", "idx", "cost", "tbl", "done")

    def __init__(self, eng, fn, dma):
        self.eng = eng
        self.fn = fn
        self.dma = dma
        self.deps = set()
        self.sem = None
        self.val = 0
        self.needed = False


class Sched:
    ENGS = ("pe", "act", "dve", "pool", "sp")
    NDSEM = 12

    def __init__(self, nc, es):
        self.nc = nc
        self.ops = {e: [] for e in self.ENGS}
        self.last_w = {}
        self.readers = {}
        self.esem = {e: es.enter_context(nc.semaphore("sem_" + e)) for e in ("pe", "act", "dve", "pool")}
        self.dsem = {q: [es.enter_context(nc.semaphore("dsem_%s_%d" % (q, i))) for i in range(self.NDSEM)]
                     for q in ("sp", "pool", "act")}
        self.dcount = {q: [0] * self.NDSEM for q in self.dsem}
        self.dlast = {q: [None] * self.NDSEM for q in self.dsem}
        self.dnext = {q: 0 for q in self.dsem}
        self.nops = 0
        self.stores = []

    def op(self, eng, fn, r=(), w=(), dma=False, store=False, cost=0.5, tbl=None):
        pr = [k for k in r if isinstance(k, tuple) and k[0] == "ps"]
        if pr:
            w = list(w) + pr
        o = Op(eng, fn, dma)
        o.cost = cost
        o.tbl = tbl
        o.idx = self.nops
        self.nops += 1
        deps = o.deps
        for k in r:
            lw = self.last_w.get(k)
            if lw is not None:
                deps.add(lw)
        for k in w:
            lw = self.last_w.get(k)
            if lw is not None:
                deps.add(lw)
            rd = self.readers.get(k)
            if rd:
                deps.update(rd.values())
        for k in r:
            self.readers.setdefault(k, {})[o.idx] = o
        for k in w:
            self.last_w[k] = o
            self.readers[k] = {}
        if dma:
            q = eng
            i = self.dnext[q]
            self.dnext[q] = (i + 1) % self.NDSEM
            prev = self.dlast[q][i]
            if prev is not None:
                deps.add(prev)
            self.dcount[q][i] += 16
            o.sem = self.dsem[q][i]
            o.val = self.dcount[q][i]
            self.dlast[q][i] = o
            o.needed = True
            if store:
                self.stores.append(o)
        deps.discard(o)
        self.ops[eng].append(o)
        return o

    def reorder(self, window=48):
        ENG = self.ENGS
        pend = {e: list(self.ops[e]) for e in ENG}
        head = {e: 0 for e in ENG}
        sched = {e: [False] * len(pend[e]) for e in ENG}
        free = {e: 0.0 for e in ENG}
        neworder = {e: [] for e in ENG}
        cur_tbl = [None]
        dma_free = [0.0]
        for e in ENG:
            for o in pend[e]:
                o.done = None
        remaining = sum(len(v) for v in pend.values())
        LAT = float(os.environ.get("KB_LAT", "0.3"))
        best = {e: None for e in ENG}
        rtc = {}
        USE_RANK = os.environ.get("KB_RANK", "1") == "1"
        allo = sorted((o for v in pend.values() for o in v), key=lambda o: o.idx)
        rank = {}
        if USE_RANK:
            succ_best = {}
            for o in reversed(allo):
                r0 = succ_best.get(o.idx, 0.0) + (o.cost + (2.0 if o.dma else 0.0))
                rank[o.idx] = r0
                for d in o.deps:
                    v = r0 + (LAT if (d.eng != o.eng or d.dma) else 0.0)
                    if v > succ_best.get(d.idx, 0.0):
                        succ_best[d.idx] = v

        def find(e):
            lst, sc = pend[e], sched[e]
            h = head[e]
            n = len(lst)
            while h < n and sc[h]:
                h += 1
            head[e] = h
            bt, bi, brk = None, -1, 0.0
            cnt = 0
            i = h
            f = free[e]
            while i < n and cnt < window:
                if not sc[i]:
                    cnt += 1
                    o = lst[i]
                    rt = rtc.get(o.idx)
                    ok = True
                    if rt is None:
                        rt = 0.0
                        for d in o.deps:
                            dd = d.done
                            if dd is None:
                                ok = False
                                break
                            if d.eng != e or d.dma:
                                dd += LAT
                            elif e != "pe":
                                dd += 0.06
                            if dd > rt:
                                rt = dd
                        if ok:
                            rtc[o.idx] = rt
                    if ok:
                        st = rt if rt > f else f
                        if USE_RANK:
                            rk = rank[o.idx]
                            if bt is None or st < bt - 1e-9 or (st <= bt + 1e-9 and rk > brk):
                                bt, bi, brk = st, i, rk
                        elif bt is None or st < bt - 1e-9:
                            bt, bi = st, i
                            if st <= f + 1e-9:
                                break
                i += 1
            best[e] = (bt, bi) if bt is not None else None

        for e in ENG:
            find(e)
        while remaining:
            be, bt, bi = None, None, -1
            for e in ENG:
                b = best[e]
                if b is not None and (bt is None or b[0] < bt):
                    be, bt, bi = e, b[0], b[1]
            assert be is not None, "scheduler deadlock"
            o = pend[be][bi]
            sched[be][bi] = True
            neworder[be].append(o)
            remaining -= 1
            if o.dma:
                free[be] = bt + 0.06
                s0 = max(bt, dma_free[0])
                dma_free[0] = s0 + o.cost
                o.done = dma_free[0] + 2.0
            else:
                c = o.cost
                if be == "act" and o.tbl is not None and o.tbl != cur_tbl[0]:
                    c += 1.3
                    cur_tbl[0] = o.tbl
                o.done = bt + c
                free[be] = o.done
            for e in ENG:
                find(e)
        self.ops = neworder
        self.sim_time = max(free.values())

    def emit(self):
        nc = self.nc
        if os.environ.get("KB_NOREORDER") != "1":
            self.reorder(window=512)
        for e in self.ENGS:
            for o in self.ops[e]:
                for d in o.deps:
                    if d.dma:
                        continue
                    if d.eng == "pe" and o.eng == "pe" and not o.dma:
                        continue
                    d.needed = True
        for e in ("pe", "act", "dve", "pool"):
            c = 0
            for o in self.ops[e]:
                if o.dma:
                    continue
                if o.needed:
                    c += 1
                    o.sem = self.esem[e]
                    o.val = c
        stores = self.stores

        def run(engname, eng, final=False):
            known = {}
            for o in self.ops[engname]:
                waits = {}
                for d in o.deps:
                    if (not d.dma) and d.eng == "pe" and engname == "pe" and not o.dma:
                        continue
                    s = d.sem
                    if waits.get(s, (None, 0))[1] < d.val:
                        waits[s] = (s, d.val)
                for s, v in waits.values():
                    if known.get(s, 0) >= v:
                        continue
                    known[s] = v
                    eng.wait_ge(s, v)
                ins = o.fn(eng)
                if o.needed:
                    ins.then_inc(o.sem, 16 if o.dma else 1)
            if final:
                waits = {}
                for d in stores:
                    if waits.get(d.sem, (None, 0))[1] < d.val:
                        waits[d.sem] = (d.sem, d.val)
                for s, v in waits.values():
                    eng.wait_ge(s, v)

        with nc.Block() as block:
            @block.sync
            def _(e):
                run("sp", e, final=True)

            @block.gpsimd
            def _(e):
                run("pool", e)

            @block.scalar
            def _(e):
                run("act", e)

            @block.vector
            def _(e):
                run("dve", e)

            @block.tensor
            def _(e):
                run("pe", e)


def _chunked(a):
    L = a.shape[0]
    return np.ascontiguousarray(a.reshape(L // 128, 128, -1).transpose(1, 0, 2))


def _bf(a):
    return np.ascontiguousarray(a.astype(np.float32)).astype(ml_dtypes.bfloat16)


CF = {}
_cf_off = 0
for _name, _w in (("ident", 128), ("ones", 128), ("bdc", 128), ("bds", 128),
                  ("cos", 1024), ("sin", 1024),
                  ("feats1024", 1024), ("feats256", 256),
                  ("ndist1024", 8), ("ndist256", 2),
                  ("cab1024", 3), ("cab256", 3)):
    CF[_name] = (_cf_off, _w)
    _cf_off += _w
CF_W = _cf_off

CB = {}
_cb_off = 0
for _name, _w in (("ident", 128), ("prot", 128), ("triu", 128), ("tril", 128), ("onesblk", 128),
                  ("smt0_1024", 8 * 128), ("sm0_1024", 1024), ("smt0_256", 2 * 128), ("sm0_256", 256),
                  ("t256", 4 * 2 * 256)):
    CB[_name] = (_cb_off, _w)
    _cb_off += _w
CB_W = _cb_off


def make_tables():
    cf = np.zeros((128, CF_W), np.float64)
    cb = np.zeros((128, CB_W), np.float64)

    def putf(name, arr):
        o, w = CF[name]
        arr = np.asarray(arr, np.float64)
        cf[: arr.shape[0], o:o + w] = arr.reshape(arr.shape[0], w)

    def putb(name, arr):
        o, w = CB[name]
        arr = np.asarray(arr, np.float64)
        cb[: arr.shape[0], o:o + w] = arr.reshape(arr.shape[0], w)

    putf("ident", np.eye(128))
    putf("ones", np.ones((128, 128)))
    w64 = np.arange(64)
    c64 = np.cos(2 * np.pi * np.outer(w64, w64) / 64) / 8.0
    s64 = np.sin(2 * np.pi * np.outer(w64, w64) / 64) / 8.0
    z = np.zeros((64, 64))
    putf("bdc", np.block([[c64, z], [z, c64]]))
    putf("bds", -np.block([[s64, z], [z, s64]]))
    t = np.arange(1024)
    inv = 10000.0 ** (-np.arange(0, 32, 2) / 32.0)
    cosT = np.zeros((128, 1024))
    sinT = np.zeros((128, 1024))
    for p in range(128):
        d = p % 64
        j = d % 16
        pos = (t // 64) if d < 32 else (t % 64)
        ang = (pos.astype(np.float32) * inv[j].astype(np.float32)).astype(np.float32)
        cosT[p] = np.cos(ang)
        sinT[p] = np.sin(ang)
    putf("cos", cosT)
    putf("sin", sinT)
    prot = np.zeros((128, 128))
    for m in range(128):
        if m % 32 < 16:
            prot[m + 16, m] = -1.0
        else:
            prot[m - 16, m] = 1.0
    putb("prot", prot)
    putb("ident", np.eye(128))
    ii = np.arange(128)[:, None]
    jj = np.arange(128)[None, :]
    putb("triu", (ii <= jj).astype(np.float64))
    putb("tril", (jj <= ii).astype(np.float64))
    putb("onesblk", (ii // 64 == jj // 64).astype(np.float64))
    big = {}
    for L in (1024, 256):
        tt = np.arange(L, dtype=np.float64)
        tn = (tt.astype(np.float32) / np.float32(L)).astype(np.float64)
        fr = np.arange(1, 17, dtype=np.float64)
        ang = 2.0 * np.pi * tn[:, None] * fr[None, :]
        feats = np.concatenate([tn[:, None], np.cos(ang), np.sin(ang)], -1)
        putf("feats%d" % L, feats.T)
        dist = np.abs(tt - L // 2) / (L / 2)
        putf("ndist%d" % L, (-dist).reshape(L // 128, 128).T)
        pm = np.arange(128) % 4
        ca = np.array([1.0, 0.0, -1.0, 0.0])[pm] / L
        cbv = np.array([0.0, 1.0, 0.0, -1.0])[pm] / L
        putf("cab%d" % L, np.stack([ca, cbv, -ca], 1))
        f = np.arange(L)[:, None]
        n = np.arange(L)[None, :]
        Cm = np.cos(np.pi * f * n / L)
        Sm = np.sin(np.pi * f * n / L)
        CL = np.cos(2 * np.pi * f * n / L) / np.sqrt(L)
        SL = np.sin(2 * np.pi * f * n / L) / np.sqrt(L)
        smt0 = Sm[:, 0:128].copy()
        smt0[:, 0] = (-1.0) ** np.arange(L)
        sm0 = Sm[0:128, :].copy()
        sm0[0, :] = (-1.0) ** np.arange(L)
        putb("smt0_%d" % L, _chunked(smt0).reshape(128, -1))
        putb("sm0_%d" % L, sm0)
        big[L] = np.stack([_chunked(CL), _chunked(SL), _chunked(Cm), _chunked(Sm)], 1)
    putb("t256", big[256].reshape(128, -1))
    return dict(cf32=np.ascontiguousarray(cf.astype(np.float32)),
                cbf=_bf(cb),
                t1024=_bf(big[1024]))


def _ap(t, off_extra, dims):
    return bass.AP(t.tensor, t.offset + off_extra, [list(t.ap[0])] + [list(d) for d in dims])


class KB:
    def __init__(self, debug=None, stop_after=None):
        self.debug = debug or {}
        self.stop_after = stop_after
        self.es = ExitStack()
        nc = self.nc = bass.Bass("TRN2", target_bir_lowering=False)
        self.s = Sched(nc, self.es)
        self.rot_i = 0
        self.dbg_outs = {}
        self._decl()

    def dram(self, name, shape, dt=F32, kind="ExternalInput"):
        return self.nc.dram_tensor(name, list(shape), dt, kind=kind).ap()

    def sb(self, name, shape, dt=F32):
        return self.es.enter_context(self.nc.sbuf_tensor(name, list(shape), dt))

    def _decl(self):
        nc = self.nc
        d = self.dram
        self.x_in = {"s": d("x_s", [1024, D]), "p": d("x_p", [512, D])}
        self.cache = {k: d(k, [2, 256, 128]) for k in ("ck_c", "cv_c", "ck_d", "cv_d")}
        self.cond = d("cond", [32, 128])
        self.w_ada = d("w_ada", [2, D, 3 * D])
        self.b_ada = d("b_ada", [2, 48, 128])
        self.w_in = d("w_in", [2, D, INW])
        self.w_f = d("w_f", [2, 512, 512])
        self.conv_w = d("conv_w", [2, 36, 128])
        self.conv_b = d("conv_b", [2, 12, 128])
        self.fw1 = d("fw1", [2, 33, 64])
        self.fb1 = d("fb1", [2, 64, 1])
        self.fw2 = d("fw2", [2, 64, 64])
        self.fb2 = d("fb2", [2, 64, 1])
        self.fw3 = d("fw3", [2, 64, 1024])
        self.ldec = d("ldec", [2, 1, 1024])
        self.skip = d("skip", [2, 1, 1024])
        self.sink = d("sink", [2, 1, 8])
        self.qn = d("qn", [2, 64, 1])
        self.kn = d("kn", [2, 64, 1])
        self.knr = d("knr", [2, 1, 64])
        self.w_out = d("w_out", [2, D, D])
        self.ln_g = d("ln_g", [2, 1, D])
        self.ln_b = d("ln_b", [2, 1, D])
        self.cf32_d = d("cf32", [128, CF_W])
        self.cbf_d = d("cbf", [128, CB_W], BF16)
        self.t1024_d = d("t1024", [128, 4, 8, 1024], BF16)
        o = lambda n, s: self.dram(n, s, F32, "ExternalOutput")
        self.y_out = {"s": o("y_s", [1024, D]), "p": o("y_p", [512, D])}
        self.kv_out = {k: o(k, [2, 2, 256, 128]) for k in ("nck", "ncv", "ndk", "ndv")}
        self.wcache = [self.dram("wcache%d" % l, [44, 128, 2048], BF16, "Internal") for l in range(DEPTH)]
        self.xm = {"s": self.dram("xm_s", [1024, D], F32, "Internal"),
                   "p": self.dram("xm_p", [512, D], F32, "Internal")}
        self.cf = self.sb("cf", [128, CF_W])
        self.cb = self.sb("cb", [128, CB_W], BF16)
        self.big1 = self.sb("big1", [128, 16, 1024], BF16)
        self.big2 = self.sb("big2", [128, 16, 1024], BF16)
        self.mix = self.sb("mix", [128, 16, 1024], BF16)
        self.NSLAB = 5
        self.slab = [self.sb("slab%d" % i, [128, 16, 128], BF16) for i in range(self.NSLAB)]
        self.slab_i = 0
        self.bg_enable = False
        self.bg_jobs = []
        self.modfm = self.sb("modfm", [128, 2, 48, 2])
        self.sc1 = self.sb("sc1", [128, 2, 16, 2])
        self.scT = self.sb("scT", [128, 32], BF16)
        self.small = self.sb("small", [128, 256])
        self.SCRW = 14848
        self.scr = self.sb("scr", [128, self.SCRW])
        self.ps = [self.es.enter_context(nc.psum_tensor("ps%d" % i, [128, 512], F32)) for i in range(8)]
        self.psb = [p.bitcast(BF16) for p in self.ps]

    def cfv(self, name, rows=128):
        o, w = CF[name]
        return self.cf[0:rows, o:o + w]

    def cbv(self, name):
        o, w = CB[name]
        return self.cb[:, o:o + w]

    def arena(self, off_words, nwords, dt=F32, shape=None):
        assert off_words + nwords <= self.SCRW, (off_words, nwords)
        a = self.scr[:, off_words:off_words + nwords]
        if dt == BF16:
            a = a.bitcast(BF16)
        if shape is not None:
            names = " ".join("d%d" % i for i in range(len(shape)))
            a = a.rearrange("p (%s) -> p %s" % (names, names), **{"d%d" % i: shape[i] for i in range(len(shape))})
        keys = [("S", u) for u in range(off_words // 128, (off_words + nwords + 127) // 128)]
        return a, keys

    nrot = 8

    def rot(self):
        i = self.rot_i % self.nrot
        self.rot_i = (i + 1) % self.nrot
        return i

    def op(self, *a, **k):
        return self.s.op(*a, **k)

    @staticmethod
    def _nfree(ap):
        n = 1
        for d in ap.shape[1:]:
            n *= d
        return n

    def dma(self, q, out, in_, r=(), w=(), store=False):
        nb = 1
        for d in in_.shape:
            nb *= d
        nb *= 4 if in_.dtype == F32 else 2
        return self.s.op(q, lambda e: e.dma_start(out=out, in_=in_), r=r, w=w, dma=True, store=store,
                         cost=nb / 230e3)

    def mm(self, out, lhsT, rhs, start, stop, r, w):
        n = self._nfree(out)
        c = max(64, n) / 1950.0 + 0.01
        if lhsT.dtype == F32:
            c *= 4
        return self.s.op("pe", lambda e: e.matmul(out, lhsT=lhsT, rhs=rhs, start=start, stop=stop,
                                                  skip_group_check=True), r=r, w=w, cost=c)

    def tr(self, out, in_, ident, r, w):
        c = 0.07 * (4 if in_.dtype == F32 else 1)
        return self.s.op("pe", lambda e: e.transpose(out, in_, ident), r=r, w=w, cost=c)

    def act(self, out, in_, func, r, w, bias=None, scale=None, eng="act"):
        kw = {}
        if bias is not None:
            kw["bias"] = bias
        if scale is not None:
            kw["scale"] = scale
        tbl = {AF.Exp: "exp", AF.Silu: "silu", AF.Sin: "silu", AF.Sqrt: "sqrt"}.get(func)
        return self.s.op(eng, lambda e: e.activation(out, in_, func, **kw), r=r, w=w,
                         cost=0.2 + 0.00075 * self._nfree(out), tbl=tbl)

    def tt(self, eng, out, in0, in1, op_, r, w):
        return self.s.op(eng, lambda e: e.tensor_tensor(out, in0, in1, op_), r=r, w=w,
                         cost=0.08 + 0.0011 * self._nfree(out))

    def ts(self, eng, out, in0, s1, s2, op0, op1, r, w):
        c = 0.08 + 0.0011 * self._nfree(out)
        if op1 is None:
            return self.s.op(eng, lambda e: e.tensor_scalar(out, in0, s1, None, op0), r=r, w=w, cost=c)
        return self.s.op(eng, lambda e: e.tensor_scalar(out, in0, s1, s2, op0, op1), r=r, w=w, cost=c)

    def stt(self, eng, out, in0, sc, in1, op0, op1, r, w):
        return self.s.op(eng, lambda e: e.scalar_tensor_tensor(out, in0, sc, in1, op0, op1), r=r, w=w,
                         cost=0.08 + 0.0011 * self._nfree(out))

    def cp(self, eng, out, in_, r, w):
        if eng == "act":
            return self.s.op(eng, lambda e: e.copy(out, in_), r=r, w=w, cost=0.2 + 0.00075 * self._nfree(out))
        return self.s.op(eng, lambda e: e.tensor_copy(out, in_), r=r, w=w, cost=0.08 + 0.0011 * self._nfree(out))

    def dump(self, name, src_ap, shape, keys, dt=F32):
        if name not in self.debug:
            return
        o = self.dram("dbg_" + name, shape, dt, "ExternalOutput")
        self.dbg_outs["dbg_" + name] = shape
        self.dma("sp", o, src_ap, r=keys, store=True)

    def load_fm(self, dst, src, n, dkeys, tmp_off):
        rows, rk = self.arena(tmp_off, 128)
        b = self.rot()
        self.dma("sp", rows[0:n, :], src, w=rk)
        self.tr(self.ps[b][:, 0:n], rows[0:n, :], self.cfv("ident")[0:n, 0:n], r=rk + ["cf"], w=[("ps", b)])
        self.cp("dve", dst, self.ps[b][:, 0:n], r=[("ps", b)], w=dkeys)

    def load_slab(self, wsrc, col0, ncols=128, bg=True):
        if bg and self.bg_enable:
            self.bg_tick()
        i = self.slab_i
        self.slab_i = (i + 1) % self.NSLAB
        sl = self.slab[i]
        src = wsrc[:, col0:col0 + ncols].rearrange("(kc p) c -> p kc c", p=128)
        self.dma("pool", sl[:, :, 0:ncols], src, w=[("slab", i)])
        return sl, [("slab", i)]

    def load_win(self, l, col0):
        idx = col0 // 128
        if self.cur_g == "s":
            sl, sk = self.load_slab(self.w_in[l], col0)
            self.dma("sp", self.wcache[l][idx], sl[:].rearrange("p a b -> p (a b)"), r=sk, w=[("wc", l, idx)])
            return sl, sk
        if self.bg_enable:
            self.bg_tick()
        i = self.slab_i
        self.slab_i = (i + 1) % self.NSLAB
        sl = self.slab[i]
        self.dma("sp", sl[:].rearrange("p a b -> p (a b)"), self.wcache[l][idx], r=[("wc", l, idx)],
                 w=[("slab", i)])
        return sl, [("slab", i)]

    def mod_slab(self, l, s):
        sl, sk = self.load_slab(self.w_ada[l], s * 128, bg=False)
        b = self.rot()
        for kc in range(KC):
            self.mm(self.ps[b][:, 0:2], sl[:, kc, :], self.scT[:, kc:32:16], kc == 0, kc == KC - 1,
                    r=sk + ["scT"], w=[("ps", b)])
        bcol = self.bfm[:, l, s:s + 1]
        key = ("modfm", l, s // 16)
        self.ts("dve", self.modfm[:, l, s, :], self.ps[b][:, 0:2], bcol, None, ALU.add, None,
                r=[("ps", b), ("bfm", l)], w=[key])
        if s // 16 == 1:
            self.ts("dve", self.sc1[:, l, s - 16, :], self.ps[b][:, 0:2], bcol, 1.0, ALU.add, ALU.add,
                    r=[("ps", b), ("bfm", l)], w=[("sc1", l)])

    def bg_tick(self):
        if self.bg_jobs:
            l, s = self.bg_jobs.pop(0)
            self.mod_slab(l, s)

    def bg_flush(self, l):
        while self.bg_jobs and self.bg_jobs[0][0] <= l:
            self.bg_tick()

    def setup(self):
        self.dma("sp", self.cf[:], self.cf32_d, w=["cf"])
        self.dma("sp", self.cb[:], self.cbf_d, w=["cb"])
        sm = self.small
        self.c_lneps = sm[:, 200:201]
        self.c_rmseps = sm[:, 201:202]
        self.op("dve", lambda e: e.memset(sm[:, 200:201], LN_EPS), w=["consts"])
        self.op("dve", lambda e: e.memset(sm[:, 201:202], RMS_EPS), w=["consts"])
        self.c_lneps_c = sm[:, 202:203]
        self.op("dve", lambda e: e.memset(sm[:, 202:203], LN_EPS / (ALPHA * ALPHA)), w=["consts"])
        cfm, ck = self.arena(256, 32)
        self.load_fm(cfm, self.cond, 32, ck, 0)
        self.act(self.scT[:], cfm, AF.Silu, r=ck, w=["scT"])
        self.bfm = self.sb("bfm", [128, 2, 48])
        for l in range(DEPTH):
            self.load_fm(self.bfm[:, l, :], self.b_ada[l], 48, [("bfm", l)], 1024 + 128 * l)
        self.bg_jobs = []
        for s in range(32):
            self.mod_slab(0, s)
        self.bg_jobs = [(0, s) for s in range(32, 48)] + [(1, s) for s in range(48)]

    @staticmethod
    def group(g):
        if g == "s":
            return dict(T=1024, cond=0, seqs=[(0, 1024)], L=1024)
        return dict(T=512, cond=1, seqs=[(0, 256), (256, 256)], L=256)

    cur_g = "s"

    def hm(self, kc, t0, t1):
        if self.cur_g == "s":
            return self.big1[:, kc, t0:t1]
        return self.mix[:, kc, 512 + t0:512 + t1]

    def hm_keys(self, kc, t0, t1):
        if self.cur_g == "s":
            return [("hm", kc, t) for t in range(t0 // 128, (t1 + 127) // 128)]
        return [("mx", kc, 4 + t) for t in range(t0 // 128, (t1 + 127) // 128)]

    def mx_keys(self, kc, t0, t1):
        return [("mx", kc, t) for t in range(t0 // 128, (t1 + 127) // 128)]

    def phase_a(self, l, g):
        G = self.group(g)
        self.cur_g = g
        while self.bg_jobs and (self.bg_jobs[0][0] < l or (self.bg_jobs[0][0] == l and self.bg_jobs[0][1] < 32)):
            self.bg_tick()
        src = self.x_in[g] if l == 0 else self.xm[g]
        cj = G["cond"]
        NB_A = 4
        xts = [self.arena(2048 * i, 2048) for i in range(NB_A)]
        xns = [self.arena(2048 * NB_A + 1024 * i, 1024, BF16) for i in range(NB_A)]
        sm = self.small
        SB = (0, 32, 96, 128)
        for tt in range(G["T"] // 128):
            bi = tt % NB_A
            xt, xk = xts[bi]
            xn, nk = xns[bi]
            sb0 = SB[bi]
            st = sm[:, sb0:sb0 + 24].rearrange("p (a b) -> p a b", b=6)
            mv = sm[:, sb0 + 24:sb0 + 26]
            rstd = sm[:, sb0 + 26:sb0 + 27]
            nmr = sm[:, sb0 + 27:sb0 + 28]
            sk = [("smA", bi)]
            self.dma("sp", xt, src[tt * 128:(tt + 1) * 128, :], w=xk)
            for j in range(4):
                self.op("dve", lambda e, j=j, st=st, xt=xt: e.bn_stats(st[:, j, :], xt[:, j * 512:(j + 1) * 512]),
                        r=xk, w=sk, cost=0.65)
            self.op("dve", lambda e, st=st, mv=mv: e.bn_aggr(mv, st), r=sk, w=sk)
            self.act(rstd, mv[:, 1:2], AF.Sqrt, r=sk + ["consts"], w=sk, bias=self.c_lneps)
            self.op("dve", lambda e, rstd=rstd: e.reciprocal(rstd, rstd), r=sk, w=sk)
            self.stt("dve", nmr, mv[:, 0:1], -1.0, rstd, ALU.mult, ALU.mult, r=sk, w=sk)
            self.act(xn, xt, AF.Identity, r=xk + sk, w=nk, bias=nmr, scale=rstd)
            for half in range(2):
                b = self.rot()
                for j in range(8):
                    kc = half * 8 + j
                    self.tr(self.psb[b][:, j * 128:(j + 1) * 128], xn[:, kc * 128:(kc + 1) * 128],
                            self.cbv("ident"), r=nk + ["cb"], w=[("ps", b)])
                for j in range(8):
                    kc = half * 8 + j
                    dst = self.hm(kc, tt * 128, (tt + 1) * 128)
                    srcp = self.psb[b][:, j * 128:(j + 1) * 128]
                    s1 = self.sc1[:, l, kc, cj:cj + 1]
                    sh = self.modfm[:, l, kc, cj:cj + 1]
                    wk = self.hm_keys(kc, tt * 128, tt * 128 + 128)
                    mk = [("ps", b), ("sc1", l), ("modfm", l, 0)]
                    if j % 2 == 0:
                        self.ts("dve", dst, srcp, s1, sh, ALU.mult, ALU.add, r=mk, w=wk)
                    else:
                        self.act(dst, srcp, AF.Identity, r=mk, w=wk, bias=sh, scale=s1)
        if l == 0:
            self.dump("hmodT_" + g, self.big1[:, :, 0:G["T"]] if g == "s" else self.mix[:, :, 512:1024], [128, 16, G["T"]],
                      [k for kc in range(16) for k in self.hm_keys(kc, 0, G["T"])], BF16)


_TABLES = None


def shared_inputs(inp):
    global _TABLES
    if _TABLES is None:
        _TABLES = make_tables()
    f = lambda a: np.ascontiguousarray(np.asarray(a, dtype=np.float32))
    sh = dict(
        w_ada=f(inp["w_ada"]),
        b_ada=f(inp["b_ada"]).reshape(2, 48, 128),
        w_in=f(inp["w_in"]),
        w_f=f(inp["w_fourier"]),
        conv_w=f(inp["conv_w"]).reshape(2, 36, 128),
        conv_b=f(inp["conv_b"]).reshape(2, 12, 128),
        fw1=f(inp["filt_w1"]),
        fb1=f(inp["filt_b1"]).reshape(2, 64, 1),
        fw2=f(inp["filt_w2"]),
        fb2=f(inp["filt_b2"]).reshape(2, 64, 1),
        fw3=f(inp["filt_w3"]),
        ldec=f(inp["filt_log_decay"]).reshape(2, 1, 1024),
        skip=f(inp["hyena_skip"]).reshape(2, 1, 1024),
        sink=f(inp["sink_logit"]).reshape(2, 1, 8),
        qn=f(inp["q_norm"]).reshape(2, 64, 1),
        kn=f(inp["k_norm"]).reshape(2, 64, 1),
        knr=f(inp["k_norm"]).reshape(2, 1, 64),
        w_out=f(inp["w_out"]),
        ln_g=f(inp["ln_g"]).reshape(2, 1, D),
        ln_b=f(inp["ln_b"]).reshape(2, 1, D),
    )
    sh.update(_TABLES)
    return sh


def core_inputs(inp, i, sh):
    f = lambda a: np.ascontiguousarray(np.asarray(a, dtype=np.float32))
    m = dict(sh)
    m["x_s"] = f(inp["x_sample"][i])
    m["x_p"] = f(inp["x_prompt"][2 * i:2 * i + 2]).reshape(512, D)
    m["ck_c"] = f(inp["cache_attn_c_k"][i]).reshape(2, 256, 128)
    m["cv_c"] = f(inp["cache_attn_c_v"][i]).reshape(2, 256, 128)
    m["ck_d"] = f(inp["cache_attn_d_k"][i]).reshape(2, 256, 128)
    m["cv_d"] = f(inp["cache_attn_d_v"][i]).reshape(2, 256, 128)
    m["cond"] = np.ascontiguousarray(
        np.concatenate([f(inp["c"][i]).reshape(16, 128), f(inp["c_ctx"]).reshape(16, 128)], 0))
    return m


def _proj_fm(self, l, col0, T, handler, lhs_of=None):
    sl, sk = self.load_win(l, col0)
    for tc in range(T // 512):
        b = self.rot()
        for kc in range(KC):
            lhsT = sl[:, kc, :] if lhs_of is None else lhs_of(sl, kc)
            self.mm(self.ps[b][:, :], lhsT, self.hm(kc, tc * 512, (tc + 1) * 512), kc == 0, kc == KC - 1,
                    r=sk + self.hm_keys(kc, tc * 512, tc * 512 + 512), w=[("ps", b)])
        handler(tc, b)
    return sl, sk


def _proj_tm(self, l, col0, T, handler):
    sl, sk = self.load_win(l, col0)
    for tt in range(T // 128):
        b = self.rot()
        for kc in range(KC):
            self.mm(self.ps[b][:, 0:128], self.hm(kc, tt * 128, (tt + 1) * 128), sl[:, kc, :],
                    kc == 0, kc == KC - 1, r=sk + self.hm_keys(kc, tt * 128, tt * 128 + 128), w=[("ps", b)])
        handler(tt, b)


def _b2_keys(self, tbl, tc0=0, tc1=8):
    return [("b2", tbl * 8 + t) for t in range(tc0, tc1)]


def _load_tables(self, first):
    v = self.big2[:].rearrange("p (a b) n -> p a b n", a=2)
    for j in range(2):
        self.dma("sp", v[:, j], self.t1024_d[:, first + j], w=self.b2_keys(j))


def _tbl(self, L, which):
    if L == 1024:
        v = self.big2[:].rearrange("p (a b) n -> p a b n", a=2)
        j = which % 2
        return (lambda tc, n0, n1: v[:, j, tc, n0:n1]), (lambda tc: [("b2", j * 8 + tc)])
    o, _ = CB["t256"]
    v = self.cb[:, o:o + 2048].rearrange("p (a b n) -> p a b n", a=4, b=2)
    return (lambda tc, n0, n1: v[:, which, tc, n0:n1]), (lambda tc: ["cb"])


def _branch_a(self, l, g):
    G = self.group(g)
    T, NT = G["T"], G["T"] // 128
    aT, aTk = self.arena(0, 2048, BF16, [4, 1024])
    gA, gAk = self.arena(2048, 2048, BF16, [4, 1024])
    wf, wfk = self.arena(4096, 2048, F32, [4, 512])
    Wc, Wck = self.arena(6144, 1024, BF16, [4, 512])
    Ws, Wsk = self.arena(7168, 1024, BF16, [4, 512])
    ac, ack = self.arena(8192, 2048, BF16, [8, 512])
    as_, ask = self.arena(10240, 2048, BF16, [8, 512])
    if g == "s":
        self.load_tables(0)
    self.dma("sp", wf, self.w_f[l].rearrange("(cc p) n -> p cc n", p=128), w=wfk)
    for (W, Wk, tbl) in ((Wc, Wck, "bdc"), (Ws, Wsk, "bds")):
        for cc in range(4):
            b = self.rot()
            self.mm(self.ps[b][:, :], self.cfv(tbl), wf[:, cc, :], True, True, r=wfk + ["cf"], w=[("ps", b)])
            self.cp("act", W[:, cc, :], self.ps[b][:, :], r=[("ps", b)], w=Wk)
    for s in range(4):
        def h_a(tc, b, s=s):
            self.cp("act", aT[:, s, tc * 512:(tc + 1) * 512], self.ps[b][:, :], r=[("ps", b)], w=aTk)
        self.proj_fm(l, O_A + s * 128, T, h_a)
    for s in range(4):
        def h_g(tc, b, s=s):
            self.act(gA[:, s, tc * 512:(tc + 1) * 512], self.ps[b][:, :], AF.Silu, r=[("ps", b)], w=gAk)
        self.proj_fm(l, O_AG + s * 128, T, h_g)
    for (dst, dk, W, Wk) in ((ac, ack, Wc, Wck), (as_, ask, Ws, Wsk)):
        for tt in range(NT):
            b = self.rot()
            for cc in range(4):
                self.mm(self.ps[b][:, :], aT[:, cc, tt * 128:(tt + 1) * 128], W[:, cc, :], cc == 0, cc == 3,
                        r=aTk + Wk, w=[("ps", b)])
            self.cp("act" if tt % 2 else "dve", dst[:, tt, :], self.ps[b][:, :], r=[("ps", b)], w=dk)
    for (o, L) in G["seqs"]:
        cl, clk = self.tbl(L, 0)
        sl_, slk = self.tbl(L, 1)
        N = min(L, 512)
        ntc = L // 128
        for cb4 in range(4):
            for n0 in range(0, L, N):
                b = self.rot()
                i = 0
                for (src, sk2, tab, tabk) in ((ac, ack, cl, clk), (as_, ask, sl_, slk)):
                    for tc in range(ntc):
                        self.mm(self.ps[b][:, 0:N], src[:, o // 128 + tc, cb4 * 128:(cb4 + 1) * 128],
                                tab(tc, n0, n0 + N), i == 0, i == 2 * ntc - 1,
                                r=sk2 + tabk(tc), w=[("ps", b)])
                        i += 1
                t0 = o + n0
                self.tt("dve", self.mix[:, cb4, t0:t0 + N], self.ps[b][:, 0:N], gA[:, cb4, t0:t0 + N], ALU.mult,
                        r=[("ps", b)] + gAk, w=self.mx_keys(cb4, t0, t0 + N))


for _n, _f in (("proj_fm", _proj_fm), ("proj_tm", _proj_tm), ("b2_keys", _b2_keys), ("load_tables", _load_tables),
               ("tbl", _tbl), ("branch_a", _branch_a)):
    setattr(KB, _n, _f)


def _dup64(ap2d, c0):
    return bass.AP(ap2d.tensor, ap2d.offset + c0, [list(ap2d.ap[0]), [0, 2], [1, 64]])


def _branch_attn(self, l, g, which):
    self.nrot = 6
    try:
        _branch_attn_body(self, l, g, which)
    finally:
        self.nrot = 8


def _branch_attn_body(self, l, g, which):
    G = self.group(g)
    T, NT = G["T"], G["T"] // 128
    sample = (g == "s")
    is_d = (which == "d")
    base = O_D if is_d else O_C
    blk0 = 12 if is_d else 8
    Tk = T + (256 if sample else 0)
    NKT = Tk // 128
    sm = self.small
    off = [0]

    def take(nwords, dt=F32, shape=None):
        a = self.arena(off[0], nwords, dt, shape)
        off[0] += nwords
        return a
    qT, qTk = take(2 * T, BF16, [4, T])
    gT, gTk = take(2 * T, BF16, [4, T])
    kT2, kT2k = take(2 * Tk, BF16, [2, 2, Tk])
    vaug, vaugk = take(NKT * 256, BF16, [NKT, 2, 2, 128])
    NE = 5
    Es = [take(256, BF16) for _ in range(NE)]
    raws = [take(512) for _ in range(2)]
    tmps = [take(512) for _ in range(2)]
    xbs = [take(256, BF16) for _ in range(2)]
    rec, reck = take(T)
    stage, stagek = take(256, F32, [2, 128])
    kb16, kb16k = take(128, BF16, [2, 128])
    kn_, knk = take(256, BF16)
    if not sample:
        kout, koutk = take(NT * 128, F32, [NT, 128])
        vout, voutk = take(NT * 128, F32, [NT, 128])
        knbc, knbck = take(64)
    cnt = {"raw": 0, "E": 0}
    gq, gk, esink = sm[:, 64:65], sm[:, 65:66], sm[:, 72:80]
    if is_d:
        for half in range(2):
            self.dma("sp", sm[half * 64:half * 64 + 64, 64:65], self.qn[l], w=["attnp"])
            self.dma("sp", sm[half * 64:half * 64 + 64, 65:66], self.kn[l], w=["attnp"])
        if not sample:
            self.dma("sp", knbc, self.knr[l].broadcast_to([128, 64]), w=knbck)
    else:
        self.dma("sp", esink, self.sink[l].broadcast_to([128, 8]), w=["attnp"])
        self.act(esink, esink, AF.Exp, r=["attnp"], w=["attnp"])
    self.op("dve", lambda e: e.memset(vaug[:, :, :, 0, 64:128], 1.0), w=vaugk)
    self.op("dve", lambda e: e.memset(vaug[:, :, :, 1, 0:64], 1.0), w=vaugk)

    def finish_qk(b, dst, dstk, tok0, gcol):
        i = cnt["raw"] % 2
        cnt["raw"] += 1
        raw, rawk = raws[i]
        tmp, tmpk = tmps[i]
        xb, xbk = xbs[i]
        if not is_d and not sample:
            self.cp("act", dst, self.ps[b][:, :], r=[("ps", b)], w=dstk)
            return
        self.cp("act", raw, self.ps[b][:, :], r=[("ps", b)], w=rawk)
        cur, curk = raw, rawk
        if is_d:
            self.act(xb, raw, AF.Square, r=rawk, w=xbk)
            b2 = self.rot()
            self.mm(self.ps[b2][:, :], self.cbv("onesblk"), xb, True, True, r=xbk + ["cb"], w=[("ps", b2)])
            self.act(tmp, self.ps[b2][:, :], AF.Sqrt, r=[("ps", b2), "consts"], w=tmpk,
                     bias=self.c_rmseps, scale=1.0 / 64.0)
            self.op("dve", lambda e, tmp=tmp: e.reciprocal(tmp, tmp), r=tmpk, w=tmpk, cost=2.2)
            if sample:
                self.stt("dve", raw, raw, gcol, tmp, ALU.mult, ALU.mult, r=rawk + tmpk + ["attnp"], w=rawk)
            else:
                self.stt("dve", dst, raw, gcol, tmp, ALU.mult, ALU.mult, r=rawk + tmpk + ["attnp"], w=dstk)
                return
        self.cp("act", xb, raw, r=rawk, w=xbk)
        b3 = self.rot()
        self.mm(self.ps[b3][:, :], self.cbv("prot"), xb, True, True, r=xbk + ["cb"], w=[("ps", b3)])
        co, _ = CF["cos"]
        so, _ = CF["sin"]
        self.tt("dve", tmp, raw, self.cf[:, co + tok0:co + tok0 + 512], ALU.mult, r=rawk + ["cf"], w=tmpk)
        self.tt("dve", raw, self.ps[b3][:, :], self.cf[:, so + tok0:so + tok0 + 512], ALU.mult,
                r=[("ps", b3), "cf"], w=rawk)
        self.tt("dve", dst, raw, tmp, ALU.add, r=rawk + tmpk, w=dstk)

    self.op("dve", lambda e: e.memset(kT2[64:128, :, 0, :], 0.0), w=kT2k, cost=1.5)
    self.op("dve", lambda e: e.memset(kT2[0:64, :, 1, :], 0.0), w=kT2k, cost=1.5)

    def spread_k(src, srck, c0, n):
        self.cp("act", kT2[0:64, 0, 0, c0:c0 + n], src[0:64, 0:n], r=srck, w=kT2k)
        self.cp("dve", kT2[64:128, 0, 1, c0:c0 + n], src[0:64, 0:n], r=srck, w=kT2k)
        self.cp("dve", kT2[0:64, 1, 0, c0:c0 + n], src[64:128, 0:n], r=srck, w=kT2k)
        self.cp("act", kT2[64:128, 1, 1, c0:c0 + n], src[64:128, 0:n], r=srck, w=kT2k)

    def h_k(tc, b):
        finish_qk(b, kn_, knk, tc * 512, gk)
        spread_k(kn_, knk, tc * 512, 512)
    self.proj_fm(l, base + 512, T, h_k)
    for s in range(4):
        def h_q(tc, b, s=s):
            finish_qk(b, qT[:, s, tc * 512:(tc + 1) * 512], qTk, tc * 512, gq)
        self.proj_fm(l, base + s * 128, T, h_q)
    def put_v(src, srck, tt):
        sv = src.rearrange("p (a b) -> p a b", b=64)
        self.cp("act", vaug[:, tt, :, 0, 0:64], sv, r=srck, w=vaugk)
        self.cp("act", vaug[:, tt, :, 1, 64:128], sv, r=srck, w=vaugk)

    def h_v(tt, b):
        put_v(self.ps[b][:, 0:128], [("ps", b)], tt)
        if not sample:
            self.cp("act", vout[:, tt, :], self.ps[b][:, 0:128], r=[("ps", b)], w=voutk)
    self.proj_tm(l, base + 640, T, h_v)
    if not sample:
        def h_ko(tt, b):
            if not is_d:
                self.cp("act", kout[:, tt, :], self.ps[b][:, 0:128], r=[("ps", b)], w=koutk)
                return
            raw, rawk = raws[tt % 2]
            tmp, tmpk = tmps[tt % 2]
            self.cp("act", raw[:, 0:128], self.ps[b][:, 0:128], r=[("ps", b)], w=rawk)
            self.act(tmp[:, 0:128], raw[:, 0:128], AF.Square, r=rawk, w=tmpk)
            self.op("dve", lambda e, tmp=tmp: e.tensor_reduce(
                tmp[:, 128:130], tmp[:, 0:128].rearrange("p (a b) -> p a b", b=64),
                mybir.AxisListType.X, ALU.add), r=tmpk, w=tmpk)
            self.act(tmp[:, 128:130], tmp[:, 128:130], AF.Sqrt, r=tmpk + ["consts"], w=tmpk,
                     bias=self.c_rmseps, scale=1.0 / 64.0)
            self.op("dve", lambda e, tmp=tmp: e.reciprocal(tmp[:, 128:130], tmp[:, 128:130]), r=tmpk, w=tmpk)
            for kvh in range(2):
                self.stt("dve", kout[:, tt, kvh * 64:(kvh + 1) * 64], raw[:, kvh * 64:(kvh + 1) * 64],
                         tmp[:, 128 + kvh:129 + kvh], knbc, ALU.mult, ALU.mult, r=rawk + tmpk + knbck, w=koutk)
        self.proj_tm(l, base + 512, T, h_ko)
        kname, vname = ("ndk", "ndv") if is_d else ("nck", "ncv")
        for si in range(2):
            self.dma("sp", self.kv_out[kname][si, l].rearrange("(a p) c -> p a c", p=128),
                     kout[:, 2 * si:2 * si + 2, :], r=koutk, store=True)
            self.dma("sp", self.kv_out[vname][si, l].rearrange("(a p) c -> p a c", p=128),
                     vout[:, 2 * si:2 * si + 2, :], r=voutk, store=True)
    if sample:
        kc_d, vc_d = (self.cache["ck_d"], self.cache["cv_d"]) if is_d else (self.cache["ck_c"], self.cache["cv_c"])
        self.dma("sp", stage, vc_d[l].rearrange("(a p) c -> p a c", p=128), w=stagek)
        for c in range(2):
            put_v(stage[:, c, :], stagek, NT + c)
        self.dma("sp", stage, kc_d[l].rearrange("(a p) c -> p a c", p=128), r=[], w=stagek)
        self.cp("act", kb16, stage, r=stagek, w=kb16k)
        for c in range(2):
            b = self.rot()
            self.tr(self.psb[b][:, 0:128], kb16[:, c, :], self.cbv("ident"), r=kb16k + ["cb"], w=[("ps", b)])
            self.cp("act", kn_[:, 0:128], self.psb[b][:, 0:128], r=[("ps", b)], w=knk)
            spread_k(kn_, knk, T + c * 128, 128)
    for s in range(4):
        def h_g(tc, b, s=s):
            self.act(gT[:, s, tc * 512:(tc + 1) * 512], self.ps[b][:, :], AF.Silu, r=[("ps", b)], w=gTk)
        self.proj_fm(l, base + 768 + s * 128, T, h_g)

    def score_pv(h, kb, q0, q1, acc_b, acc_c0, start, stop, masks=()):
        kvh, s, half = h // 4, h // 2, h % 2
        rows = slice(half * 64, half * 64 + 64)
        n = q1 - q0
        b = self.rot()
        self.mm(self.ps[b][:, 0:n], kT2[:, kvh, half, kb * 128:(kb + 1) * 128], qT[:, s, q0:q1], True, True,
                r=kT2k + qTk, w=[("ps", b)])
        E, Ek = Es[cnt["E"] % NE]
        cnt["E"] += 1
        self.act(E[:, 0:n], self.ps[b][:, 0:n], AF.Exp, r=[("ps", b)], w=Ek, scale=0.125)
        for (c0, mname) in masks:
            self.tt("dve", E[:, c0:c0 + 128], E[:, c0:c0 + 128], self.cbv(mname), ALU.mult, r=Ek + ["cb"], w=Ek)
        self.mm(self.ps[acc_b][:, acc_c0:acc_c0 + n], vaug[:, kb, kvh, half, :], E[:, 0:n], start, stop,
                r=vaugk + Ek, w=[("ps", acc_b)])

    def finalize(h, acc_b, ncols, tok0):
        s, half = h // 2, h % 2
        rows = slice(half * 64, half * 64 + 64)
        drows = slice(64, 128) if half == 0 else slice(0, 64)
        tmp, tmpk = tmps[h % 2]
        if is_d:
            self.op("dve", lambda e: e.reciprocal(rec[drows, tok0:tok0 + ncols], self.ps[acc_b][drows, 0:ncols]),
                    r=[("ps", acc_b)], w=reck, cost=2.2)
        else:
            self.ts("dve", rec[drows, tok0:tok0 + ncols], self.ps[acc_b][drows, 0:ncols], esink[drows, h:h + 1],
                    None, ALU.add, None, r=[("ps", acc_b), "attnp"], w=reck)
            self.op("dve", lambda e: e.reciprocal(rec[drows, tok0:tok0 + ncols], rec[drows, tok0:tok0 + ncols]),
                    r=reck, w=reck, cost=2.2)
        self.tt("dve", tmp[rows, 0:ncols], self.ps[acc_b][rows, 0:ncols], rec[drows, tok0:tok0 + ncols], ALU.mult,
                r=[("ps", acc_b)] + reck, w=tmpk)
        self.tt("dve", self.mix[rows, blk0 + s, tok0:tok0 + ncols], tmp[rows, 0:ncols],
                gT[rows, s, tok0:tok0 + ncols], ALU.mult, r=tmpk + gTk,
                w=self.mx_keys(blk0 + s, tok0, tok0 + ncols))

    accn = [0]

    def next_acc():
        b = 6 + (accn[0] % 2)
        accn[0] += 1
        return b

    for h in range(8):
        if not sample:
            acc = next_acc()
            for si, (o, L) in enumerate(G["seqs"]):
                for kb in range(2):
                    score_pv(h, (o // 128) + kb, o, o + L, acc, si * 256, kb == 0, kb == 1)
            finalize(h, acc, 512, 0)
        elif is_d:
            for qc in range(2):
                acc = next_acc()
                for kb in range(NKT):
                    score_pv(h, kb, qc * 512, qc * 512 + 512, acc, 0, kb == 0, kb == NKT - 1)
                finalize(h, acc, 512, qc * 512)
        else:
            for qc in range(2):
                acc = next_acc()
                for kb in (8, 9):
                    score_pv(h, kb, qc * 512, qc * 512 + 512, acc, 0, kb == 8, False)
                for kb in range(8):
                    q0, q1 = max(0, kb - 1) * 128, min(8, kb + 2) * 128
                    a, bq = max(q0, qc * 512), min(q1, qc * 512 + 512)
                    if a >= bq:
                        continue
                    masks = []
                    if kb >= 1 and a <= (kb - 1) * 128 < bq:
                        masks.append(((kb - 1) * 128 - a, "triu"))
                    if kb <= 6 and a <= (kb + 1) * 128 < bq:
                        masks.append(((kb + 1) * 128 - a, "tril"))
                    score_pv(h, kb, a, bq, acc, a % 512, False, False, masks)
                finalize(h, acc, 512, qc * 512)


def _proj_fm2(self, l, col0, T, handler, lhs_of=None, slab=None):
    if slab is None:
        sl, sk = self.load_win(l, col0)
    else:
        sl, sk = slab
    for tc in range(T // 512):
        b = self.rot()
        for kc in range(KC):
            lhsT = sl[:, kc, :] if lhs_of is None else lhs_of(sl, kc)
            self.mm(self.ps[b][:, :], lhsT, self.hm(kc, tc * 512, (tc + 1) * 512), kc == 0, kc == KC - 1,
                    r=sk + self.hm_keys(kc, tc * 512, tc * 512 + 512), w=[("ps", b)])
        handler(tc, b)
    return sl, sk


KB.proj_fm = _proj_fm2
KB.branch_attn = _branch_attn
KB.branch_c = lambda self, l, g: _branch_attn(self, l, g, "c")
KB.branch_d = lambda self, l, g: _branch_attn(self, l, g, "d")


MAGIC = 12582912.0


def _branch_b(self, l, g):
    G = self.group(g)
    T, L, seqs = G["T"], G["L"], G["seqs"]
    NS = len(seqs)
    n128 = L // 128
    W = NS * 128
    off = [0]

    def take(nwords, dt=F32, shape=None):
        a = self.arena(off[0], nwords, dt, shape)
        off[0] += nwords
        return a
    h2T, h2Tk = take(1024)
    w3p, w3pk = take(128)
    edp, edpk = take(128)
    skp, skpk = take(128)
    cfm, cfmk = take(64)
    fwt, fwtk = take(192)
    rsb, rsbk = take(128)
    g1q0, g1q0k = take(128)
    rawb, rawbk = take(512, BF16)
    cv, cvk = take(1024)
    vb, vbk = take(512, BF16)
    x1b, x1bk = take(512, BF16)
    x2g, x2gk = take(512, BF16)
    gB, gBk = take(512, BF16)
    vtm, vtmk = take(n128 * W // 2, BF16, [n128, W])
    AB, ABk = take(n128 * 2 * W, F32, [n128, 2, W])
    G1, G1k = take(n128 * 128, F32, [n128, 128])
    G2, G2k = take(n128 * 128, F32, [n128, 128])
    Pb, Pbk = take(n128 * W // 2, BF16, [n128, W])
    Qb, Qbk = take(n128 * W // 2, BF16, [n128, W])
    hw, hwk = take(1024)
    htm, htmk = take(n128 * 64, BF16, [n128, 128])
    zb, zbk = take(512, BF16)
    hw3 = hw[:, 0:n128 * 128].rearrange("p (a b) -> p a b", b=128)
    cm, cmk = self.tbl(L, 2)
    smm, smk = self.tbl(L, 3)
    o_, w_ = CB["smt0_%d" % L]
    smt0 = self.cb[:, o_:o_ + w_].rearrange("p (a b) -> p a b", b=128)
    o_, w_ = CB["sm0_%d" % L]
    sm0 = self.cb[:, o_:o_ + w_]
    ndist = self.cfv("ndist%d" % L)
    cab = self.cfv("cab%d" % L)
    feats = self.cfv("feats%d" % L)
    if g == "s":
        self.load_tables(2)
    self.load_fm(cfm[:, 0:36], self.conv_w[l], 36, cfmk, 13312)
    self.load_fm(cfm[:, 36:48], self.conv_b[l], 12, cfmk, 13440)
    self.dma("sp", fwt[0:33, 0:64], self.fw1[l], w=fwtk)
    self.dma("sp", fwt[0:64, 64:128], self.fw2[l], w=fwtk)
    self.dma("sp", fwt[0:64, 128:129], self.fb1[l], w=fwtk)
    self.dma("sp", fwt[0:64, 129:130], self.fb2[l], w=fwtk)
    t0 = hw[:, 0:512]
    t1 = hw[:, 512:1024]
    h1T = cv
    N = min(L, 512)
    for (dst, dstk, wl, kdim, bcol, src, srck) in ((h1T, cvk, fwt[0:33, 0:64], 33, fwt[0:64, 128:129], feats, ["cf"]),
                                                    (h2T, h2Tk, fwt[0:64, 64:128], 64, fwt[0:64, 129:130], h1T, cvk)):
        for n0 in range(0, L, N):
            b = self.rot()
            self.mm(self.ps[b][0:64, 0:N], wl, src[0:kdim, n0:n0 + N], True, True, r=fwtk + srck, w=[("ps", b)])
            self.ts("dve", t0[0:64, 0:N], self.ps[b][0:64, 0:N], bcol, None, ALU.add, None,
                    r=[("ps", b)] + fwtk, w=hwk)
            self.ts("dve", t1[0:64, 0:N], t0[0:64, 0:N], 1.0 / TWO_PI, MAGIC, ALU.mult, ALU.add, r=hwk, w=hwk)
            self.ts("dve", t1[0:64, 0:N], t1[0:64, 0:N], -MAGIC, None, ALU.add, None, r=hwk, w=hwk)
            self.stt("dve", t0[0:64, 0:N], t1[0:64, 0:N], -TWO_PI, t0[0:64, 0:N], ALU.mult, ALU.add, r=hwk, w=hwk)
            self.act(dst[0:64, n0:n0 + N], t0[0:64, 0:N], AF.Sin, r=hwk, w=dstk, scale=0.999999)
    h2keep = h2T

    def short_conv(which, cbi, dst, dstk):
        jb = which * 4 + cbi
        w0, w1c, w2c = cfm[:, jb:jb + 1], cfm[:, 12 + jb:13 + jb], cfm[:, 24 + jb:25 + jb]
        bc = cfm[:, 36 + jb:37 + jb]

        def h_raw(tc, b):
            self.cp("act", rawb[:, tc * 512:(tc + 1) * 512], self.ps[b][:, :], r=[("ps", b)], w=rawbk)
        self.proj_fm(l, (O_BV, O_BX1, O_BX2)[which] + cbi * 128, T, h_raw)
        for (o, Ls) in seqs:
            e = o + Ls
            self.ts("dve", cv[:, o:e], rawb[:, o:e], w1c, bc, ALU.mult, ALU.add, r=rawbk + cfmk, w=cvk)
            self.stt("dve", cv[:, o + 1:e], rawb[:, o:e - 1], w0, cv[:, o + 1:e], ALU.mult, ALU.add,
                     r=rawbk + cfmk + cvk, w=cvk)
            self.stt("dve", dst[:, o:e - 1], rawb[:, o + 1:e], w2c, cv[:, o:e - 1], ALU.mult, ALU.add,
                     r=rawbk + cfmk + cvk, w=dstk)
            self.cp("dve", dst[:, e - 1:e], cv[:, e - 1:e], r=cvk, w=dstk)

    def fwd_dft(src_tm, srck, dstAB, dstk):
        width = src_tm.shape[-1]
        for fb in range(n128):
            b = self.rot()
            for tc in range(n128):
                self.mm(self.ps[b][:, 0:width], cm(tc, fb * 128, fb * 128 + 128), src_tm[:, tc, :],
                        tc == 0, tc == n128 - 1, r=cmk(tc) + srck, w=[("ps", b)])
            for tc in range(n128):
                lhs = smt0[:, tc, :] if fb == 0 else smm(tc, fb * 128, fb * 128 + 128)
                self.mm(self.ps[b][:, 256:256 + width], lhs, src_tm[:, tc, :],
                        tc == 0, tc == n128 - 1, r=(["cb"] if fb == 0 else smk(tc)) + srck, w=[("ps", b)])
            pv = self.ps[b][:, :].rearrange("p (a b) -> p a b", a=2)[:, :, 0:width]
            self.cp("act", dstAB[:, fb, :, 0:width], pv, r=[("ps", b)], w=dstk)

    def make_filter(conv, cbi):
        c0 = conv * 512 + cbi * 128
        self.dma("sp", w3p[0:64, :], self.fw3[l][:, c0:c0 + 128], w=w3pk)
        self.dma("sp", edp, self.ldec[l][:, c0:c0 + 128].broadcast_to([128, 128]), w=edpk)
        self.dma("sp", skp[0:1, :], self.skip[l][:, c0:c0 + 128], w=skpk)
        self.act(edp, edp, AF.Exp, r=edpk, w=edpk)
        for tc in range(n128):
            self.act(hw3[:, tc, :], edp, AF.Exp, r=edpk + ["cf"], w=hwk, scale=ndist[:, tc:tc + 1])
        for q in range(0, n128, 4):
            b = self.rot()
            nq = min(4, n128 - q)
            for j in range(nq):
                self.mm(self.ps[b][:, j * 128:(j + 1) * 128], h2keep[0:64, (q + j) * 128:(q + j + 1) * 128],
                        w3p[0:64, :], True, True, r=h2Tk + w3pk, w=[("ps", b)])
            self.stt("dve", hw[:, q * 128:(q + nq) * 128], hw[:, q * 128:(q + nq) * 128], DECAY_SHIFT,
                     self.ps[b][:, 0:nq * 128], ALU.add, ALU.mult, r=hwk + [("ps", b)], w=hwk)
        hsq = AB[:, :, 0, 0:128]
        self.act(hsq, hw3, AF.Square, r=hwk, w=ABk)
        b = self.rot()
        for tc in range(n128):
            self.mm(self.ps[b][:, 0:128], self.cfv("ones"), hsq[:, tc, :], tc == 0, tc == n128 - 1,
                    r=ABk + ["cf"], w=[("ps", b)])
        self.act(rsb, self.ps[b][:, 0:128], AF.Sqrt, r=[("ps", b), "consts"], w=rsbk, bias=self.c_rmseps)
        self.op("dve", lambda e: e.reciprocal(rsb, rsb), r=rsbk, w=rsbk, cost=0.6)
        rs_b = bass.AP(rsb.tensor, rsb.offset, [list(rsb.ap[0]), [0, n128], [1, 128]])
        self.tt("dve", hw3, hw3, rs_b, ALU.mult, r=hwk + rsbk, w=hwk)
        mid = (L // 2) // 128
        self.tt("dve", hw3[0:1, mid, :], hw3[0:1, mid, :], skp[0:1, :], ALU.add, r=hwk + skpk, w=hwk)
        self.cp("act", htm, hw3, r=hwk, w=htmk)
        fwd_dft(htm, htmk, AB, ABk)
        Ah, Bh = AB[:, :, 0, 0:128], AB[:, :, 1, 0:128]
        ca, cbb, nca = cab[:, 0:1], cab[:, 1:2], cab[:, 2:3]
        self.ts("dve", G1, Ah, ca, None, ALU.mult, None, r=ABk + ["cf"], w=G1k)
        self.stt("dve", G1, Bh, cbb, G1, ALU.mult, ALU.add, r=ABk + ["cf"] + G1k, w=G1k)
        self.ts("dve", G2, Ah, cbb, None, ALU.mult, None, r=ABk + ["cf"], w=G2k)
        self.stt("dve", G2, Bh, nca, G2, ALU.mult, ALU.add, r=ABk + ["cf"] + G2k, w=G2k)
        self.ts("dve", g1q0[0:1, :], AB[0:1, 0, 1, 0:128], 1.0 / (2 * L), None, ALU.mult, None, r=ABk, w=g1q0k)
        self.ts("dve", G1[0:1, 0, :], G1[0:1, 0, :], 0.5, None, ALU.mult, None, r=G1k, w=G1k)
        self.op("dve", lambda e: e.memset(G2[0:1, 0, :], 0.0), w=G2k)

    def long_conv(src, srck, mulsrc, mulk, dst_of, dstk_of):
        per_bank = 8
        idx = [(tc, si) for tc in range(n128) for si in range(NS)]
        for q in range(0, len(idx), per_bank):
            b = self.rot()
            grp = idx[q:q + per_bank]
            for j, (tc, si) in enumerate(grp):
                o = seqs[si][0]
                self.tr(self.psb[b][:, j * 128:(j + 1) * 128], src[:, o + tc * 128:o + (tc + 1) * 128],
                        self.cbv("ident"), r=srck + ["cb"], w=[("ps", b)])
            flat = vtm.rearrange("p a b -> p (a b)")
            self.cp("act", flat[:, q * 128:(q + len(grp)) * 128], self.psb[b][:, 0:len(grp) * 128],
                    r=[("ps", b)], w=vtmk)
        fwd_dft(vtm, vtmk, AB, ABk)
        A_, B_ = AB[:, :, 0, :], AB[:, :, 1, :]
        if NS == 1:
            g1b, g2b = G1, G2
            t1v = hw[:, 0:n128 * W].rearrange("p (a b) -> p a b", b=W)
            t2v = cv[:, 0:n128 * W].rearrange("p (a b) -> p a b", b=W)
            A4, B4, P4, Q4 = A_, B_, Pb, Qb
        else:
            def bc4(t):
                return bass.AP(t.tensor, t.offset, [list(t.ap[0]), [128, n128], [0, NS], [1, 128]])
            g1b, g2b = bc4(G1), bc4(G2)
            r4 = lambda t: t.rearrange("p a (s c) -> p a s c", s=NS)
            t1v = r4(hw[:, 0:n128 * W].rearrange("p (a b) -> p a b", b=W))
            t2v = r4(cv[:, 0:n128 * W].rearrange("p (a b) -> p a b", b=W))
            A4, B4, P4, Q4 = r4(A_), r4(B_), r4(Pb), r4(Qb)
        rk = ABk + G1k + G2k
        self.tt("dve", t1v, A4, g1b, ALU.mult, r=rk, w=hwk)
        self.tt("dve", t2v, B4, g2b, ALU.mult, r=rk, w=cvk)
        self.tt("dve", P4, t1v, t2v, ALU.add, r=hwk + cvk, w=Pbk)
        self.tt("dve", t1v, B4, g1b, ALU.mult, r=rk, w=hwk)
        self.tt("dve", t2v, A4, g2b, ALU.mult, r=rk, w=cvk)
        self.tt("dve", Q4, t1v, t2v, ALU.subtract, r=hwk + cvk, w=Qbk)
        if NS == 1:
            self.tt("dve", Qb[0:1, 0, :], AB[0:1, 0, 1, :], g1q0[0:1, :], ALU.mult, r=ABk + g1q0k + Qbk, w=Qbk)
        else:
            for si in range(NS):
                self.tt("dve", Qb[0:1, 0, si * 128:(si + 1) * 128], AB[0:1, 0, 1, si * 128:(si + 1) * 128],
                        g1q0[0:1, :], ALU.mult, r=ABk + g1q0k + Qbk, w=Qbk)
        Nn = min(L, 512)
        for si, (o, Ls) in enumerate(seqs):
            for n0 in range(0, Ls, Nn):
                b = self.rot()
                i = 0
                for fb in range(n128):
                    self.mm(self.ps[b][:, 0:Nn], Pb[:, fb, si * 128:(si + 1) * 128], cm(fb, n0, n0 + Nn),
                            i == 0, False, r=Pbk + cmk(fb), w=[("ps", b)])
                    i += 1
                for fb in range(n128):
                    rhs = sm0[:, n0:n0 + Nn] if fb == 0 else smm(fb, n0, n0 + Nn)
                    self.mm(self.ps[b][:, 0:Nn], Qb[:, fb, si * 128:(si + 1) * 128], rhs,
                            False, fb == n128 - 1, r=Qbk + (["cb"] if fb == 0 else smk(fb)), w=[("ps", b)])
                t0_ = o + n0
                self.tt("dve", dst_of(t0_, Nn), self.ps[b][:, 0:Nn], mulsrc[:, t0_:t0_ + Nn], ALU.mult,
                        r=[("ps", b)] + mulk, w=dstk_of(t0_, Nn))

    for cbi in range(4):
        short_conv(0, cbi, vb, vbk)
        short_conv(1, cbi, x1b, x1bk)
        short_conv(2, cbi, x2g, x2gk)

        def h_g(tc, b):
            self.act(gB[:, tc * 512:(tc + 1) * 512], self.ps[b][:, :], AF.Silu, r=[("ps", b)], w=gBk)
        self.proj_fm(l, O_BG + cbi * 128, T, h_g)
        self.tt("dve", x2g[:, 0:T], x2g[:, 0:T], gB[:, 0:T], ALU.mult, r=x2gk + gBk, w=x2gk)
        make_filter(0, cbi)
        long_conv(vb, vbk, x1b, x1bk, lambda t0_, n: zb[:, t0_:t0_ + n], lambda t0_, n: zbk)
        make_filter(1, cbi)
        long_conv(zb, zbk, x2g, x2gk, lambda t0_, n: self.mix[:, 4 + cbi, t0_:t0_ + n],
                  lambda t0_, n: self.mx_keys(4 + cbi, t0_, t0_ + n))


DECAY_SHIFT = 0.05
KB.branch_b = _branch_b


def _phase_c(self, l, g):
    G = self.group(g)
    self.bg_flush(l)
    T, NT = G["T"], G["T"] // 128
    cj = G["cond"]
    src = self.x_in[g] if l == 0 else self.xm[g]
    dst = self.xm[g] if l == 0 else self.y_out[g]
    wv = [self.big1[:].rearrange("p a b -> p (a b)").rearrange("p (k n) -> p k n", n=2048),
          self.big2[:].rearrange("p a b -> p (a b)").rearrange("p (k n) -> p k n", n=2048)]

    def wkeys(kc):
        j = kc % 8
        if kc < 8:
            return [("hm", 2 * j + d, t) for d in range(2) for t in range(8)]
        return [("b2", 2 * j), ("b2", 2 * j + 1)]
    if g == "s":
        for kc in range(KC):
            self.dma("pool", wv[kc // 8][:, kc % 8, :], self.w_out[l][kc * 128:(kc + 1) * 128, :], w=wkeys(kc))
    xts = [self.arena(0, 2048), self.arena(2048, 2048), self.arena(11392, 2048)]
    gbc, gbck = self.arena(4096, 2048)
    lng, lngk = self.arena(6144, 2048)
    lnb, lnbk = self.arena(8192, 2048)
    gblk, gblkk = self.arena(10240, 128)
    tbs = [self.arena(10368, 512), self.arena(10880, 512), self.arena(13440, 512), self.arena(13952, 512)]
    sm = self.small
    self.dma("sp", lng, self.ln_g[l].broadcast_to([128, D]), w=lngk)
    self.dma("sp", lnb, self.ln_b[l].broadcast_to([128, D]), w=lnbk)
    for q in range(4):
        b = self.rot()
        for j in range(4):
            blk = q * 4 + j
            gcol = self.modfm[:, l, 32 + blk, cj:cj + 1]
            gsrc = bass.AP(gcol.tensor, gcol.offset, [list(gcol.ap[0]), [0, 128]])
            self.cp("dve", gblk, gsrc, r=[("modfm", l, 2)], w=gblkk)
            self.mm(self.ps[b][:, j * 128:(j + 1) * 128], gblk, self.cfv("ident"), True, True,
                    r=gblkk + ["cf"], w=[("ps", b)])
        self.act(gbc[:, q * 512:(q + 1) * 512], self.ps[b][:, :], AF.Identity, r=[("ps", b)], w=gbck,
                 scale=1.0 / ALPHA)
    SB = (0, 32, 96)
    for tt in range(NT):
        bi = tt % 3
        xt, xk = xts[bi]
        sb0 = SB[bi]
        st = sm[:, sb0:sb0 + 24].rearrange("p (a b) -> p a b", b=6)
        mv = sm[:, sb0 + 24:sb0 + 26]
        rstd = sm[:, sb0 + 26:sb0 + 27]
        nmr = sm[:, sb0 + 27:sb0 + 28]
        sk = [("smA", bi)]
        rows = slice(tt * 128, (tt + 1) * 128)
        self.dma("sp", xt, src[rows, :], r=[("xm", g, tt)] if l > 0 else [], w=xk)
        for nb in range(4):
            b = self.rot()
            for kc in range(KC):
                self.mm(self.ps[b][:, :], self.mix[:, kc, rows], wv[kc // 8][:, kc % 8, nb * 512:(nb + 1) * 512],
                        kc == 0, kc == KC - 1, r=self.mx_keys(kc, tt * 128, tt * 128 + 128) + wkeys(kc),
                        w=[("ps", b)])
            tb, tbk = tbs[nb % 4]
            cols = slice(nb * 512, (nb + 1) * 512)
            self.tt("dve", tb, self.ps[b][:, :], gbc[:, cols], ALU.mult, r=[("ps", b)] + gbck, w=tbk)
            self.tt("dve", xt[:, cols], xt[:, cols], tb, ALU.add, r=xk + tbk, w=xk)
        for j in range(4):
            self.op("dve", lambda e, j=j, st=st, xt=xt: e.bn_stats(st[:, j, :], xt[:, j * 512:(j + 1) * 512]),
                    r=xk, w=sk, cost=0.65)
        self.op("dve", lambda e, st=st, mv=mv: e.bn_aggr(mv, st), r=sk, w=sk)
        self.act(rstd, mv[:, 1:2], AF.Sqrt, r=sk + ["consts"], w=sk, bias=self.c_lneps_c)
        self.op("dve", lambda e, rstd=rstd: e.reciprocal(rstd, rstd), r=sk, w=sk)
        self.stt("dve", nmr, mv[:, 0:1], -1.0, rstd, ALU.mult, ALU.mult, r=sk, w=sk)
        self.act(xt, xt, AF.Identity, r=xk + sk, w=xk, bias=nmr, scale=rstd)
        self.tt("dve", xt, xt, lng, ALU.mult, r=xk + lngk, w=xk)
        self.tt("dve", xt, xt, lnb, ALU.add, r=xk + lnbk, w=xk)
        self.dma("sp", dst[rows, :], xt, r=xk, w=[("xm", g, tt)] if l == 0 else [], store=(l == DEPTH - 1))


KB.phase_c = _phase_c


def build_full(debug=None):
    kb = KB(debug=debug)
    kb.setup()
    kb.bg_enable = True
    for l in range(DEPTH):
        for g in ("s", "p"):
            kb.phase_a(l, g)
            kb.branch_a(l, g)
            kb.branch_b(l, g)
            kb.branch_c(l, g)
            kb.branch_d(l, g)
            kb.phase_c(l, g)
    kb.s.emit()
    return kb


_KB = None


def kernel(**inputs):
    global _KB
    if _KB is None:
        _KB = build_full()
    kb = _KB
    sh = shared_inputs(inputs)
    in_maps = [core_inputs(inputs, i, sh) for i in range(8)]
    res = run_bass_kernel_spmd(kb.nc, in_maps, core_ids=list(range(8)))
    rs = res.results
    y_p = np.concatenate([np.asarray(r["y_p"], np.float32).reshape(2, 256, D) for r in rs], 0)
    y_s = np.stack([np.asarray(r["y_s"], np.float32) for r in rs], 0)
    kv = []
    for name in ("nck", "ncv", "ndk", "ndv"):
        kv.append(np.concatenate([np.asarray(r[name], np.float32).reshape(2, 2, 256, 2, 64) for r in rs], 0))
    return (y_p, y_s, kv[0], kv[1], kv[2], kv[3])
```

```python
import math
import os
from contextlib import ExitStack

import numpy as np
import ml_dtypes

import concourse.bass as bass
import concourse.mybir as mybir
from concourse.bass_utils import run_bass_kernel_spmd

F32 = mybir.dt.float32
BF16 = mybir.dt.bfloat16
AF = mybir.ActivationFunctionType
ALU = mybir.AluOpType

D = 2048
KC = 16
INW = 5632
DEPTH = 2
HD = 64
ALPHA = (2 * DEPTH) ** 0.25
LN_EPS = 1e-5
RMS_EPS = 1e-6
PI = math.pi
TWO_PI = 2.0 * math.pi
SIN_OFF = PI + 16 * TWO_PI

O_A, O_AG = 0, 512
O_BV, O_BX1, O_BX2, O_BG = 1024, 1536, 2048, 2560
O_C, O_D = 3072, 4352


class Op:
    __slots__ = ("eng", "fn", "deps", "dma", "sem", "val", "needed", "idx", "cost", "tbl", "done")

    def __init__(self, eng, fn, dma):
        self.eng = eng
        self.fn = fn
        self.dma = dma
        self.deps = set()
        self.sem = None
        self.val = 0
        self.needed = False


class Sched:
    ENGS = ("pe", "act", "dve", "pool", "sp")
    NDSEM = 12

    def __init__(self, nc, es):
        self.nc = nc
        self.ops = {e: [] for e in self.ENGS}
        self.last_w = {}
        self.readers = {}
        self.esem = {e: es.enter_context(nc.semaphore("sem_" + e)) for e in ("pe", "act", "dve", "pool")}
        self.dsem = {q: [es.enter_context(nc.semaphore("dsem_%s_%d" % (q, i))) for i in range(self.NDSEM)]
                     for q in ("sp", "pool", "act")}
        self.dcount = {q: [0] * self.NDSEM for q in self.dsem}
        self.dlast = {q: [None] * self.NDSEM for q in self.dsem}
        self.dnext = {q: 0 for q in self.dsem}
        self.nops = 0
        self.stores = []

    def op(self, eng, fn, r=(), w=(), dma=False, store=False, cost=0.5, tbl=None):
        pr = [k for k in r if isinstance(k, tuple) and k[0] == "ps"]
        if pr:
            w = list(w) + pr
        o = Op(eng, fn, dma)
        o.cost = cost
        o.tbl = tbl
        o.idx = self.nops
        self.nops += 1
        deps = o.deps
        for k in r:
            lw = self.last_w.get(k)
            if lw is not None:
                deps.add(lw)
        for k in w:
            lw = self.last_w.get(k)
            if lw is not None:
                deps.add(lw)
            rd = self.readers.get(k)
            if rd:
                deps.update(rd.values())
        for k in r:
            self.readers.setdefault(k, {})[o.idx] = o
        for k in w:
            self.last_w[k] = o
            self.readers[k] = {}
        if dma:
            q = eng
            i = self.dnext[q]
            self.dnext[q] = (i + 1) % self.NDSEM
            prev = self.dlast[q][i]
            if prev is not None:
                deps.add(prev)
            self.dcount[q][i] += 16
            o.sem = self.dsem[q][i]
            o.val = self.dcount[q][i]
            self.dlast[q][i] = o
            o.needed = True
            if store:
                self.stores.append(o)
        deps.discard(o)
        self.ops[eng].append(o)
        return o

    def reorder(self, window=48):
        ENG = self.ENGS
        pend = {e: list(self.ops[e]) for e in ENG}
        head = {e: 0 for e in ENG}
        sched = {e: [False] * len(pend[e]) for e in ENG}
        free = {e: 0.0 for e in ENG}
        neworder = {e: [] for e in ENG}
        cur_tbl = [None]
        dma_free = [0.0]
        for e in ENG:
            for o in pend[e]:
                o.done = None
        remaining = sum(len(v) for v in pend.values())
        LAT = float(os.environ.get("KB_LAT", "0.35"))
        best = {e: None for e in ENG}
        rtc = {}
        USE_RANK = os.environ.get("KB_RANK", "1") == "1"
        allo = sorted((o for v in pend.values() for o in v), key=lambda o: o.idx)
        rank = {}
        if USE_RANK:
            succ_best = {}
            for o in reversed(allo):
                r0 = succ_best.get(o.idx, 0.0) + (o.cost + (2.0 if o.dma else 0.0))
                rank[o.idx] = r0
                for d in o.deps:
                    v = r0 + (LAT if (d.eng != o.eng or d.dma) else 0.0)
                    if v > succ_best.get(d.idx, 0.0):
                        succ_best[d.idx] = v

        def find(e):
            lst, sc = pend[e], sched[e]
            h = head[e]
            n = len(lst)
            while h < n and sc[h]:
                h += 1
            head[e] = h
            bt, bi, brk = None, -1, 0.0
            cnt = 0
            i = h
            f = free[e]
            while i < n and cnt < window:
                if not sc[i]:
                    cnt += 1
                    o = lst[i]
                    rt = rtc.get(o.idx)
                    ok = True
                    if rt is None:
                        rt = 0.0
                        for d in o.deps:
                            dd = d.done
                            if dd is None:
                                ok = False
                                break
                            if d.eng != e or d.dma:
                                dd += LAT
                            elif e != "pe":
                                dd += 0.06
                            if dd > rt:
                                rt = dd
                        if ok:
                            rtc[o.idx] = rt
                    if ok:
                        st = rt if rt > f else f
                        if USE_RANK:
                            rk = rank[o.idx]
                            if bt is None or st < bt - 1e-9 or (st <= bt + 1e-9 and rk > brk):
                                bt, bi, brk = st, i, rk
                        elif bt is None or st < bt - 1e-9:
                            bt, bi = st, i
                            if st <= f + 1e-9:
                                break
                i += 1
            best[e] = (bt, bi) if bt is not None else None

        for e in ENG:
            find(e)
        while remaining:
            be, bt, bi = None, None, -1
            for e in ENG:
                b = best[e]
                if b is not None and (bt is None or b[0] < bt):
                    be, bt, bi = e, b[0], b[1]
            assert be is not None, "scheduler deadlock"
            o = pend[be][bi]
            sched[be][bi] = True
            neworder[be].append(o)
            remaining -= 1
            if o.dma:
                free[be] = bt + 0.06
                s0 = max(bt, dma_free[0])
                dma_free[0] = s0 + o.cost
                o.done = dma_free[0] + 2.0
            else:
                c = o.cost
                if be == "act" and o.tbl is not None and o.tbl != cur_tbl[0]:
                    c += 1.3
                    cur_tbl[0] = o.tbl
                o.done = bt + c
                free[be] = o.done
            for e in ENG:
                find(e)
        self.ops = neworder
        self.sim_time = max(free.values())

    def emit(self):
        nc = self.nc
        if os.environ.get("KB_NOREORDER") != "1":
            self.reorder(window=256)
        for e in self.ENGS:
            for o in self.ops[e]:
                for d in o.deps:
                    if d.dma:
                        continue
                    if d.eng == "pe" and o.eng == "pe" and not o.dma:
                        continue
                    d.needed = True
        for e in ("pe", "act", "dve", "pool"):
            c = 0
            for o in self.ops[e]:
                if o.dma:
                    continue
                if o.needed:
                    c += 1
                    o.sem = self.esem[e]
                    o.val = c
        stores = self.stores

        def run(engname, eng, final=False):
            known = {}
            for o in self.ops[engname]:
                waits = {}
                for d in o.deps:
                    if (not d.dma) and d.eng == "pe" and engname == "pe" and not o.dma:
                        continue
                    s = d.sem
                    if waits.get(s, (None, 0))[1] < d.val:
                        waits[s] = (s, d.val)
                for s, v in waits.values():
                    if known.get(s, 0) >= v:
                        continue
                    known[s] = v
                    eng.wait_ge(s, v)
                ins = o.fn(eng)
                if o.needed:
                    ins.then_inc(o.sem, 16 if o.dma else 1)
            if final:
                waits = {}
                for d in stores:
                    if waits.get(d.sem, (None, 0))[1] < d.val:
                        waits[d.sem] = (d.sem, d.val)
                for s, v in waits.values():
                    eng.wait_ge(s, v)

        with nc.Block() as block:
            @block.sync
            def _(e):
                run("sp", e, final=True)

            @block.gpsimd
            def _(e):
                run("pool", e)

            @block.scalar
            def _(e):
                run("act", e)

            @block.vector
            def _(e):
                run("dve", e)

            @block.tensor
            def _(e):
                run("pe", e)


def _chunked(a):
    L = a.shape[0]
    return np.ascontiguousarray(a.reshape(L // 128, 128, -1).transpose(1, 0, 2))


def _bf(a):
    return np.ascontiguousarray(a.astype(np.float32)).astype(ml_dtypes.bfloat16)


CF = {}
_cf_off = 0
for _name, _w in (("ident", 128), ("ones", 128), ("bdc", 128), ("bds", 128),
                  ("cos", 1024), ("sin", 1024),
                  ("feats1024", 1024), ("feats256", 256),
                  ("ndist1024", 8), ("ndist256", 2),
                  ("cab1024", 3), ("cab256", 3)):
    CF[_name] = (_cf_off, _w)
    _cf_off += _w
CF_W = _cf_off

CB = {}
_cb_off = 0
for _name, _w in (("ident", 128), ("prot", 128), ("triu", 128), ("tril", 128), ("onesblk", 128),
                  ("smt0_1024", 8 * 128), ("sm0_1024", 1024), ("smt0_256", 2 * 128), ("sm0_256", 256),
                  ("t256", 4 * 2 * 256)):
    CB[_name] = (_cb_off, _w)
    _cb_off += _w
CB_W = _cb_off


def make_tables():
    cf = np.zeros((128, CF_W), np.float64)
    cb = np.zeros((128, CB_W), np.float64)

    def putf(name, arr):
        o, w = CF[name]
        arr = np.asarray(arr, np.float64)
        cf[: arr.shape[0], o:o + w] = arr.reshape(arr.shape[0], w)

    def putb(name, arr):
        o, w = CB[name]
        arr = np.asarray(arr, np.float64)
        cb[: arr.shape[0], o:o + w] = arr.reshape(arr.shape[0], w)

    putf("ident", np.eye(128))
    putf("ones", np.ones((128, 128)))
    w64 = np.arange(64)
    c64 = np.cos(2 * np.pi * np.outer(w64, w64) / 64) / 8.0
    s64 = np.sin(2 * np.pi * np.outer(w64, w64) / 64) / 8.0
    z = np.zeros((64, 64))
    putf("bdc", np.block([[c64, z], [z, c64]]))
    putf("bds", -np.block([[s64, z], [z, s64]]))
    t = np.arange(1024)
    inv = 10000.0 ** (-np.arange(0, 32, 2) / 32.0)
    cosT = np.zeros((128, 1024))
    sinT = np.zeros((128, 1024))
    for p in range(128):
        d = p % 64
        j = d % 16
        pos = (t // 64) if d < 32 else (t % 64)
        ang = (pos.astype(np.float32) * inv[j].astype(np.float32)).astype(np.float32)
        cosT[p] = np.cos(ang)
        sinT[p] = np.sin(ang)
    putf("cos", cosT)
    putf("sin", sinT)
    prot = np.zeros((128, 128))
    for m in range(128):
        if m % 32 < 16:
            prot[m + 16, m] = -1.0
        else:
            prot[m - 16, m] = 1.0
    putb("prot", prot)
    putb("ident", np.eye(128))
    ii = np.arange(128)[:, None]
    jj = np.arange(128)[None, :]
    putb("triu", (ii <= jj).astype(np.float64))
    putb("tril", (jj <= ii).astype(np.float64))
    putb("onesblk", (ii // 64 == jj // 64).astype(np.float64))
    big = {}
    for L in (1024, 256):
        tt = np.arange(L, dtype=np.float64)
        tn = (tt.astype(np.float32) / np.float32(L)).astype(np.float64)
        fr = np.arange(1, 17, dtype=np.float64)
        ang = 2.0 * np.pi * tn[:, None] * fr[None, :]
        feats = np.concatenate([tn[:, None], np.cos(ang), np.sin(ang)], -1)
        putf("feats%d" % L, feats.T)
        dist = np.abs(tt - L // 2) / (L / 2)
        putf("ndist%d" % L, (-dist).reshape(L // 128, 128).T)
        pm = np.arange(128) % 4
        ca = np.array([1.0, 0.0, -1.0, 0.0])[pm] / L
        cbv = np.array([0.0, 1.0, 0.0, -1.0])[pm] / L
        putf("cab%d" % L, np.stack([ca, cbv, -ca], 1))
        f = np.arange(L)[:, None]
        n = np.arange(L)[None, :]
        Cm = np.cos(np.pi * f * n / L)
        Sm = np.sin(np.pi * f * n / L)
        CL = np.cos(2 * np.pi * f * n / L) / np.sqrt(L)
        SL = np.sin(2 * np.pi * f * n / L) / np.sqrt(L)
        smt0 = Sm[:, 0:128].copy()
        smt0[:, 0] = (-1.0) ** np.arange(L)
        sm0 = Sm[0:128, :].copy()
        sm0[0, :] = (-1.0) ** np.arange(L)
        putb("smt0_%d" % L, _chunked(smt0).reshape(128, -1))
        putb("sm0_%d" % L, sm0)
        big[L] = np.stack([_chunked(CL), _chunked(SL), _chunked(Cm), _chunked(Sm)], 1)
    putb("t256", big[256].reshape(128, -1))
    return dict(cf32=np.ascontiguousarray(cf.astype(np.float32)),
                cbf=_bf(cb),
                t1024=_bf(big[1024]))


def _ap(t, off_extra, dims):
    return bass.AP(t.tensor, t.offset + off_extra, [list(t.ap[0])] + [list(d) for d in dims])


class KB:
    def __init__(self, debug=None, stop_after=None):
        self.debug = debug or {}
        self.stop_after = stop_after
        self.es = ExitStack()
        nc = self.nc = bass.Bass("TRN2", target_bir_lowering=False)
        self.s = Sched(nc, self.es)
        self.rot_i = 0
        self.dbg_outs = {}
        self._decl()

    def dram(self, name, shape, dt=F32, kind="ExternalInput"):
        return self.nc.dram_tensor(name, list(shape), dt, kind=kind).ap()

    def sb(self, name, shape, dt=F32):
        return self.es.enter_context(self.nc.sbuf_tensor(name, list(shape), dt))

    def _decl(self):
        nc = self.nc
        d = self.dram
        self.x_in = {"s": d("x_s", [1024, D]), "p": d("x_p", [512, D])}
        self.cache = {k: d(k, [2, 256, 128]) for k in ("ck_c", "cv_c", "ck_d", "cv_d")}
        self.cond = d("cond", [32, 128])
        self.w_ada = d("w_ada", [2, D, 3 * D])
        self.b_ada = d("b_ada", [2, 48, 128])
        self.w_in = d("w_in", [2, D, INW])
        self.w_f = d("w_f", [2, 512, 512])
        self.conv_w = d("conv_w", [2, 36, 128])
        self.conv_b = d("conv_b", [2, 12, 128])
        self.fw1 = d("fw1", [2, 33, 64])
        self.fb1 = d("fb1", [2, 64, 1])
        self.fw2 = d("fw2", [2, 64, 64])
        self.fb2 = d("fb2", [2, 64, 1])
        self.fw3 = d("fw3", [2, 64, 1024])
        self.ldec = d("ldec", [2, 1, 1024])
        self.skip = d("skip", [2, 1, 1024])
        self.sink = d("sink", [2, 1, 8])
        self.qn = d("qn", [2, 64, 1])
        self.kn = d("kn", [2, 64, 1])
        self.knr = d("knr", [2, 1, 64])
        self.w_out = d("w_out", [2, D, D])
        self.ln_g = d("ln_g", [2, 1, D])
        self.ln_b = d("ln_b", [2, 1, D])
        self.cf32_d = d("cf32", [128, CF_W])
        self.cbf_d = d("cbf", [128, CB_W], BF16)
        self.t1024_d = d("t1024", [128, 4, 8, 1024], BF16)
        o = lambda n, s: self.dram(n, s, F32, "ExternalOutput")
        self.y_out = {"s": o("y_s", [1024, D]), "p": o("y_p", [512, D])}
        self.kv_out = {k: o(k, [2, 2, 256, 128]) for k in ("nck", "ncv", "ndk", "ndv")}
        self.wcache = [self.dram("wcache%d" % l, [44, 128, 2048], BF16, "Internal") for l in range(DEPTH)]
        self.xm = {"s": self.dram("xm_s", [1024, D], F32, "Internal"),
                   "p": self.dram("xm_p", [512, D], F32, "Internal")}
        self.cf = self.sb("cf", [128, CF_W])
        self.cb = self.sb("cb", [128, CB_W], BF16)
        self.big1 = self.sb("big1", [128, 16, 1024], BF16)
        self.big2 = self.sb("big2", [128, 16, 1024], BF16)
        self.mix = self.sb("mix", [128, 16, 1024], BF16)
        self.NSLAB = 5
        self.slab = [self.sb("slab%d" % i, [128, 16, 128], BF16) for i in range(self.NSLAB)]
        self.slab_i = 0
        self.bg_enable = False
        self.bg_jobs = []
        self.modfm = self.sb("modfm", [128, 2, 48, 2])
        self.sc1 = self.sb("sc1", [128, 2, 16, 2])
        self.scT = self.sb("scT", [128, 32], BF16)
        self.small = self.sb("small", [128, 256])
        self.SCRW = 14848
        self.scr = self.sb("scr", [128, self.SCRW])
        self.ps = [self.es.enter_context(nc.psum_tensor("ps%d" % i, [128, 512], F32)) for i in range(8)]
        self.psb = [p.bitcast(BF16) for p in self.ps]

    def cfv(self, name, rows=128):
        o, w = CF[name]
        return self.cf[0:rows, o:o + w]

    def cbv(self, name):
        o, w = CB[name]
        return self.cb[:, o:o + w]

    def arena(self, off_words, nwords, dt=F32, shape=None):
        assert off_words + nwords <= self.SCRW, (off_words, nwords)
        a = self.scr[:, off_words:off_words + nwords]
        if dt == BF16:
            a = a.bitcast(BF16)
        if shape is not None:
            names = " ".join("d%d" % i for i in range(len(shape)))
            a = a.rearrange("p (%s) -> p %s" % (names, names), **{"d%d" % i: shape[i] for i in range(len(shape))})
        keys = [("S", u) for u in range(off_words // 128, (off_words + nwords + 127) // 128)]
        return a, keys

    nrot = 8

    def rot(self):
        i = self.rot_i % self.nrot
        self.rot_i = (i + 1) % self.nrot
        return i

    def op(self, *a, **k):
        return self.s.op(*a, **k)

    @staticmethod
    def _nfree(ap):
        n = 1
        for d in ap.shape[1:]:
            n *= d
        return n

    def dma(self, q, out, in_, r=(), w=(), store=False):
        nb = 1
        for d in in_.shape:
            nb *= d
        nb *= 4 if in_.dtype == F32 else 2
        return self.s.op(q, lambda e: e.dma_start(out=out, in_=in_), r=r, w=w, dma=True, store=store,
                         cost=nb / 230e3)

    def mm(self, out, lhsT, rhs, start, stop, r, w):
        n = self._nfree(out)
        c = max(64, n) / 1950.0 + 0.01
        if lhsT.dtype == F32:
            c *= 4
        return self.s.op("pe", lambda e: e.matmul(out, lhsT=lhsT, rhs=rhs, start=start, stop=stop,
                                                  skip_group_check=True), r=r, w=w, cost=c)

    def tr(self, out, in_, ident, r, w):
        c = 0.07 * (4 if in_.dtype == F32 else 1)
        return self.s.op("pe", lambda e: e.transpose(out, in_, ident), r=r, w=w, cost=c)

    def act(self, out, in_, func, r, w, bias=None, scale=None, eng="act"):
        kw = {}
        if bias is not None:
            kw["bias"] = bias
        if scale is not None:
            kw["scale"] = scale
        tbl = {AF.Exp: "exp", AF.Silu: "silu", AF.Sin: "silu", AF.Sqrt: "sqrt"}.get(func)
        return self.s.op(eng, lambda e: e.activation(out, in_, func, **kw), r=r, w=w,
                         cost=0.2 + 0.00075 * self._nfree(out), tbl=tbl)

    def tt(self, eng, out, in0, in1, op_, r, w):
        return self.s.op(eng, lambda e: e.tensor_tensor(out, in0, in1, op_), r=r, w=w,
                         cost=0.08 + 0.0011 * self._nfree(out))

    def ts(self, eng, out, in0, s1, s2, op0, op1, r, w):
        c = 0.08 + 0.0011 * self._nfree(out)
        if op1 is None:
            return self.s.op(eng, lambda e: e.tensor_scalar(out, in0, s1, None, op0), r=r, w=w, cost=c)
        return self.s.op(eng, lambda e: e.tensor_scalar(out, in0, s1, s2, op0, op1), r=r, w=w, cost=c)

    def stt(self, eng, out, in0, sc, in1, op0, op1, r, w):
        return self.s.op(eng, lambda e: e.scalar_tensor_tensor(out, in0, sc, in1, op0, op1), r=r, w=w,
                         cost=0.08 + 0.0011 * self._nfree(out))

    def cp(self, eng, out, in_, r, w):
        if eng == "act":
            return self.s.op(eng, lambda e: e.copy(out, in_), r=r, w=w, cost=0.2 + 0.00075 * self._nfree(out))
        return self.s.op(eng, lambda e: e.tensor_copy(out, in_), r=r, w=w, cost=0.08 + 0.0011 * self._nfree(out))

    def dump(self, name, src_ap, shape, keys, dt=F32):
        if name not in self.debug:
            return
        o = self.dram("dbg_" + name, shape, dt, "ExternalOutput")
        self.dbg_outs["dbg_" + name] = shape
        self.dma("sp", o, src_ap, r=keys, store=True)

    def load_fm(self, dst, src, n, dkeys, tmp_off):
        rows, rk = self.arena(tmp_off, 128)
        b = self.rot()
        self.dma("sp", rows[0:n, :], src, w=rk)
        self.tr(self.ps[b][:, 0:n], rows[0:n, :], self.cfv("ident")[0:n, 0:n], r=rk + ["cf"], w=[("ps", b)])
        self.cp("dve", dst, self.ps[b][:, 0:n], r=[("ps", b)], w=dkeys)

    def load_slab(self, wsrc, col0, ncols=128, bg=True):
        if bg and self.bg_enable:
            self.bg_tick()
        i = self.slab_i
        self.slab_i = (i + 1) % self.NSLAB
        sl = self.slab[i]
        src = wsrc[:, col0:col0 + ncols].rearrange("(kc p) c -> p kc c", p=128)
        self.dma("pool", sl[:, :, 0:ncols], src, w=[("slab", i)])
        return sl, [("slab", i)]

    def load_win(self, l, col0):
        idx = col0 // 128
        if self.cur_g == "s":
            sl, sk = self.load_slab(self.w_in[l], col0)
            self.dma("sp", self.wcache[l][idx], sl[:].rearrange("p a b -> p (a b)"), r=sk, w=[("wc", l, idx)])
            return sl, sk
        if self.bg_enable:
            self.bg_tick()
        i = self.slab_i
        self.slab_i = (i + 1) % self.NSLAB
        sl = self.slab[i]
        self.dma("sp", sl[:].rearrange("p a b -> p (a b)"), self.wcache[l][idx], r=[("wc", l, idx)],
                 w=[("slab", i)])
        return sl, [("slab", i)]

    def mod_slab(self, l, s):
        sl, sk = self.load_slab(self.w_ada[l], s * 128, bg=False)
        b = self.rot()
        for kc in range(KC):
            self.mm(self.ps[b][:, 0:2], sl[:, kc, :], self.scT[:, kc:32:16], kc == 0, kc == KC - 1,
                    r=sk + ["scT"], w=[("ps", b)])
        bcol = self.bfm[:, l, s:s + 1]
        key = ("modfm", l, s // 16)
        self.ts("dve", self.modfm[:, l, s, :], self.ps[b][:, 0:2], bcol, None, ALU.add, None,
                r=[("ps", b), ("bfm", l)], w=[key])
        if s // 16 == 1:
            self.ts("dve", self.sc1[:, l, s - 16, :], self.ps[b][:, 0:2], bcol, 1.0, ALU.add, ALU.add,
                    r=[("ps", b), ("bfm", l)], w=[("sc1", l)])

    def bg_tick(self):
        if self.bg_jobs:
            l, s = self.bg_jobs.pop(0)
            self.mod_slab(l, s)

    def bg_flush(self, l):
        while self.bg_jobs and self.bg_jobs[0][0] <= l:
            self.bg_tick()

    def setup(self):
        self.dma("sp", self.cf[:], self.cf32_d, w=["cf"])
        self.dma("sp", self.cb[:], self.cbf_d, w=["cb"])
        sm = self.small
        self.c_lneps = sm[:, 200:201]
        self.c_rmseps = sm[:, 201:202]
        self.op("dve", lambda e: e.memset(sm[:, 200:201], LN_EPS), w=["consts"])
        self.op("dve", lambda e: e.memset(sm[:, 201:202], RMS_EPS), w=["consts"])
        self.c_lneps_c = sm[:, 202:203]
        self.op("dve", lambda e: e.memset(sm[:, 202:203], LN_EPS / (ALPHA * ALPHA)), w=["consts"])
        cfm, ck = self.arena(256, 32)
        self.load_fm(cfm, self.cond, 32, ck, 0)
        self.act(self.scT[:], cfm, AF.Silu, r=ck, w=["scT"])
        self.bfm = self.sb("bfm", [128, 2, 48])
        for l in range(DEPTH):
            self.load_fm(self.bfm[:, l, :], self.b_ada[l], 48, [("bfm", l)], 1024 + 128 * l)
        self.bg_jobs = []
        for s in range(32):
            self.mod_slab(0, s)
        self.bg_jobs = [(0, s) for s in range(32, 48)] + [(1, s) for s in range(48)]

    @staticmethod
    def group(g):
        if g == "s":
            return dict(T=1024, cond=0, seqs=[(0, 1024)], L=1024)
        return dict(T=512, cond=1, seqs=[(0, 256), (256, 256)], L=256)

    cur_g = "s"

    def hm(self, kc, t0, t1):
        if self.cur_g == "s":
            return self.big1[:, kc, t0:t1]
        return self.mix[:, kc, 512 + t0:512 + t1]

    def hm_keys(self, kc, t0, t1):
        if self.cur_g == "s":
            return [("hm", kc, t) for t in range(t0 // 128, (t1 + 127) // 128)]
        return [("mx", kc, 4 + t) for t in range(t0 // 128, (t1 + 127) // 128)]

    def mx_keys(self, kc, t0, t1):
        return [("mx", kc, t) for t in range(t0 // 128, (t1 + 127) // 128)]

    def phase_a(self, l, g):
        G = self.group(g)
        self.cur_g = g
        while self.bg_jobs and (self.bg_jobs[0][0] < l or (self.bg_jobs[0][0] == l and self.bg_jobs[0][1] < 32)):
            self.bg_tick()
        src = self.x_in[g] if l == 0 else self.xm[g]
        cj = G["cond"]
        NB_A = 4
        xts = [self.arena(2048 * i, 2048) for i in range(NB_A)]
        xns = [self.arena(2048 * NB_A + 1024 * i, 1024, BF16) for i in range(NB_A)]
        sm = self.small
        SB = (0, 32, 96, 128)
        for tt in range(G["T"] // 128):
            bi = tt % NB_A
            xt, xk = xts[bi]
            xn, nk = xns[bi]
            sb0 = SB[bi]
            st = sm[:, sb0:sb0 + 24].rearrange("p (a b) -> p a b", b=6)
            mv = sm[:, sb0 + 24:sb0 + 26]
            rstd = sm[:, sb0 + 26:sb0 + 27]
            nmr = sm[:, sb0 + 27:sb0 + 28]
            sk = [("smA", bi)]
            self.dma("sp", xt, src[tt * 128:(tt + 1) * 128, :], w=xk)
            for j in range(4):
                self.op("dve", lambda e, j=j, st=st, xt=xt: e.bn_stats(st[:, j, :], xt[:, j * 512:(j + 1) * 512]),
                        r=xk, w=sk, cost=0.65)
            self.op("dve", lambda e, st=st, mv=mv: e.bn_aggr(mv, st), r=sk, w=sk)
            self.act(rstd, mv[:, 1:2], AF.Sqrt, r=sk + ["consts"], w=sk, bias=self.c_lneps)
            self.op("dve", lambda e, rstd=rstd: e.reciprocal(rstd, rstd), r=sk, w=sk)
            self.stt("dve", nmr, mv[:, 0:1], -1.0, rstd, ALU.mult, ALU.mult, r=sk, w=sk)
            self.act(xn, xt, AF.Identity, r=xk + sk, w=nk, bias=nmr, scale=rstd)
            for half in range(2):
                b = self.rot()
                for j in range(8):
                    kc = half * 8 + j
                    self.tr(self.psb[b][:, j * 128:(j + 1) * 128], xn[:, kc * 128:(kc + 1) * 128],
                            self.cbv("ident"), r=nk + ["cb"], w=[("ps", b)])
                for j in range(8):
                    kc = half * 8 + j
                    dst = self.hm(kc, tt * 128, (tt + 1) * 128)
                    srcp = self.psb[b][:, j * 128:(j + 1) * 128]
                    s1 = self.sc1[:, l, kc, cj:cj + 1]
                    sh = self.modfm[:, l, kc, cj:cj + 1]
                    wk = self.hm_keys(kc, tt * 128, tt * 128 + 128)
                    mk = [("ps", b), ("sc1", l), ("modfm", l, 0)]
                    if j % 2 == 0:
                        self.ts("dve", dst, srcp, s1, sh, ALU.mult, ALU.add, r=mk, w=wk)
                    else:
                        self.act(dst, srcp, AF.Identity, r=mk, w=wk, bias=sh, scale=s1)
        if l == 0:
            self.dump("hmodT_" + g, self.big1[:, :, 0:G["T"]] if g == "s" else self.mix[:, :, 512:1024], [128, 16, G["T"]],
                      [k for kc in range(16) for k in self.hm_keys(kc, 0, G["T"])], BF16)


_TABLES = None


def shared_inputs(inp):
    global _TABLES
    if _TABLES is None:
        _TABLES = make_tables()
    f = lambda a: np.ascontiguousarray(np.asarray(a, dtype=np.float32))
    sh = dict(
        w_ada=f(inp["w_ada"]),
        b_ada=f(inp["b_ada"]).reshape(2, 48, 128),
        w_in=f(inp["w_in"]),
        w_f=f(inp["w_fourier"]),
        conv_w=f(inp["conv_w"]).reshape(2, 36, 128),
        conv_b=f(inp["conv_b"]).reshape(2, 12, 128),
        fw1=f(inp["filt_w1"]),
        fb1=f(inp["filt_b1"]).reshape(2, 64, 1),
        fw2=f(inp["filt_w2"]),
        fb2=f(inp["filt_b2"]).reshape(2, 64, 1),
        fw3=f(inp["filt_w3"]),
        ldec=f(inp["filt_log_decay"]).reshape(2, 1, 1024),
        skip=f(inp["hyena_skip"]).reshape(2, 1, 1024),
        sink=f(inp["sink_logit"]).reshape(2, 1, 8),
        qn=f(inp["q_norm"]).reshape(2, 64, 1),
        kn=f(inp["k_norm"]).reshape(2, 64, 1),
        knr=f(inp["k_norm"]).reshape(2, 1, 64),
        w_out=f(inp["w_out"]),
        ln_g=f(inp["ln_g"]).reshape(2, 1, D),
        ln_b=f(inp["ln_b"]).reshape(2, 1, D),
    )
    sh.update(_TABLES)
    return sh


def core_inputs(inp, i, sh):
    f = lambda a: np.ascontiguousarray(np.asarray(a, dtype=np.float32))
    m = dict(sh)
    m["x_s"] = f(inp["x_sample"][i])
    m["x_p"] = f(inp["x_prompt"][2 * i:2 * i + 2]).reshape(512, D)
    m["ck_c"] = f(inp["cache_attn_c_k"][i]).reshape(2, 256, 128)
    m["cv_c"] = f(inp["cache_attn_c_v"][i]).reshape(2, 256, 128)
    m["ck_d"] = f(inp["cache_attn_d_k"][i]).reshape(2, 256, 128)
    m["cv_d"] = f(inp["cache_attn_d_v"][i]).reshape(2, 256, 128)
    m["cond"] = np.ascontiguousarray(
        np.concatenate([f(inp["c"][i]).reshape(16, 128), f(inp["c_ctx"]).reshape(16, 128)], 0))
    return m


def _proj_fm(self, l, col0, T, handler, lhs_of=None):
    sl, sk = self.load_win(l, col0)
    for tc in range(T // 512):
        b = self.rot()
        for kc in range(KC):
            lhsT = sl[:, kc, :] if lhs_of is None else lhs_of(sl, kc)
            self.mm(self.ps[b][:, :], lhsT, self.hm(kc, tc * 512, (tc + 1) * 512), kc == 0, kc == KC - 1,
                    r=sk + self.hm_keys(kc, tc * 512, tc * 512 + 512), w=[("ps", b)])
        handler(tc, b)
    return sl, sk


def _proj_tm(self, l, col0, T, handler):
    sl, sk = self.load_win(l, col0)
    for tt in range(T // 128):
        b = self.rot()
        for kc in range(KC):
            self.mm(self.ps[b][:, 0:128], self.hm(kc, tt * 128, (tt + 1) * 128), sl[:, kc, :],
                    kc == 0, kc == KC - 1, r=sk + self.hm_keys(kc, tt * 128, tt * 128 + 128), w=[("ps", b)])
        handler(tt, b)


def _b2_keys(self, tbl, tc0=0, tc1=8):
    return [("b2", tbl * 8 + t) for t in range(tc0, tc1)]


def _load_tables(self, first):
    v = self.big2[:].rearrange("p (a b) n -> p a b n", a=2)
    for j in range(2):
        self.dma("sp", v[:, j], self.t1024_d[:, first + j], w=self.b2_keys(j))


def _tbl(self, L, which):
    if L == 1024:
        v = self.big2[:].rearrange("p (a b) n -> p a b n", a=2)
        j = which % 2
        return (lambda tc, n0, n1: v[:, j, tc, n0:n1]), (lambda tc: [("b2", j * 8 + tc)])
    o, _ = CB["t256"]
    v = self.cb[:, o:o + 2048].rearrange("p (a b n) -> p a b n", a=4, b=2)
    return (lambda tc, n0, n1: v[:, which, tc, n0:n1]), (lambda tc: ["cb"])


def _branch_a(self, l, g):
    G = self.group(g)
    T, NT = G["T"], G["T"] // 128
    aT, aTk = self.arena(0, 2048, BF16, [4, 1024])
    gA, gAk = self.arena(2048, 2048, BF16, [4, 1024])
    wf, wfk = self.arena(4096, 2048, F32, [4, 512])
    Wc, Wck = self.arena(6144, 1024, BF16, [4, 512])
    Ws, Wsk = self.arena(7168, 1024, BF16, [4, 512])
    ac, ack = self.arena(8192, 2048, BF16, [8, 512])
    as_, ask = self.arena(10240, 2048, BF16, [8, 512])
    if g == "s":
        self.load_tables(0)
    self.dma("sp", wf, self.w_f[l].rearrange("(cc p) n -> p cc n", p=128), w=wfk)
    for (W, Wk, tbl) in ((Wc, Wck, "bdc"), (Ws, Wsk, "bds")):
        for cc in range(4):
            b = self.rot()
            self.mm(self.ps[b][:, :], self.cfv(tbl), wf[:, cc, :], True, True, r=wfk + ["cf"], w=[("ps", b)])
            self.cp("act", W[:, cc, :], self.ps[b][:, :], r=[("ps", b)], w=Wk)
    for s in range(4):
        def h_a(tc, b, s=s):
            self.cp("act", aT[:, s, tc * 512:(tc + 1) * 512], self.ps[b][:, :], r=[("ps", b)], w=aTk)
        self.proj_fm(l, O_A + s * 128, T, h_a)
    for s in range(4):
        def h_g(tc, b, s=s):
            self.act(gA[:, s, tc * 512:(tc + 1) * 512], self.ps[b][:, :], AF.Silu, r=[("ps", b)], w=gAk)
        self.proj_fm(l, O_AG + s * 128, T, h_g)
    for (dst, dk, W, Wk) in ((ac, ack, Wc, Wck), (as_, ask, Ws, Wsk)):
        for tt in range(NT):
            b = self.rot()
            for cc in range(4):
                self.mm(self.ps[b][:, :], aT[:, cc, tt * 128:(tt + 1) * 128], W[:, cc, :], cc == 0, cc == 3,
                        r=aTk + Wk, w=[("ps", b)])
            self.cp("act" if tt % 2 else "dve", dst[:, tt, :], self.ps[b][:, :], r=[("ps", b)], w=dk)
    for (o, L) in G["seqs"]:
        cl, clk = self.tbl(L, 0)
        sl_, slk = self.tbl(L, 1)
        N = min(L, 512)
        ntc = L // 128
        for cb4 in range(4):
            for n0 in range(0, L, N):
                b = self.rot()
                i = 0
                for (src, sk2, tab, tabk) in ((ac, ack, cl, clk), (as_, ask, sl_, slk)):
                    for tc in range(ntc):
                        self.mm(self.ps[b][:, 0:N], src[:, o // 128 + tc, cb4 * 128:(cb4 + 1) * 128],
                                tab(tc, n0, n0 + N), i == 0, i == 2 * ntc - 1,
                                r=sk2 + tabk(tc), w=[("ps", b)])
                        i += 1
                t0 = o + n0
                self.tt("dve", self.mix[:, cb4, t0:t0 + N], self.ps[b][:, 0:N], gA[:, cb4, t0:t0 + N], ALU.mult,
                        r=[("ps", b)] + gAk, w=self.mx_keys(cb4, t0, t0 + N))


for _n, _f in (("proj_fm", _proj_fm), ("proj_tm", _proj_tm), ("b2_keys", _b2_keys), ("load_tables", _load_tables),
               ("tbl", _tbl), ("branch_a", _branch_a)):
    setattr(KB, _n, _f)


def _dup64(ap2d, c0):
    return bass.AP(ap2d.tensor, ap2d.offset + c0, [list(ap2d.ap[0]), [0, 2], [1, 64]])


def _branch_attn(self, l, g, which):
    self.nrot = 6
    try:
        _branch_attn_body(self, l, g, which)
    finally:
        self.nrot = 8


def _branch_attn_body(self, l, g, which):
    G = self.group(g)
    T, NT = G["T"], G["T"] // 128
    sample = (g == "s")
    is_d = (which == "d")
    base = O_D if is_d else O_C
    blk0 = 12 if is_d else 8
    Tk = T + (256 if sample else 0)
    NKT = Tk // 128
    sm = self.small
    off = [0]

    def take(nwords, dt=F32, shape=None):
        a = self.arena(off[0], nwords, dt, shape)
        off[0] += nwords
        return a
    qT, qTk = take(2 * T, BF16, [4, T])
    gT, gTk = take(2 * T, BF16, [4, T])
    kT2, kT2k = take(2 * Tk, BF16, [2, 2, Tk])
    vaug, vaugk = take(NKT * 256, BF16, [NKT, 2, 2, 128])
    NE = 5
    Es = [take(256, BF16) for _ in range(NE)]
    raws = [take(512) for _ in range(2)]
    tmps = [take(512) for _ in range(2)]
    xbs = [take(256, BF16) for _ in range(2)]
    rec, reck = take(T)
    stage, stagek = take(256, F32, [2, 128])
    kb16, kb16k = take(128, BF16, [2, 128])
    kn_, knk = take(256, BF16)
    if not sample:
        kout, koutk = take(NT * 128, F32, [NT, 128])
        vout, voutk = take(NT * 128, F32, [NT, 128])
        knbc, knbck = take(64)
    cnt = {"raw": 0, "E": 0}
    gq, gk, esink = sm[:, 64:65], sm[:, 65:66], sm[:, 72:80]
    if is_d:
        for half in range(2):
            self.dma("sp", sm[half * 64:half * 64 + 64, 64:65], self.qn[l], w=["attnp"])
            self.dma("sp", sm[half * 64:half * 64 + 64, 65:66], self.kn[l], w=["attnp"])
        if not sample:
            self.dma("sp", knbc, self.knr[l].broadcast_to([128, 64]), w=knbck)
    else:
        self.dma("sp", esink, self.sink[l].broadcast_to([128, 8]), w=["attnp"])
        self.act(esink, esink, AF.Exp, r=["attnp"], w=["attnp"])
    self.op("dve", lambda e: e.memset(vaug[:, :, :, 0, 64:128], 1.0), w=vaugk)
    self.op("dve", lambda e: e.memset(vaug[:, :, :, 1, 0:64], 1.0), w=vaugk)

    def finish_qk(b, dst, dstk, tok0, gcol):
        i = cnt["raw"] % 2
        cnt["raw"] += 1
        raw, rawk = raws[i]
        tmp, tmpk = tmps[i]
        xb, xbk = xbs[i]
        if not is_d and not sample:
            self.cp("act", dst, self.ps[b][:, :], r=[("ps", b)], w=dstk)
            return
        self.cp("act", raw, self.ps[b][:, :], r=[("ps", b)], w=rawk)
        cur, curk = raw, rawk
        if is_d:
            self.act(xb, raw, AF.Square, r=rawk, w=xbk)
            b2 = self.rot()
            self.mm(self.ps[b2][:, :], self.cbv("onesblk"), xb, True, True, r=xbk + ["cb"], w=[("ps", b2)])
            self.act(tmp, self.ps[b2][:, :], AF.Sqrt, r=[("ps", b2), "consts"], w=tmpk,
                     bias=self.c_rmseps, scale=1.0 / 64.0)
            self.op("dve", lambda e, tmp=tmp: e.reciprocal(tmp, tmp), r=tmpk, w=tmpk, cost=2.2)
            if sample:
                self.stt("dve", raw, raw, gcol, tmp, ALU.mult, ALU.mult, r=rawk + tmpk + ["attnp"], w=rawk)
            else:
                self.stt("dve", dst, raw, gcol, tmp, ALU.mult, ALU.mult, r=rawk + tmpk + ["attnp"], w=dstk)
                return
        self.cp("act", xb, raw, r=rawk, w=xbk)
        b3 = self.rot()
        self.mm(self.ps[b3][:, :], self.cbv("prot"), xb, True, True, r=xbk + ["cb"], w=[("ps", b3)])
        co, _ = CF["cos"]
        so, _ = CF["sin"]
        self.tt("dve", tmp, raw, self.cf[:, co + tok0:co + tok0 + 512], ALU.mult, r=rawk + ["cf"], w=tmpk)
        self.tt("dve", raw, self.ps[b3][:, :], self.cf[:, so + tok0:so + tok0 + 512], ALU.mult,
                r=[("ps", b3), "cf"], w=rawk)
        self.tt("dve", dst, raw, tmp, ALU.add, r=rawk + tmpk, w=dstk)

    self.op("dve", lambda e: e.memset(kT2[64:128, :, 0, :], 0.0), w=kT2k, cost=1.5)
    self.op("dve", lambda e: e.memset(kT2[0:64, :, 1, :], 0.0), w=kT2k, cost=1.5)

    def spread_k(src, srck, c0, n):
        self.cp("act", kT2[0:64, 0, 0, c0:c0 + n], src[0:64, 0:n], r=srck, w=kT2k)
        self.cp("dve", kT2[64:128, 0, 1, c0:c0 + n], src[0:64, 0:n], r=srck, w=kT2k)
        self.cp("dve", kT2[0:64, 1, 0, c0:c0 + n], src[64:128, 0:n], r=srck, w=kT2k)
        self.cp("act", kT2[64:128, 1, 1, c0:c0 + n], src[64:128, 0:n], r=srck, w=kT2k)

    def h_k(tc, b):
        finish_qk(b, kn_, knk, tc * 512, gk)
        spread_k(kn_, knk, tc * 512, 512)
    self.proj_fm(l, base + 512, T, h_k)
    for s in range(4):
        def h_q(tc, b, s=s):
            finish_qk(b, qT[:, s, tc * 512:(tc + 1) * 512], qTk, tc * 512, gq)
        self.proj_fm(l, base + s * 128, T, h_q)
    def put_v(src, srck, tt):
        sv = src.rearrange("p (a b) -> p a b", b=64)
        self.cp("act", vaug[:, tt, :, 0, 0:64], sv, r=srck, w=vaugk)
        self.cp("act", vaug[:, tt, :, 1, 64:128], sv, r=srck, w=vaugk)

    def h_v(tt, b):
        put_v(self.ps[b][:, 0:128], [("ps", b)], tt)
        if not sample:
            self.cp("act", vout[:, tt, :], self.ps[b][:, 0:128], r=[("ps", b)], w=voutk)
    self.proj_tm(l, base + 640, T, h_v)
    if not sample:
        def h_ko(tt, b):
            if not is_d:
                self.cp("act", kout[:, tt, :], self.ps[b][:, 0:128], r=[("ps", b)], w=koutk)
                return
            raw, rawk = raws[tt % 2]
            tmp, tmpk = tmps[tt % 2]
            self.cp("act", raw[:, 0:128], self.ps[b][:, 0:128], r=[("ps", b)], w=rawk)
            self.act(tmp[:, 0:128], raw[:, 0:128], AF.Square, r=rawk, w=tmpk)
            self.op("dve", lambda e, tmp=tmp: e.tensor_reduce(
                tmp[:, 128:130], tmp[:, 0:128].rearrange("p (a b) -> p a b", b=64),
                mybir.AxisListType.X, ALU.add), r=tmpk, w=tmpk)
            self.act(tmp[:, 128:130], tmp[:, 128:130], AF.Sqrt, r=tmpk + ["consts"], w=tmpk,
                     bias=self.c_rmseps, scale=1.0 / 64.0)
            self.op("dve", lambda e, tmp=tmp: e.reciprocal(tmp[:, 128:130], tmp[:, 128:130]), r=tmpk, w=tmpk)
            for kvh in range(2):
                self.stt("dve", kout[:, tt, kvh * 64:(kvh + 1) * 64], raw[:, kvh * 64:(kvh + 1) * 64],
                         tmp[:, 128 + kvh:129 + kvh], knbc, ALU.mult, ALU.mult, r=rawk + tmpk + knbck, w=koutk)
        self.proj_tm(l, base + 512, T, h_ko)
        kname, vname = ("ndk", "ndv") if is_d else ("nck", "ncv")
        for si in range(2):
            self.dma("sp", self.kv_out[kname][si, l].rearrange("(a p) c -> p a c", p=128),
                     kout[:, 2 * si:2 * si + 2, :], r=koutk, store=True)
            self.dma("sp", self.kv_out[vname][si, l].rearrange("(a p) c -> p a c", p=128),
                     vout[:, 2 * si:2 * si + 2, :], r=voutk, store=True)
    if sample:
        kc_d, vc_d = (self.cache["ck_d"], self.cache["cv_d"]) if is_d else (self.cache["ck_c"], self.cache["cv_c"])
        self.dma("sp", stage, vc_d[l].rearrange("(a p) c -> p a c", p=128), w=stagek)
        for c in range(2):
            put_v(stage[:, c, :], stagek, NT + c)
        self.dma("sp", stage, kc_d[l].rearrange("(a p) c -> p a c", p=128), r=[], w=stagek)
        self.cp("act", kb16, stage, r=stagek, w=kb16k)
        for c in range(2):
            b = self.rot()
            self.tr(self.psb[b][:, 0:128], kb16[:, c, :], self.cbv("ident"), r=kb16k + ["cb"], w=[("ps", b)])
            self.cp("act", kn_[:, 0:128], self.psb[b][:, 0:128], r=[("ps", b)], w=knk)
            spread_k(kn_, knk, T + c * 128, 128)
    for s in range(4):
        def h_g(tc, b, s=s):
            self.act(gT[:, s, tc * 512:(tc + 1) * 512], self.ps[b][:, :], AF.Silu, r=[("ps", b)], w=gTk)
        self.proj_fm(l, base + 768 + s * 128, T, h_g)

    def score_pv(h, kb, q0, q1, acc_b, acc_c0, start, stop, masks=()):
        kvh, s, half = h // 4, h // 2, h % 2
        rows = slice(half * 64, half * 64 + 64)
        n = q1 - q0
        b = self.rot()
        self.mm(self.ps[b][:, 0:n], kT2[:, kvh, half, kb * 128:(kb + 1) * 128], qT[:, s, q0:q1], True, True,
                r=kT2k + qTk, w=[("ps", b)])
        E, Ek = Es[cnt["E"] % NE]
        cnt["E"] += 1
        self.act(E[:, 0:n], self.ps[b][:, 0:n], AF.Exp, r=[("ps", b)], w=Ek, scale=0.125)
        for (c0, mname) in masks:
            self.tt("dve", E[:, c0:c0 + 128], E[:, c0:c0 + 128], self.cbv(mname), ALU.mult, r=Ek + ["cb"], w=Ek)
        self.mm(self.ps[acc_b][:, acc_c0:acc_c0 + n], vaug[:, kb, kvh, half, :], E[:, 0:n], start, stop,
                r=vaugk + Ek, w=[("ps", acc_b)])

    def finalize(h, acc_b, ncols, tok0):
        s, half = h // 2, h % 2
        rows = slice(half * 64, half * 64 + 64)
        drows = slice(64, 128) if half == 0 else slice(0, 64)
        tmp, tmpk = tmps[h % 2]
        if is_d:
            self.op("dve", lambda e: e.reciprocal(rec[drows, tok0:tok0 + ncols], self.ps[acc_b][drows, 0:ncols]),
                    r=[("ps", acc_b)], w=reck, cost=2.2)
        else:
            self.ts("dve", rec[drows, tok0:tok0 + ncols], self.ps[acc_b][drows, 0:ncols], esink[drows, h:h + 1],
                    None, ALU.add, None, r=[("ps", acc_b), "attnp"], w=reck)
            self.op("dve", lambda e: e.reciprocal(rec[drows, tok0:tok0 + ncols], rec[drows, tok0:tok0 + ncols]),
                    r=reck, w=reck, cost=2.2)
        self.tt("dve", tmp[rows, 0:ncols], self.ps[acc_b][rows, 0:ncols], rec[drows, tok0:tok0 + ncols], ALU.mult,
                r=[("ps", acc_b)] + reck, w=tmpk)
        self.tt("dve", self.mix[rows, blk0 + s, tok0:tok0 + ncols], tmp[rows, 0:ncols],
                gT[rows, s, tok0:tok0 + ncols], ALU.mult, r=tmpk + gTk,
                w=self.mx_keys(blk0 + s, tok0, tok0 + ncols))

    accn = [0]

    def next_acc():
        b = 6 + (accn[0] % 2)
        accn[0] += 1
        return b

    for h in range(8):
        if not sample:
            acc = next_acc()
            for si, (o, L) in enumerate(G["seqs"]):
                for kb in range(2):
                    score_pv(h, (o // 128) + kb, o, o + L, acc, si * 256, kb == 0, kb == 1)
            finalize(h, acc, 512, 0)
        elif is_d:
            for qc in range(2):
                acc = next_acc()
                for kb in range(NKT):
                    score_pv(h, kb, qc * 512, qc * 512 + 512, acc, 0, kb == 0, kb == NKT - 1)
                finalize(h, acc, 512, qc * 512)
        else:
            for qc in range(2):
                acc = next_acc()
                for kb in (8, 9):
                    score_pv(h, kb, qc * 512, qc * 512 + 512, acc, 0, kb == 8, False)
                for kb in range(8):
                    q0, q1 = max(0, kb - 1) * 128, min(8, kb + 2) * 128
                    a, bq = max(q0, qc * 512), min(q1, qc * 512 + 512)
                    if a >= bq:
                        continue
                    masks = []
                    if kb >= 1 and a <= (kb - 1) * 128 < bq:
                        masks.append(((kb - 1) * 128 - a, "triu"))
                    if kb <= 6 and a <= (kb + 1) * 128 < bq:
                        masks.append(((kb + 1) * 128 - a, "tril"))
                    score_pv(h, kb, a, bq, acc, a % 512, False, False, masks)
                finalize(h, acc, 512, qc * 512)


def _proj_fm2(self, l, col0, T, handler, lhs_of=None, slab=None):
    if slab is None:
        sl, sk = self.load_win(l, col0)
    else:
        sl, sk = slab
    for tc in range(T // 512):
        b = self.rot()
        for kc in range(KC):
            lhsT = sl[:, kc, :] if lhs_of is None else lhs_of(sl, kc)
            self.mm(self.ps[b][:, :], lhsT, self.hm(kc, tc * 512, (tc + 1) * 512), kc == 0, kc == KC - 1,
                    r=sk + self.hm_keys(kc, tc * 512, tc * 512 + 512), w=[("ps", b)])
        handler(tc, b)
    return sl, sk


KB.proj_fm = _proj_fm2
KB.branch_attn = _branch_attn
KB.branch_c = lambda self, l, g: _branch_attn(self, l, g, "c")
KB.branch_d = lambda self, l, g: _branch_attn(self, l, g, "d")


MAGIC = 12582912.0


def _branch_b(self, l, g):
    G = self.group(g)
    T, L, seqs = G["T"], G["L"], G["seqs"]
    NS = len(seqs)
    n128 = L // 128
    W = NS * 128
    off = [0]

    def take(nwords, dt=F32, shape=None):
        a = self.arena(off[0], nwords, dt, shape)
        off[0] += nwords
        return a
    h2T, h2Tk = take(1024)
    w3p, w3pk = take(128)
    edp, edpk = take(128)
    skp, skpk = take(128)
    cfm, cfmk = take(64)
    fwt, fwtk = take(192)
    rsb, rsbk = take(128)
    g1q0, g1q0k = take(128)
    rawb, rawbk = take(512, BF16)
    cv, cvk = take(1024)
    vb, vbk = take(512, BF16)
    x1b, x1bk = take(512, BF16)
    x2g, x2gk = take(512, BF16)
    gB, gBk = take(512, BF16)
    vtm, vtmk = take(n128 * W // 2, BF16, [n128, W])
    AB, ABk = take(n128 * 2 * W, F32, [n128, 2, W])
    G1, G1k = take(n128 * 128, F32, [n128, 128])
    G2, G2k = take(n128 * 128, F32, [n128, 128])
    Pb, Pbk = take(n128 * W // 2, BF16, [n128, W])
    Qb, Qbk = take(n128 * W // 2, BF16, [n128, W])
    hw, hwk = take(1024)
    htm, htmk = take(n128 * 64, BF16, [n128, 128])
    zb, zbk = take(512, BF16)
    hw3 = hw[:, 0:n128 * 128].rearrange("p (a b) -> p a b", b=128)
    cm, cmk = self.tbl(L, 2)
    smm, smk = self.tbl(L, 3)
    o_, w_ = CB["smt0_%d" % L]
    smt0 = self.cb[:, o_:o_ + w_].rearrange("p (a b) -> p a b", b=128)
    o_, w_ = CB["sm0_%d" % L]
    sm0 = self.cb[:, o_:o_ + w_]
    ndist = self.cfv("ndist%d" % L)
    cab = self.cfv("cab%d" % L)
    feats = self.cfv("feats%d" % L)
    if g == "s":
        self.load_tables(2)
    self.load_fm(cfm[:, 0:36], self.conv_w[l], 36, cfmk, 13312)
    self.load_fm(cfm[:, 36:48], self.conv_b[l], 12, cfmk, 13440)
    self.dma("sp", fwt[0:33, 0:64], self.fw1[l], w=fwtk)
    self.dma("sp", fwt[0:64, 64:128], self.fw2[l], w=fwtk)
    self.dma("sp", fwt[0:64, 128:129], self.fb1[l], w=fwtk)
    self.dma("sp", fwt[0:64, 129:130], self.fb2[l], w=fwtk)
    t0 = hw[:, 0:512]
    t1 = hw[:, 512:1024]
    h1T = cv
    N = min(L, 512)
    for (dst, dstk, wl, kdim, bcol, src, srck) in ((h1T, cvk, fwt[0:33, 0:64], 33, fwt[0:64, 128:129], feats, ["cf"]),
                                                    (h2T, h2Tk, fwt[0:64, 64:128], 64, fwt[0:64, 129:130], h1T, cvk)):
        for n0 in range(0, L, N):
            b = self.rot()
            self.mm(self.ps[b][0:64, 0:N], wl, src[0:kdim, n0:n0 + N], True, True, r=fwtk + srck, w=[("ps", b)])
            self.ts("dve", t0[0:64, 0:N], self.ps[b][0:64, 0:N], bcol, None, ALU.add, None,
                    r=[("ps", b)] + fwtk, w=hwk)
            self.ts("dve", t1[0:64, 0:N], t0[0:64, 0:N], 1.0 / TWO_PI, MAGIC, ALU.mult, ALU.add, r=hwk, w=hwk)
            self.ts("dve", t1[0:64, 0:N], t1[0:64, 0:N], -MAGIC, None, ALU.add, None, r=hwk, w=hwk)
            self.stt("dve", t0[0:64, 0:N], t1[0:64, 0:N], -TWO_PI, t0[0:64, 0:N], ALU.mult, ALU.add, r=hwk, w=hwk)
            self.act(dst[0:64, n0:n0 + N], t0[0:64, 0:N], AF.Sin, r=hwk, w=dstk, scale=0.999999)
    h2keep = h2T

    def short_conv(which, cbi, dst, dstk):
        jb = which * 4 + cbi
        w0, w1c, w2c = cfm[:, jb:jb + 1], cfm[:, 12 + jb:13 + jb], cfm[:, 24 + jb:25 + jb]
        bc = cfm[:, 36 + jb:37 + jb]

        def h_raw(tc, b):
            self.cp("act", rawb[:, tc * 512:(tc + 1) * 512], self.ps[b][:, :], r=[("ps", b)], w=rawbk)
        self.proj_fm(l, (O_BV, O_BX1, O_BX2)[which] + cbi * 128, T, h_raw)
        for (o, Ls) in seqs:
            e = o + Ls
            self.ts("dve", cv[:, o:e], rawb[:, o:e], w1c, bc, ALU.mult, ALU.add, r=rawbk + cfmk, w=cvk)
            self.stt("dve", cv[:, o + 1:e], rawb[:, o:e - 1], w0, cv[:, o + 1:e], ALU.mult, ALU.add,
                     r=rawbk + cfmk + cvk, w=cvk)
            self.stt("dve", dst[:, o:e - 1], rawb[:, o + 1:e], w2c, cv[:, o:e - 1], ALU.mult, ALU.add,
                     r=rawbk + cfmk + cvk, w=dstk)
            self.cp("dve", dst[:, e - 1:e], cv[:, e - 1:e], r=cvk, w=dstk)

    def fwd_dft(src_tm, srck, dstAB, dstk):
        width = src_tm.shape[-1]
        for fb in range(n128):
            b = self.rot()
            for tc in range(n128):
                self.mm(self.ps[b][:, 0:width], cm(tc, fb * 128, fb * 128 + 128), src_tm[:, tc, :],
                        tc == 0, tc == n128 - 1, r=cmk(tc) + srck, w=[("ps", b)])
            for tc in range(n128):
                lhs = smt0[:, tc, :] if fb == 0 else smm(tc, fb * 128, fb * 128 + 128)
                self.mm(self.ps[b][:, 256:256 + width], lhs, src_tm[:, tc, :],
                        tc == 0, tc == n128 - 1, r=(["cb"] if fb == 0 else smk(tc)) + srck, w=[("ps", b)])
            pv = self.ps[b][:, :].rearrange("p (a b) -> p a b", a=2)[:, :, 0:width]
            self.cp("act", dstAB[:, fb, :, 0:width], pv, r=[("ps", b)], w=dstk)

    def make_filter(conv, cbi):
        c0 = conv * 512 + cbi * 128
        self.dma("sp", w3p[0:64, :], self.fw3[l][:, c0:c0 + 128], w=w3pk)
        self.dma("sp", edp, self.ldec[l][:, c0:c0 + 128].broadcast_to([128, 128]), w=edpk)
        self.dma("sp", skp[0:1, :], self.skip[l][:, c0:c0 + 128], w=skpk)
        self.act(edp, edp, AF.Exp, r=edpk, w=edpk)
        for tc in range(n128):
            self.act(hw3[:, tc, :], edp, AF.Exp, r=edpk + ["cf"], w=hwk, scale=ndist[:, tc:tc + 1])
        for q in range(0, n128, 4):
            b = self.rot()
            nq = min(4, n128 - q)
            for j in range(nq):
                self.mm(self.ps[b][:, j * 128:(j + 1) * 128], h2keep[0:64, (q + j) * 128:(q + j + 1) * 128],
                        w3p[0:64, :], True, True, r=h2Tk + w3pk, w=[("ps", b)])
            self.stt("dve", hw[:, q * 128:(q + nq) * 128], hw[:, q * 128:(q + nq) * 128], DECAY_SHIFT,
                     self.ps[b][:, 0:nq * 128], ALU.add, ALU.mult, r=hwk + [("ps", b)], w=hwk)
        hsq = AB[:, :, 0, 0:128]
        self.act(hsq, hw3, AF.Square, r=hwk, w=ABk)
        b = self.rot()
        for tc in range(n128):
            self.mm(self.ps[b][:, 0:128], self.cfv("ones"), hsq[:, tc, :], tc == 0, tc == n128 - 1,
                    r=ABk + ["cf"], w=[("ps", b)])
        self.act(rsb, self.ps[b][:, 0:128], AF.Sqrt, r=[("ps", b), "consts"], w=rsbk, bias=self.c_rmseps)
        self.op("dve", lambda e: e.reciprocal(rsb, rsb), r=rsbk, w=rsbk, cost=0.6)
        rs_b = bass.AP(rsb.tensor, rsb.offset, [list(rsb.ap[0]), [0, n128], [1, 128]])
        self.tt("dve", hw3, hw3, rs_b, ALU.mult, r=hwk + rsbk, w=hwk)
        mid = (L // 2) // 128
        self.tt("dve", hw3[0:1, mid, :], hw3[0:1, mid, :], skp[0:1, :], ALU.add, r=hwk + skpk, w=hwk)
        self.cp("act", htm, hw3, r=hwk, w=htmk)
        fwd_dft(htm, htmk, AB, ABk)
        Ah, Bh = AB[:, :, 0, 0:128], AB[:, :, 1, 0:128]
        ca, cbb, nca = cab[:, 0:1], cab[:, 1:2], cab[:, 2:3]
        self.ts("dve", G1, Ah, ca, None, ALU.mult, None, r=ABk + ["cf"], w=G1k)
        self.stt("dve", G1, Bh, cbb, G1, ALU.mult, ALU.add, r=ABk + ["cf"] + G1k, w=G1k)
        self.ts("dve", G2, Ah, cbb, None, ALU.mult, None, r=ABk + ["cf"], w=G2k)
        self.stt("dve", G2, Bh, nca, G2, ALU.mult, ALU.add, r=ABk + ["cf"] + G2k, w=G2k)
        self.ts("dve", g1q0[0:1, :], AB[0:1, 0, 1, 0:128], 1.0 / (2 * L), None, ALU.mult, None, r=ABk, w=g1q0k)
        self.ts("dve", G1[0:1, 0, :], G1[0:1, 0, :], 0.5, None, ALU.mult, None, r=G1k, w=G1k)
        self.op("dve", lambda e: e.memset(G2[0:1, 0, :], 0.0), w=G2k)

    def long_conv(src, srck, mulsrc, mulk, dst_of, dstk_of):
        per_bank = 8
        idx = [(tc, si) for tc in range(n128) for si in range(NS)]
        for q in range(0, len(idx), per_bank):
            b = self.rot()
            grp = idx[q:q + per_bank]
            for j, (tc, si) in enumerate(grp):
                o = seqs[si][0]
                self.tr(self.psb[b][:, j * 128:(j + 1) * 128], src[:, o + tc * 128:o + (tc + 1) * 128],
                        self.cbv("ident"), r=srck + ["cb"], w=[("ps", b)])
            flat = vtm.rearrange("p a b -> p (a b)")
            self.cp("act", flat[:, q * 128:(q + len(grp)) * 128], self.psb[b][:, 0:len(grp) * 128],
                    r=[("ps", b)], w=vtmk)
        fwd_dft(vtm, vtmk, AB, ABk)
        A_, B_ = AB[:, :, 0, :], AB[:, :, 1, :]
        if NS == 1:
            g1b, g2b = G1, G2
            t1v = hw[:, 0:n128 * W].rearrange("p (a b) -> p a b", b=W)
            t2v = cv[:, 0:n128 * W].rearrange("p (a b) -> p a b", b=W)
            A4, B4, P4, Q4 = A_, B_, Pb, Qb
        else:
            def bc4(t):
                return bass.AP(t.tensor, t.offset, [list(t.ap[0]), [128, n128], [0, NS], [1, 128]])
            g1b, g2b = bc4(G1), bc4(G2)
            r4 = lambda t: t.rearrange("p a (s c) -> p a s c", s=NS)
            t1v = r4(hw[:, 0:n128 * W].rearrange("p (a b) -> p a b", b=W))
            t2v = r4(cv[:, 0:n128 * W].rearrange("p (a b) -> p a b", b=W))
            A4, B4, P4, Q4 = r4(A_), r4(B_), r4(Pb), r4(Qb)
        rk = ABk + G1k + G2k
        self.tt("dve", t1v, A4, g1b, ALU.mult, r=rk, w=hwk)
        self.tt("dve", t2v, B4, g2b, ALU.mult, r=rk, w=cvk)
        self.tt("dve", P4, t1v, t2v, ALU.add, r=hwk + cvk, w=Pbk)
        self.tt("dve", t1v, B4, g1b, ALU.mult, r=rk, w=hwk)
        self.tt("dve", t2v, A4, g2b, ALU.mult, r=rk, w=cvk)
        self.tt("dve", Q4, t1v, t2v, ALU.subtract, r=hwk + cvk, w=Qbk)
        if NS == 1:
            self.tt("dve", Qb[0:1, 0, :], AB[0:1, 0, 1, :], g1q0[0:1, :], ALU.mult, r=ABk + g1q0k + Qbk, w=Qbk)
        else:
            for si in range(NS):
                self.tt("dve", Qb[0:1, 0, si * 128:(si + 1) * 128], AB[0:1, 0, 1, si * 128:(si + 1) * 128],
                        g1q0[0:1, :], ALU.mult, r=ABk + g1q0k + Qbk, w=Qbk)
        Nn = min(L, 512)
        for si, (o, Ls) in enumerate(seqs):
            for n0 in range(0, Ls, Nn):
                b = self.rot()
                i = 0
                for fb in range(n128):
                    self.mm(self.ps[b][:, 0:Nn], Pb[:, fb, si * 128:(si + 1) * 128], cm(fb, n0, n0 + Nn),
                            i == 0, False, r=Pbk + cmk(fb), w=[("ps", b)])
                    i += 1
                for fb in range(n128):
                    rhs = sm0[:, n0:n0 + Nn] if fb == 0 else smm(fb, n0, n0 + Nn)
                    self.mm(self.ps[b][:, 0:Nn], Qb[:, fb, si * 128:(si + 1) * 128], rhs,
                            False, fb == n128 - 1, r=Qbk + (["cb"] if fb == 0 else smk(fb)), w=[("ps", b)])
                t0_ = o + n0
                self.tt("dve", dst_of(t0_, Nn), self.ps[b][:, 0:Nn], mulsrc[:, t0_:t0_ + Nn], ALU.mult,
                        r=[("ps", b)] + mulk, w=dstk_of(t0_, Nn))

    for cbi in range(4):
        short_conv(0, cbi, vb, vbk)
        short_conv(1, cbi, x1b, x1bk)
        short_conv(2, cbi, x2g, x2gk)

        def h_g(tc, b):
            self.act(gB[:, tc * 512:(tc + 1) * 512], self.ps[b][:, :], AF.Silu, r=[("ps", b)], w=gBk)
        self.proj_fm(l, O_BG + cbi * 128, T, h_g)
        self.tt("dve", x2g[:, 0:T], x2g[:, 0:T], gB[:, 0:T], ALU.mult, r=x2gk + gBk, w=x2gk)
        make_filter(0, cbi)
        long_conv(vb, vbk, x1b, x1bk, lambda t0_, n: zb[:, t0_:t0_ + n], lambda t0_, n: zbk)
        make_filter(1, cbi)
        long_conv(zb, zbk, x2g, x2gk, lambda t0_, n: self.mix[:, 4 + cbi, t0_:t0_ + n],
                  lambda t0_, n: self.mx_keys(4 + cbi, t0_, t0_ + n))


DECAY_SHIFT = 0.05
KB.branch_b = _branch_b


def _phase_c(self, l, g):
    G = self.group(g)
    self.bg_flush(l)
    T, NT = G["T"], G["T"] // 128
    cj = G["cond"]
    src = self.x_in[g] if l == 0 else self.xm[g]
    dst = self.xm[g] if l == 0 else self.y_out[g]
    wv = [self.big1[:].rearrange("p a b -> p (a b)").rearrange("p (k n) -> p k n", n=2048),
          self.big2[:].rearrange("p a b -> p (a b)").rearrange("p (k n) -> p k n", n=2048)]

    def wkeys(kc):
        j = kc % 8
        if kc < 8:
            return [("hm", 2 * j + d, t) for d in range(2) for t in range(8)]
        return [("b2", 2 * j), ("b2", 2 * j + 1)]
    if g == "s":
        for kc in range(KC):
            self.dma("pool", wv[kc // 8][:, kc % 8, :], self.w_out[l][kc * 128:(kc + 1) * 128, :], w=wkeys(kc))
    xts = [self.arena(0, 2048), self.arena(2048, 2048), self.arena(11392, 2048)]
    gbc, gbck = self.arena(4096, 2048)
    lng, lngk = self.arena(6144, 2048)
    lnb, lnbk = self.arena(8192, 2048)
    gblk, gblkk = self.arena(10240, 128)
    tbs = [self.arena(10368, 512), self.arena(10880, 512), self.arena(13440, 512), self.arena(13952, 512)]
    sm = self.small
    self.dma("sp", lng, self.ln_g[l].broadcast_to([128, D]), w=lngk)
    self.dma("sp", lnb, self.ln_b[l].broadcast_to([128, D]), w=lnbk)
    for q in range(4):
        b = self.rot()
        for j in range(4):
            blk = q * 4 + j
            gcol = self.modfm[:, l, 32 + blk, cj:cj + 1]
            gsrc = bass.AP(gcol.tensor, gcol.offset, [list(gcol.ap[0]), [0, 128]])
            self.cp("dve", gblk, gsrc, r=[("modfm", l, 2)], w=gblkk)
            self.mm(self.ps[b][:, j * 128:(j + 1) * 128], gblk, self.cfv("ident"), True, True,
                    r=gblkk + ["cf"], w=[("ps", b)])
        self.act(gbc[:, q * 512:(q + 1) * 512], self.ps[b][:, :], AF.Identity, r=[("ps", b)], w=gbck,
                 scale=1.0 / ALPHA)
    SB = (0, 32, 96)
    for tt in range(NT):
        bi = tt % 3
        xt, xk = xts[bi]
        sb0 = SB[bi]
        st = sm[:, sb0:sb0 + 24].rearrange("p (a b) -> p a b", b=6)
        mv = sm[:, sb0 + 24:sb0 + 26]
        rstd = sm[:, sb0 + 26:sb0 + 27]
        nmr = sm[:, sb0 + 27:sb0 + 28]
        sk = [("smA", bi)]
        rows = slice(tt * 128, (tt + 1) * 128)
        self.dma("sp", xt, src[rows, :], r=[("xm", g, tt)] if l > 0 else [], w=xk)
        for nb in range(4):
            b = self.rot()
            for kc in range(KC):
                self.mm(self.ps[b][:, :], self.mix[:, kc, rows], wv[kc // 8][:, kc % 8, nb * 512:(nb + 1) * 512],
                        kc == 0, kc == KC - 1, r=self.mx_keys(kc, tt * 128, tt * 128 + 128) + wkeys(kc),
                        w=[("ps", b)])
            tb, tbk = tbs[nb % 4]
            cols = slice(nb * 512, (nb + 1) * 512)
            self.tt("dve", tb, self.ps[b][:, :], gbc[:, cols], ALU.mult, r=[("ps", b)] + gbck, w=tbk)
            self.tt("dve", xt[:, cols], xt[:, cols], tb, ALU.add, r=xk + tbk, w=xk)
        for j in range(4):
            self.op("dve", lambda e, j=j, st=st, xt=xt: e.bn_stats(st[:, j, :], xt[:, j * 512:(j + 1) * 512]),
                    r=xk, w=sk, cost=0.65)
        self.op("dve", lambda e, st=st, mv=mv: e.bn_aggr(mv, st), r=sk, w=sk)
        self.act(rstd, mv[:, 1:2], AF.Sqrt, r=sk + ["consts"], w=sk, bias=self.c_lneps_c)
        self.op("dve", lambda e, rstd=rstd: e.reciprocal(rstd, rstd), r=sk, w=sk)
        self.stt("dve", nmr, mv[:, 0:1], -1.0, rstd, ALU.mult, ALU.mult, r=sk, w=sk)
        self.act(xt, xt, AF.Identity, r=xk + sk, w=xk, bias=nmr, scale=rstd)
        self.tt("dve", xt, xt, lng, ALU.mult, r=xk + lngk, w=xk)
        self.tt("dve", xt, xt, lnb, ALU.add, r=xk + lnbk, w=xk)
        self.dma("sp", dst[rows, :], xt, r=xk, w=[("xm", g, tt)] if l == 0 else [], store=(l == DEPTH - 1))


KB.phase_c = _phase_c


def build_full(debug=None):
    kb = KB(debug=debug)
    kb.setup()
    kb.bg_enable = True
    for l in range(DEPTH):
        for g in ("s", "p"):
            kb.phase_a(l, g)
            kb.branch_a(l, g)
            kb.branch_b(l, g)
            kb.branch_c(l, g)
            kb.branch_d(l, g)
            kb.phase_c(l, g)
    kb.s.emit()
    return kb


_KB = None


def kernel(**inputs):
    global _KB
    if _KB is None:
        _KB = build_full()
    kb = _KB
    sh = shared_inputs(inputs)
    in_maps = [core_inputs(inputs, i, sh) for i in range(8)]
    res = run_bass_kernel_spmd(kb.nc, in_maps, core_ids=list(range(8)))
    rs = res.results
    y_p = np.concatenate([np.asarray(r["y_p"], np.float32).reshape(2, 256, D) for r in rs], 0)
    y_s = np.stack([np.asarray(r["y_s"], np.float32) for r in rs], 0)
    kv = []
    for name in ("nck", "ncv", "ndk", "ndv"):
        kv.append(np.concatenate([np.asarray(r[name], np.float32).reshape(2, 2, 256, 2, 64) for r in rs], 0))
    return (y_p, y_s, kv[0], kv[1], kv[2], kv[3])
```
